# Optimizing a Trainium2 kernel written in Bass

```python
import jax, jax.numpy as jnp
from jax import lax
import numpy as np

D_MODEL = 2048
BATCH = 2
SEQ = 4096
DEPTH = 4

N_META = 16
D_INNER = 2 * D_MODEL
SSM_HEAD_DIM = 64
SSM_HEADS = D_INNER // SSM_HEAD_DIM
SSM_GROUPS = 8
SSM_STATE = 128
CONV_WIDTH = 4
CHUNK = 128
CONV_DIM = D_INNER + 2 * SSM_GROUPS * SSM_STATE
ATTN_HEADS = 16
ATTN_KV_HEADS = 4
ATTN_HEAD_DIM = 128
ATTN_WIDTH = ATTN_HEADS * ATTN_HEAD_DIM
KV_WIDTH = ATTN_KV_HEADS * ATTN_HEAD_DIM
IDX_HEADS = 16
IDX_DIM = 64
TOPK_MAX = 256
Q_BLOCK = 128
ROPE_THETA = 500000.0
ROPE_FRACTION = 4
EPS = 1e-6
IN_WIDTHS = (
    D_INNER,
    CONV_DIM,
    SSM_HEADS,
    ATTN_WIDTH,
    KV_WIDTH,
    KV_WIDTH,
    ATTN_WIDTH,
    IDX_HEADS * IDX_DIM,
    IDX_DIM,
    IDX_HEADS,
    D_MODEL,
    D_MODEL,
)
N_IN = 20624

kernel_name = "hybrid_ssd_dsa_gated_trunk"


def split_columns(proj):
    outs, start = [], 0
    for w in IN_WIDTHS:
        outs.append(proj[..., start:start + w])
        start += w
    return outs


def rms_norm(x, w):
    xf = x.astype(jnp.float32)
    y = xf * lax.rsqrt(jnp.mean(xf * xf, axis=-1, keepdims=True) + EPS)
    return (y * w.astype(jnp.float32)).astype(x.dtype)


def rope_tables(positions, head_dim):
    rot = head_dim // ROPE_FRACTION
    inv = ROPE_THETA ** (-jnp.arange(0, rot, 2, dtype=jnp.float32) / rot)
    ang = positions.astype(jnp.float32)[:, None] * inv[None, :]
    return jnp.cos(ang), jnp.sin(ang)


def apply_partial_rope(x, cos, sin):
    half = cos.shape[-1]
    rot = 2 * half
    shape = (1, cos.shape[0]) + (1,) * (x.ndim - 3) + (half,)
    c, s = cos.reshape(shape), sin.reshape(shape)
    xf = x.astype(jnp.float32)
    x1, x2, xp = xf[..., :half], xf[..., half:rot], xf[..., rot:]
    return jnp.concatenate([x1 * c - x2 * s, x2 * c + x1 * s, xp], axis=-1).astype(x.dtype)


def causal_depthwise_conv(u, w, b):
    out = lax.conv_general_dilated(
        u, w[:, None, :].astype(u.dtype), window_strides=(1,),
        padding=[(CONV_WIDTH - 1, 0)], dimension_numbers=("NWC", "WIO", "NWC"),
        feature_group_count=u.shape[-1])
    return out + b.astype(u.dtype)


def ssd_chunked(xs, dt, a, bm, cm):
    f32 = jnp.float32
    Bsz, Tp, H, P = xs.shape
    G, N = bm.shape[-2:]
    R = H // G
    nc = Tp // CHUNK
    x = xs.astype(f32).reshape(Bsz, nc, CHUNK, G, R, P)
    dtc = dt.astype(f32).reshape(Bsz, nc, CHUNK, G, R)
    bc = bm.astype(f32).reshape(Bsz, nc, CHUNK, G, N)
    cc = cm.astype(f32).reshape(Bsz, nc, CHUNK, G, N)
    xdt = x * dtc[..., None]
    acs = jnp.cumsum(dtc * a.astype(f32).reshape(G, R), axis=2)
    acs_t = jnp.moveaxis(acs, 2, -1)
    causal = jnp.tril(jnp.ones((CHUNK, CHUNK), dtype=bool))
    lmat = jnp.exp(jnp.where(causal, acs_t[..., :, None] - acs_t[..., None, :], -jnp.inf))
    cb = jnp.einsum('bclgn,bcsgn->bcgls', cc, bc)
    y_diag = jnp.einsum('bcgls,bcgrls,bcsgrp->bclgrp', cb, lmat, xdt)
    decay_to_end = jnp.exp(acs[:, :, -1:] - acs)
    chunk_states = jnp.einsum('bclgn,bclgr,bclgrp->bcgrpn', bc, decay_to_end, xdt)
    chunk_decay = jnp.exp(acs[:, :, -1])

    def step(h, inp):
        s_c, d_c = inp
        return h * d_c[..., None, None] + s_c, h

    h0 = jnp.zeros((Bsz, G, R, P, N), f32)
    _, h_in = lax.scan(step, h0, (jnp.moveaxis(chunk_states, 1, 0), jnp.moveaxis(chunk_decay, 1, 0)))
    h_in = jnp.moveaxis(h_in, 0, 1)
    y_off = jnp.einsum('bclgn,bcgrpn,bclgr->bclgrp', cc, h_in, jnp.exp(acs))
    return (y_diag + y_off).reshape(Bsz, Tp, H, P)


def ssm_branch(z, xbc, dt_raw, conv_w, conv_b, dt_bias, a_log, d_skip, norm_w):
    Bsz, T, _ = xbc.shape
    xbc = jax.nn.silu(causal_depthwise_conv(xbc, conv_w, conv_b))
    dt = jax.nn.softplus(dt_raw.astype(jnp.float32) + dt_bias.astype(jnp.float32))
    lead = CHUNK - N_META
    xbc_p = jnp.pad(xbc, ((0, 0), (lead, 0), (0, 0)))
    dt_p = jnp.pad(dt, ((0, 0), (lead, 0), (0, 0)))
    Tp = T + lead
    xs = xbc_p[..., :D_INNER].reshape(Bsz, Tp, SSM_HEADS, SSM_HEAD_DIM)
    bm = xbc_p[..., D_INNER:D_INNER + SSM_GROUPS * SSM_STATE].reshape(Bsz, Tp, SSM_GROUPS, SSM_STATE)
    cm = xbc_p[..., D_INNER + SSM_GROUPS * SSM_STATE:].reshape(Bsz, Tp, SSM_GROUPS, SSM_STATE)
    a = -jnp.exp(a_log.astype(jnp.float32))
    y = ssd_chunked(xs, dt_p, a, bm, cm)
    y = y + d_skip.astype(jnp.float32)[:, None] * xs.astype(jnp.float32)
    y = y[:, lead:].reshape(Bsz, T, D_INNER)
    g = (y * jax.nn.silu(z.astype(jnp.float32))).reshape(Bsz, T, SSM_GROUPS, D_INNER // SSM_GROUPS)
    g = rms_norm(g, norm_w.reshape(SSM_GROUPS, D_INNER // SSM_GROUPS))
    return g.reshape(Bsz, T, D_INNER).astype(z.dtype)


def indexer_sparse_attention(q, k, v, iq, ik, iw, k_sel):
    f32 = jnp.float32
    Bsz, T = q.shape[:2]
    nblk = -(-T // Q_BLOCK)
    pad = nblk * Q_BLOCK - T
    group = ATTN_HEADS // ATTN_KV_HEADS
    key_pos = jnp.arange(T)
    q_pos = jnp.arange(nblk * Q_BLOCK).reshape(nblk, Q_BLOCK)
    ikf = ik.astype(f32)
    gather = jax.vmap(lambda src, idx: src[idx])

    def to_blocks(a):
        a = jnp.pad(a, [(0, 0), (0, pad)] + [(0, 0)] * (a.ndim - 2))
        return jnp.moveaxis(a.reshape((Bsz, nblk, Q_BLOCK) + a.shape[2:]), 1, 0)

    def one_block(blk):
        qb, iqb, iwb, pos = blk
        logits = jnp.einsum('bqhd,bsd->bqhs', iqb.astype(f32), ikf)
        score = jnp.einsum('bqh,bqhs->bqs', iwb.astype(f32), jax.nn.relu(logits))
        visible = key_pos[None, :] <= pos[:, None]
        score = jnp.where(visible[None], score, -jnp.inf)
        _, sel = lax.top_k(score, k_sel)
        ok = sel <= pos[None, :, None]
        kg = gather(k, sel)
        vg = gather(v, sel)
        qg = qb.reshape(Bsz, Q_BLOCK, ATTN_KV_HEADS, group, ATTN_HEAD_DIM)
        s = jnp.einsum('bqgrd,bqkgd->bqgrk', qg, kg).astype(f32) * (ATTN_HEAD_DIM ** -0.5)
        s = jnp.where(ok[:, :, None, None, :], s, -jnp.inf)
        p = jax.nn.softmax(s, axis=-1).astype(v.dtype)
        o = jnp.einsum('bqgrk,bqkgd->bqgrd', p, vg)
        return o.reshape(Bsz, Q_BLOCK, ATTN_WIDTH)

    out = lax.map(one_block, (to_blocks(q), to_blocks(iq), to_blocks(iw), q_pos))
    out = jnp.moveaxis(out, 0, 1).reshape(Bsz, nblk * Q_BLOCK, ATTN_WIDTH)
    return out[:, :T]


def attn_branch(q, k, v, z, iq, ik, iw, q_norm_w, k_norm_w, idx_k_norm_w, cos_a, sin_a, cos_i, sin_i, k_sel):
    Bsz, T = q.shape[:2]
    q = q.reshape(Bsz, T, ATTN_HEADS, ATTN_HEAD_DIM)
    k = k.reshape(Bsz, T, ATTN_KV_HEADS, ATTN_HEAD_DIM)
    v = v.reshape(Bsz, T, ATTN_KV_HEADS, ATTN_HEAD_DIM)
    q = apply_partial_rope(rms_norm(q, q_norm_w), cos_a, sin_a)
    k = apply_partial_rope(rms_norm(k, k_norm_w), cos_a, sin_a)
    iq = apply_partial_rope(iq.reshape(Bsz, T, IDX_HEADS, IDX_DIM), cos_i, sin_i)
    ik = apply_partial_rope(rms_norm(ik, idx_k_norm_w), cos_i, sin_i)
    iw = iw * (IDX_HEADS ** -0.5 * IDX_DIM ** -0.5)
    o = indexer_sparse_attention(q, k, v, iq, ik, iw, k_sel)
    return o * jax.nn.silu(z)


def setup_inputs(seed: int = 0) -> dict:
    key = jax.random.key(seed)
    ks = jax.random.split(key, 16)
    f32 = jnp.float32

    def normal(k, shape, scale):
        return jax.random.normal(k, shape, f32) * scale

    dt0 = jnp.exp(jax.random.uniform(ks[6], (DEPTH, SSM_HEADS), f32, np.log(1e-3), np.log(1e-1)))
    return {
        "x": normal(ks[0], (BATCH, SEQ, D_MODEL), 1.0),
        "meta_tokens": normal(ks[1], (N_META, D_MODEL), 1.0),
        "norm_w": 1.0 + normal(ks[2], (DEPTH, D_MODEL), 0.02),
        "w_in": normal(ks[3], (DEPTH, D_MODEL, N_IN), D_MODEL ** -0.5),
        "conv_w": normal(ks[4], (DEPTH, CONV_WIDTH, CONV_DIM), CONV_WIDTH ** -0.5),
        "conv_b": normal(ks[5], (DEPTH, CONV_DIM), 0.02),
        "dt_bias": dt0 + jnp.log(-jnp.expm1(-dt0)),
        "a_log": jnp.log(jax.random.uniform(ks[7], (DEPTH, SSM_HEADS), f32, 1.0, 16.0)),
        "d_skip": 1.0 + normal(ks[8], (DEPTH, SSM_HEADS), 0.02),
        "ssm_norm_w": 1.0 + normal(ks[9], (DEPTH, D_INNER), 0.02),
        "w_ssm_out": normal(ks[10], (DEPTH, D_INNER, D_MODEL), D_INNER ** -0.5),
        "q_norm_w": 1.0 + normal(ks[11], (DEPTH, ATTN_HEAD_DIM), 0.02),
        "k_norm_w": 1.0 + normal(ks[12], (DEPTH, ATTN_HEAD_DIM), 0.02),
        "idx_k_norm_w": 1.0 + normal(ks[13], (DEPTH, IDX_DIM), 0.02),
        "w_attn_out": normal(ks[14], (DEPTH, ATTN_WIDTH, D_MODEL), ATTN_WIDTH ** -0.5),
        "w_out": normal(ks[15], (DEPTH, D_MODEL, D_MODEL), D_MODEL ** -0.5),
    }


def reference(x, meta_tokens, norm_w, w_in, conv_w, conv_b, dt_bias, a_log, d_skip, ssm_norm_w,
              w_ssm_out, q_norm_w, k_norm_w, idx_k_norm_w, w_attn_out, w_out):
    Bsz, seq = x.shape[0], x.shape[1]
    k_sel = min(TOPK_MAX, seq // 4)
    meta = jnp.broadcast_to(meta_tokens[None].astype(x.dtype), (Bsz, N_META, D_MODEL))
    h = jnp.concatenate([meta, x], axis=1)
    T = h.shape[1]
    positions = jnp.arange(T)
    cos_a, sin_a = rope_tables(positions, ATTN_HEAD_DIM)
    cos_i, sin_i = rope_tables(positions, IDX_DIM)
    for l in range(DEPTH):
        hn = rms_norm(h, norm_w[l])
        proj = hn @ w_in[l]
        (s_z, s_xbc, s_dt, a_q, a_k, a_v, a_z, i_q, i_k, i_w, g_s, g_a) = split_columns(proj)
        y_s = ssm_branch(s_z, s_xbc, s_dt, conv_w[l], conv_b[l], dt_bias[l], a_log[l], d_skip[l], ssm_norm_w[l])
        y_a = attn_branch(a_q, a_k, a_v, a_z, i_q, i_k, i_w, q_norm_w[l], k_norm_w[l], idx_k_norm_w[l],
                          cos_a, sin_a, cos_i, sin_i, k_sel)
        y_s = y_s @ w_ssm_out[l]
        y_a = y_a @ w_attn_out[l]
        merged = jax.nn.sigmoid(g_s) * y_s + jax.nn.sigmoid(g_a) * y_a
        h = h + merged @ w_out[l]
    return h[:, N_META:]
```

```python
import numpy as np
import ml_dtypes
from contextlib import ExitStack
import concourse.bass as bass
import concourse.mybir as mybir
from concourse.bass_utils import run_bass_kernel_spmd

F32 = mybir.dt.float32
BF16 = mybir.dt.bfloat16
AF = mybir.ActivationFunctionType
ALU = mybir.AluOpType
AX = mybir.AxisListType

D = 2048
NIN = 20624
DI = 4096
NH = 64
EPS = 1e-6
NIT = 26
NEG = -1.0e30
NDS = 12
SEM_EPOCH = 20000


class Buf:
    __slots__ = ('w', 'r', 'excl')

    def __init__(self, excl=False):
        self.w = None
        self.r = {}
        self.excl = excl


def PB():
    return Buf(True)


class Ctx:
    def __init__(self, nc, es):
        self.nc = nc
        self.es = es
        self.eng = {'pe': nc.tensor, 'act': nc.scalar, 'dve': nc.vector, 'pool': nc.gpsimd, 'sp': nc.sync}
        self.sem = {}
        self.cnt = {}
        self.nsem = 0
        self.waited = {e: {} for e in self.eng}
        for e in self.eng:
            self._newsem(e)
        self.dsem = {}
        self.dnext = {}
        for q in ('sp', 'act', 'pool'):
            self.dsem[q] = [[es.enter_context(nc.semaphore(f'dq_{q}_{i}')), 0] for i in range(NDS)]
            self.dnext[q] = 0
        self.nops = 0
        self.nwaits = 0

    def _newsem(self, e):
        self.nsem += 1
        self.sem[e] = self.es.enter_context(self.nc.semaphore(f's_{e}_{self.nsem}'))
        self.cnt[e] = 0

    def _wait(self, e, tok):
        sem, val, key, src = tok
        if self.waited[e].get(key, 0) >= val:
            return
        self.eng[e].wait_ge(sem, val)
        self.nwaits += 1
        self.waited[e][key] = val

    def _dep1(self, e, tok):
        if tok[3] == e and e == 'pe':
            return
        self._wait(e, tok)

    def _deps(self, e, reads, writes):
        for b in reads:
            if b.w is not None:
                self._dep1(e, b.w)
        for b in writes:
            if b.w is not None:
                self._dep1(e, b.w)
            for t in b.r.values():
                self._dep1(e, t)

    def _mark(self, tok, reads, writes):
        for b in writes:
            b.w = tok
            b.r = {}
        for b in reads:
            if tok[3] == 'dma':
                b.r[tok[2]] = tok
            else:
                b.r[tok[3]] = tok

    def op(self, e, fn, reads=(), writes=()):
        if any(b.excl for b in reads):
            writes = list(writes) + [b for b in reads if b.excl]
            reads = [b for b in reads if not b.excl]
        self._deps(e, reads, writes)
        inst = fn(self.eng[e])
        if self.cnt[e] >= SEM_EPOCH:
            self._newsem(e)
        self.cnt[e] += 1
        inst.then_inc(self.sem[e], 1)
        tok = (self.sem[e], self.cnt[e], id(self.sem[e]), e)
        self._mark(tok, reads, writes)
        self.nops += 1
        return tok

    def dma(self, q, out, in_, reads=(), writes=(), **kw):
        slot = self.dsem[q][self.dnext[q]]
        self.dnext[q] = (self.dnext[q] + 1) % NDS
        sem, cnt = slot
        if cnt > 0:
            self._wait(q, (sem, cnt, id(sem), 'dma'))
        self._deps(q, reads, writes)
        inst = self.eng[q].dma_start(out=out, in_=in_, **kw)
        inst.then_inc(sem, 16)
        slot[1] = cnt + 16
        tok = (sem, cnt + 16, id(sem), 'dma')
        self._mark(tok, reads, writes)
        self.nops += 1
        return tok

    def barrier(self):
        toks = [(self.sem[e], self.cnt[e], id(self.sem[e]), e) for e in self.eng if self.cnt[e] > 0]
        for q in self.dsem:
            for (s, cn) in self.dsem[q]:
                if cn > 0:
                    toks.append((s, cn, id(s), 'dma'))
        for e in self.eng:
            for t in toks:
                if t[3] != e:
                    self._wait(e, t)


def build(cfg):
    NT = cfg['NT']
    L = cfg['DEPTH']
    KSEL = cfg['KSEL']
    DBG = cfg.get('debug', False)
    PH = cfg.get('phases', 'ABCD')
    TP = NT * 128
    nc = bass.Bass("TRN2", target_bir_lowering=False)

    def din(name, shape, dt=F32):
        return nc.dram_tensor(name, list(shape), dt, kind="ExternalInput").ap()

    def dscr(name, shape, dt):
        return nc.dram_tensor(name, list(shape), dt, kind=("ExternalOutput" if DBG else "Internal")).ap()

    h0 = din("h0", [TP, D])
    w_in = din("w_in", [L, D, NIN])
    w_so = din("w_ssm_out", [L, DI, D])
    w_ao = din("w_attn_out", [L, D, D])
    w_o = din("w_out", [L, D, D])
    norm_w = din("norm_w", [L, D])
    convw_p = din("convw_p", [L, 128, 192])
    convb_p = din("convb_p", [L, 128, 48])
    conv_b = din("conv_b", [L, 6144])
    dt_bias = din("dt_bias", [L, NH])
    a_log = din("a_log", [L, NH])
    d_skip = din("d_skip", [L, NH])
    ssm_nw = din("ssm_norm_w", [L, DI])
    qnw = din("q_norm_w", [L, 128])
    knw = din("k_norm_w", [L, 128])
    iknw = din("idx_k_norm_w", [L, 64])
    c_identb = din("c_identb", [128, 128], BF16)
    c_identf = din("c_identf", [128, 128])
    c_trile = din("c_trile", [128, 128])
    c_strict = din("c_strict", [128, 128])
    c_triq = din("c_triq", [2, 128, 128])
    c_negq = din("c_negq", [2, 128, 128])
    c_pow2 = din("c_pow2", [128, NIT + 2])
    c_cosA = din("c_cosA", [128, TP])
    c_sinA = din("c_sinA", [128, TP])
    c_cosI = din("c_cosI", [128, TP])
    c_sinI = din("c_sinI", [128, TP])
    c_rA = din("c_rA", [128, 128], BF16)
    c_rI = din("c_rI", [128, 128], BF16)
    c_cosK = din("c_cosK", [128, NT, 8])
    c_sinK = din("c_sinK", [128, NT, 8])
    out = nc.dram_tensor("out", [(NT - 1) * 128, D], F32, kind="ExternalOutput").ap()

    HA = dscr("HA", [TP, D], F32)
    HB = dscr("HB", [TP, D], F32)
    Z = dscr("Z", [TP, DI], BF16)
    XBCT = dscr("XBCT", [48, 128, TP + 3], BF16)
    DT = dscr("DT", [TP, 64], F32)
    QT = dscr("QT", [16, 128, TP], BF16)
    KT = dscr("KT", [4, 128, TP], BF16)
    V = dscr("V", [TP, 512], BF16)
    AZT = dscr("AZT", [16, 128, TP], BF16)
    IQT = dscr("IQT", [8, 128, TP], BF16)
    IKW = dscr("IKW", [TP, 80], F32)
    GST = dscr("GST", [16, 128, TP], BF16)
    GAT = dscr("GAT", [16, 128, TP], BF16)
    GT = dscr("GT", [32, 128, TP], BF16)
    QR = dscr("QR", [16, 128, TP], BF16)
    KR = dscr("KR", [4, 128, TP], BF16)
    IQR = dscr("IQR", [16, 64, TP], BF16)
    OT = dscr("OT", [16, 128, TP], BF16)
    MT = dscr("MT", [16, 128, TP], BF16)

    dbuf = {n: Buf() for n in ['HA', 'HB', 'Z', 'XBCT', 'DT', 'QT', 'KT', 'V', 'AZT', 'IQT', 'IKW', 'GST', 'GAT',
                               'GT', 'QR', 'KR', 'IQR', 'OT', 'MT', 'out', 'h0']}

    es = ExitStack()
    with es:
        c = Ctx(nc, es)

        uid = [0]

        def sb(st, name, shape, dt):
            uid[0] += 1
            return st.enter_context(nc.sbuf_tensor(f"{name}_{uid[0]}", list(shape), dt))

        def ps(st, name, shape, dt):
            uid[0] += 1
            return st.enter_context(nc.psum_tensor(f"{name}_{uid[0]}", list(shape), dt))

        identb = sb(es, "identb", [128, 128], BF16)
        identf = sb(es, "identf", [128, 128], F32)
        onesb = sb(es, "onesb", [128, 128], BF16)
        onesf = sb(es, "onesf", [128, 128], F32)
        trile = sb(es, "trile", [128, 128], F32)
        strict = sb(es, "strict", [128, 128], F32)
        bconst = Buf()
        c.dma('sp', identb[:], c_identb, writes=[bconst])
        c.dma('sp', identf[:], c_identf, writes=[bconst])
        c.dma('sp', trile[:], c_trile, writes=[bconst])
        c.dma('sp', strict[:], c_strict, writes=[bconst])
        c.op('dve', lambda e: e.memset(onesb[:], 1.0), writes=[bconst])
        c.op('dve', lambda e: e.memset(onesf[:], 1.0), writes=[bconst])

        hseq = [(h0, 'h0')]
        for l in range(L):
            if l == L - 1:
                hseq.append((out, 'out'))
            else:
                hseq.append((HA, 'HA') if l % 2 == 0 else (HB, 'HB'))

        def phase_A(l, hin, hin_n):
            NBT = min(11, NT)
            with ExitStack() as st:
                hnT = sb(st, "hnT", [128, 16, NBT * 128], BF16)
                b_hnT = Buf()
                normw = sb(st, "normw", [128, D], F32)
                b_nw = Buf()
                c.dma('sp', normw[:], norm_w[l].partition_broadcast(128), writes=[b_nw])
                Ht = [sb(st, f"Ht{s}", [128, D], F32) for s in range(2)]
                b_Ht = [Buf(), Buf()]
                hnb = [sb(st, f"hnb{s}", [128, D], BF16) for s in range(2)]
                b_hnb = [Buf(), Buf()]
                junk = sb(st, "junkA", [128, D], BF16)
                b_junk = Buf()
                ssq = [sb(st, f"ssq{s}", [128, 1], F32) for s in range(2)]
                b_ss = [Buf(), Buf()]
                rs = [sb(st, f"rs{s}", [128, 1], F32) for s in range(2)]
                b_rs = [Buf(), Buf()]
                Wb = [sb(st, f"Wb{s}", [128, 16, 512], BF16) for s in range(3)]
                b_Wb = [Buf() for _ in range(3)]
                otb = [sb(st, f"otb{s}", [128, 512], BF16) for s in range(3)]
                b_otb = [Buf() for _ in range(3)]
                otf = [sb(st, f"otf{s}", [128, 128], F32) for s in range(2)]
                b_otf = [Buf() for _ in range(2)]
                ofm = [sb(st, f"ofm{s}", [128, NBT * 128], BF16) for s in range(3)]
                b_ofm = [Buf() for _ in range(3)]
                pt = [ps(st, f"ptA{s}", [128, 8, 128], BF16) for s in range(2)]
                b_pt = [PB(), PB()]
                pm = [ps(st, f"pmA{s}", [128, 512], F32) for s in range(4)]
                b_pm = [PB() for _ in range(4)]
                cnt = {'w': 0, 'pm': 0, 'otb': 0, 'otf': 0, 'ofm': 0, 'ev': 0}
                w_l = w_in[l].rearrange("(k p) n -> p k n", p=128)

                chunks = []
                for c0 in range(0, 4096, 512):
                    chunks.append(('tok', c0, 512, 'Z', c0))
                for c0 in range(4096, 10240, 512):
                    chunks.append(('fm', c0, 512, 'XBCT', (c0 - 4096) // 128))
                chunks.append(('tok', 10240, 64, 'DT', 0))
                for c0 in range(10304, 12352, 512):
                    chunks.append(('fm', c0, 512, 'QT', (c0 - 10304) // 128))
                chunks.append(('fm', 12352, 512, 'KT', 0))
                chunks.append(('tok', 12864, 512, 'V', 0))
                for c0 in range(13376, 15424, 512):
                    chunks.append(('fm', c0, 512, 'AZT', (c0 - 13376) // 128))
                for c0 in range(15424, 16448, 512):
                    chunks.append(('fm', c0, 512, 'IQT', (c0 - 15424) // 128))
                chunks.append(('tok', 16448, 80, 'IKW', 0))
                for c0 in range(16528, 18576, 512):
                    chunks.append(('fm', c0, 512, 'GST', (c0 - 16528) // 128))
                for c0 in range(18576, 20624, 512):
                    chunks.append(('fm', c0, 512, 'GAT', (c0 - 18576) // 128))
                dmap = {'Z': Z, 'XBCT': XBCT, 'DT': DT, 'QT': QT, 'KT': KT, 'V': V, 'AZT': AZT, 'IQT': IQT,
                        'IKW': IKW, 'GST': GST, 'GAT': GAT}

                def evac(dst_ap, src_ap, reads, writes):
                    cnt['ev'] += 1
                    if cnt['ev'] % 2 == 0:
                        c.op('act', lambda e: e.copy(out=dst_ap, in_=src_ap), reads=reads, writes=writes)
                    else:
                        c.op('dve', lambda e: e.tensor_copy(out=dst_ap, in_=src_ap), reads=reads, writes=writes)

                for t0 in range(0, NT, NBT):
                    nb = min(NBT, NT - t0)
                    for ti in range(nb):
                        i = t0 + ti
                        s = ti % 2
                        c.dma('sp', Ht[s][:], hin[i * 128:(i + 1) * 128, :], reads=[dbuf[hin_n]], writes=[b_Ht[s]])
                        c.op('act', lambda e: e.activation(out=junk[:], in_=Ht[s][:], func=AF.Square, accum_out=ssq[s][:]),
                             reads=[b_Ht[s]], writes=[b_junk, b_ss[s]])
                        c.op('act', lambda e: e.activation(out=rs[s][:], in_=ssq[s][:], func=AF.Ln, scale=1.0 / D, bias=EPS),
                             reads=[b_ss[s]], writes=[b_rs[s]])
                        c.op('act', lambda e: e.activation(out=rs[s][:], in_=rs[s][:], func=AF.Exp, scale=-0.5),
                             reads=[b_rs[s]], writes=[b_rs[s]])
                        c.op('dve', lambda e: e.scalar_tensor_tensor(out=hnb[s][:], in0=Ht[s][:], scalar=rs[s][:, 0:1], in1=normw[:],
                                                                     op0=ALU.mult, op1=ALU.mult),
                             reads=[b_Ht[s], b_rs[s], b_nw], writes=[b_hnb[s]])
                        for hh in range(2):
                            for k in range(8):
                                kk = hh * 8 + k
                                c.op('pe', lambda e: e.transpose(out=pt[hh][:, k, :], in_=hnb[s][:, kk * 128:(kk + 1) * 128], identity=identb[:]),
                                     reads=[b_hnb[s], bconst], writes=[b_pt[hh]])
                            evac(hnT[:, hh * 8:(hh + 1) * 8, ti * 128:(ti + 1) * 128], pt[hh][:], [b_pt[hh]], [b_hnT])
                    for (kind, c0, cw, dn, dof) in chunks[:cfg.get('nchunks', 1000)]:
                        wi = cnt['w'] % 3
                        cnt['w'] += 1
                        wb = Wb[wi]
                        c.dma('pool', wb[:, :, :cw], w_l[:, :, c0:c0 + cw], writes=[b_Wb[wi]])
                        dst = dmap[dn]
                        if kind == 'tok':
                            for ti in range(nb):
                                i = t0 + ti
                                pi = cnt['pm'] % 4
                                cnt['pm'] += 1
                                for k in range(16):
                                    c.op('pe', lambda e: e.matmul(pm[pi][:, :cw], lhsT=hnT[:, k, ti * 128:(ti + 1) * 128], rhs=wb[:, k, :cw],
                                                                  start=(k == 0), stop=(k == 15)),
                                         reads=[b_hnT, b_Wb[wi]], writes=[b_pm[pi]])
                                if dn in ('DT', 'IKW'):
                                    oi = cnt['otf'] % 2
                                    cnt['otf'] += 1
                                    evac(otf[oi][:, :cw], pm[pi][:, :cw], [b_pm[pi]], [b_otf[oi]])
                                    c.dma('sp', dst[i * 128:(i + 1) * 128, :], otf[oi][:, :cw], reads=[b_otf[oi]], writes=[dbuf[dn]])
                                else:
                                    oi = cnt['otb'] % 3
                                    cnt['otb'] += 1
                                    evac(otb[oi][:, :cw], pm[pi][:, :cw], [b_pm[pi]], [b_otb[oi]])
                                    c.dma('sp', dst[i * 128:(i + 1) * 128, dof:dof + cw], otb[oi][:, :cw], reads=[b_otb[oi]], writes=[dbuf[dn]])
                        else:
                            for j in range(cw // 128):
                                fi = cnt['ofm'] % 3
                                cnt['ofm'] += 1
                                for s0 in range(0, nb * 128, 512):
                                    sn = min(512, nb * 128 - s0)
                                    pi = cnt['pm'] % 4
                                    cnt['pm'] += 1
                                    for k in range(16):
                                        c.op('pe', lambda e: e.matmul(pm[pi][:, :sn], lhsT=wb[:, k, j * 128:(j + 1) * 128], rhs=hnT[:, k, s0:s0 + sn],
                                                                      start=(k == 0), stop=(k == 15)),
                                             reads=[b_hnT, b_Wb[wi]], writes=[b_pm[pi]])
                                    evac(ofm[fi][:, s0:s0 + sn], pm[pi][:, :sn], [b_pm[pi]], [b_ofm[fi]])
                                co = 3 if dn == 'XBCT' else 0
                                c.dma('sp', dst[dof + j][:, co + t0 * 128: co + (t0 + nb) * 128], ofm[fi][:, :nb * 128],
                                      reads=[b_ofm[fi]], writes=[dbuf[dn]])
            c.barrier()

        def phase_B(l):
            with ExitStack() as st:
                diagF = sb(st, "diagF", [128, 192, 128], BF16)
                b_diagF = Buf()
                cw_p = sb(st, "cw_p", [128, 192], F32)
                cb_p = sb(st, "cb_p", [128, 48], F32)
                cb_row = sb(st, "cb_row", [1, 6144], BF16)
                ones_row = sb(st, "ones_row", [1, 128], BF16)
                dtb_bc = sb(st, "dtb_bc", [128, NH], F32)
                a_bc = sb(st, "a_bc", [128, NH], F32)
                dsk_bc = sb(st, "dsk_bc", [128, NH], F32)
                snw_bc = sb(st, "snw_bc", [128, DI], F32)
                b_par = Buf()
                c.dma('sp', cw_p[:], convw_p[l], writes=[b_par])
                c.dma('sp', cb_p[:], convb_p[l], writes=[b_par])
                c.dma('pool', cb_row[:], conv_b[l].unsqueeze(0), writes=[b_par])
                c.dma('sp', dtb_bc[:], dt_bias[l].partition_broadcast(128), writes=[b_par])
                c.dma('sp', a_bc[:], a_log[l].partition_broadcast(128), writes=[b_par])
                c.dma('sp', dsk_bc[:], d_skip[l].partition_broadcast(128), writes=[b_par])
                c.dma('sp', snw_bc[:], ssm_nw[l].partition_broadcast(128), writes=[b_par])
                c.op('dve', lambda e: e.memset(ones_row[:], 1.0), writes=[b_par])
                c.op('act', lambda e: e.activation(out=a_bc[:], in_=a_bc[:], func=AF.Exp), reads=[b_par], writes=[b_par])
                c.op('dve', lambda e: e.tensor_scalar(out=a_bc[:], in0=a_bc[:], scalar1=-1.0, scalar2=None, op0=ALU.mult),
                     reads=[b_par], writes=[b_par])

                for a in range(192):
                    c.op('dve', lambda e: e.tensor_scalar(out=diagF[:, a, :], in0=identb[:], scalar1=cw_p[:, a:a + 1], scalar2=None, op0=ALU.mult),
                         reads=[b_par, bconst], writes=[b_diagF])
                Hin = sb(st, "Hin", [128, 8, 512], F32)
                Hinb = sb(st, "Hinb", [128, 8, 512], BF16)
                b_Hin = [Buf() for _ in range(8)]
                b_Hinb = [Buf() for _ in range(8)]
                c.op('dve', lambda e: e.memset(Hin[:], 0.0), writes=b_Hin)
                c.op('pool', lambda e: e.memset(Hinb[:], 0.0), writes=b_Hinb)

                def T2(name, shape, dt, n=2):
                    return [sb(st, f"{name}{s}", shape, dt) for s in range(n)], [Buf() for _ in range(n)]

                u, b_u = T2("u", [128, 6, 515], BF16)
                zt_, b_zt = T2("zt", [128, 512], BF16)
                dtr, b_dtr = T2("dtr", [128, NH], F32)
                dtt, b_dt = T2("dtt", [128, NH], F32, 8)
                tA, b_tA = T2("tA", [128, NH], F32)
                tB, b_tB = T2("tB", [128, NH], F32)
                adt, b_adt = T2("adt", [128, NH], F32, 8)
                acs_sb, b_acs = T2("acs_sb", [128, NH], F32)
                eacs, b_eacs = T2("eacs", [128, NH], F32, 8)
                dte, b_dte = T2("dte", [128, NH], F32, 8)
                dec, b_dec = T2("dec", [128, NH], F32, 8)
                rhsD, b_rhsD = T2("rhsD", [128, 8, 128], F32)
                Eb, b_E = T2("Eb", [128, 8, 128], BF16)
                MTb, b_MT = T2("MTb", [128, 8, 128], BF16)
                xtm, b_xtm = T2("xtm", [128, 512], BF16)
                xdt, b_xdt = T2("xdt", [128, 512], BF16)
                xw, b_xw = T2("xw", [128, 512], BF16)
                xD, b_xD = T2("xD", [128, 512], BF16)
                BTt, b_BT = T2("BTt", [128, 128], BF16)
                CTt, b_CT = T2("CTt", [128, 128], BF16)
                Btm, b_Btm = T2("Btm", [128, 128], BF16)
                cbm, b_cbm = T2("cbm", [128, 128], BF16)
                tmp, b_tmp = T2("tmpB", [128, 512], F32, 1)
                ysb, b_ysb = T2("ysb", [128, 512], F32, 1)
                sz, b_sz = T2("sz", [128, 512], F32, 1)
                gy, b_gy = T2("gy", [128, 512], F32, 1)
                junkB, b_junkB = T2("junkB", [128, 512], BF16, 1)
                gss, b_gss = T2("gss", [128, 1], F32, 1)
                grs, b_grs = T2("grs", [128, 1], F32, 1)
                gn, b_gn = T2("gn", [128, 512], BF16)
                gst, b_gst = T2("gst", [128, 4, 512], BF16)
                hup, b_hup = T2("hup", [128, 512], F32, 1)

                xps = ps(st, "xps", [128, 512], F32)
                bank1 = ps(st, "bank1", [128, 512], F32)
                bcps = bank1[:, 0:384]
                btps = bank1[:, 384:512].bitcast(BF16)
                Dps = [ps(st, f"Dps{s}", [128, 512], F32) for s in range(2)]
                yps = ps(st, "yps", [128, 512], F32)
                ups = ps(st, "ups", [128, 512], F32)
                sps = ps(st, "sps", [128, 512], F32)
                bank7 = ps(st, "bank7", [128, 512], F32)
                acsps = bank7[:, 0:256]
                trps = bank7[:, 256:512].bitcast(BF16).rearrange("p (a t) -> p a t", a=4)
                b_xps, b_bcps, b_yps, b_ups, b_sps, b_acsps = [PB() for _ in range(6)]
                b_btps = b_bcps
                b_trps = b_acsps
                b_Dps = [PB(), PB()]

                n4 = (NT + 3) // 4
                for ib in range(n4 if cfg.get('lvlB', 99) > -3 else 0):
                    tiles = list(range(ib * 4, min(NT, ib * 4 + 4)))
                    ntl = len(tiles)
                    ncol = ntl * 128 + 3
                    for ti, i in enumerate(tiles):
                        sd = (ib % 2) * 4 + ti
                        s = i % 2
                        c.dma('sp', dtr[s][:], DT[i * 128:(i + 1) * 128, :], reads=[dbuf['DT']], writes=[b_dtr[s]])
                        c.op('dve', lambda e: e.tensor_tensor(out=tA[s][:], in0=dtr[s][:], in1=dtb_bc[:], op=ALU.add),
                             reads=[b_dtr[s], b_par], writes=[b_tA[s]])
                        c.op('act', lambda e: e.activation(out=tB[s][:], in_=tA[s][:], func=AF.Abs),
                             reads=[b_tA[s]], writes=[b_tB[s]])
                        c.op('act', lambda e: e.activation(out=tB[s][:], in_=tB[s][:], func=AF.Exp, scale=-1.0),
                             reads=[b_tB[s]], writes=[b_tB[s]])
                        c.op('act', lambda e: e.activation(out=tB[s][:], in_=tB[s][:], func=AF.Ln, bias=1.0),
                             reads=[b_tB[s]], writes=[b_tB[s]])
                        c.op('dve', lambda e: e.scalar_tensor_tensor(out=dtt[sd][:], in0=tA[s][:], scalar=0.0, in1=tB[s][:], op0=ALU.max, op1=ALU.add),
                             reads=[b_tA[s], b_tB[s]], writes=[b_dt[sd]])
                        if i == 0:
                            c.op('dve', lambda e: e.memset(dtt[sd][0:112, :], 0.0), writes=[b_dt[sd]])
                        c.op('dve', lambda e: e.tensor_tensor(out=adt[sd][:], in0=dtt[sd][:], in1=a_bc[:], op=ALU.mult),
                             reads=[b_dt[sd], b_par], writes=[b_adt[sd]])
                        c.op('pe', lambda e: e.matmul(acsps[:, 0:64], lhsT=trile[:], rhs=adt[sd][:], start=True, stop=True),
                             reads=[b_adt[sd], bconst], writes=[b_acsps])
                        c.op('pe', lambda e: e.matmul(acsps[:, 64:128], lhsT=onesf[:], rhs=adt[sd][:], start=True, stop=True),
                             reads=[b_adt[sd], bconst], writes=[b_acsps])
                        c.op('act', lambda e: e.copy(out=acs_sb[s][:], in_=acsps[:, 0:64]), reads=[b_acsps], writes=[b_acs[s]])
                        c.op('act', lambda e: e.activation(out=eacs[sd][:], in_=acsps[:, 0:64], func=AF.Exp), reads=[b_acsps], writes=[b_eacs[sd]])
                        c.op('act', lambda e: e.activation(out=dec[sd][:], in_=acsps[:, 64:128], func=AF.Exp), reads=[b_acsps], writes=[b_dec[sd]])
                        c.op('dve', lambda e: e.tensor_tensor(out=dte[sd][:], in0=acsps[:, 64:128], in1=acs_sb[s][:], op=ALU.subtract),
                             reads=[b_acsps, b_acs[s]], writes=[b_dte[sd]])
                        c.op('act', lambda e: e.activation(out=dte[sd][:], in_=dte[sd][:], func=AF.Exp), reads=[b_dte[sd]], writes=[b_dte[sd]])

                    for g in range(8):
                        us = (ib * 8 + g) % 2
                        c0_ = 3 if ib == 0 else 0
                        if ib == 0:
                            c.op('pool', lambda e: e.memset(u[us][:, :, 0:3], 0.0), writes=[b_u[us]])
                        c.dma('sp', u[us][:, 0:4, c0_:ncol], XBCT[4 * g:4 * g + 4, :, ib * 512 + c0_: ib * 512 + ncol].rearrange("a p c -> p a c"),
                              reads=[dbuf['XBCT']], writes=[b_u[us]])
                        c.dma('sp', u[us][:, 4, c0_:ncol], XBCT[32 + g, :, ib * 512 + c0_: ib * 512 + ncol], reads=[dbuf['XBCT']], writes=[b_u[us]])
                        c.dma('sp', u[us][:, 5, c0_:ncol], XBCT[40 + g, :, ib * 512 + c0_: ib * 512 + ncol], reads=[dbuf['XBCT']], writes=[b_u[us]])
                        gs_ = (ib * 8 + g) % 2
                        diag = diagF
                        b_diag = b_diagF
                        for ti, i in enumerate(tiles):
                            s = (i * 8 + g) % 2
                            o = ti * 128
                            if cfg.get('lvlB', 99) < -1:
                                continue
                            hs = slice(8 * g, 8 * g + 8)
                            sd = (ib % 2) * 4 + ti
                            for a in range(4):
                                ct = 4 * g + a
                                for k in range(4):
                                    c.op('pe', lambda e: e.matmul(xps[:, a * 128:(a + 1) * 128], lhsT=u[us][:, a, o + k:o + k + 128], rhs=diag[:, ct * 4 + k, :],
                                                                  start=(k == 0), stop=False),
                                         reads=[b_u[us], b_diag], writes=[b_xps])
                                c.op('pe', lambda e: e.matmul(xps[:, a * 128:(a + 1) * 128], lhsT=ones_row[0:1, :], rhs=cb_row[0:1, ct * 128:(ct + 1) * 128],
                                                              start=False, stop=True),
                                     reads=[b_par], writes=[b_xps])
                            c.op('act', lambda e: e.activation(out=xtm[s][:], in_=xps[:], func=AF.Silu), reads=[b_xps], writes=[b_xtm[s]])
                            if cfg.get('lvlB', 99) < 2:
                                continue
                            for (which, a, ct) in ((0, 4, 32 + g), (1, 5, 40 + g)):
                                for k in range(4):
                                    c.op('pe', lambda e: e.matmul(bcps[:, which * 128:(which + 1) * 128], lhsT=diag[:, ct * 4 + k, :], rhs=u[us][:, a, o + k:o + k + 128],
                                                                  start=(k == 0), stop=(k == 3)),
                                         reads=[b_u[us], b_diag], writes=[b_bcps])
                            c.op('act', lambda e: e.activation(out=BTt[s][:], in_=bcps[:, 0:128], func=AF.Silu, bias=cb_p[:, 32 + g:33 + g]),
                                 reads=[b_bcps, b_par], writes=[b_BT[s]])
                            c.op('act', lambda e: e.activation(out=CTt[s][:], in_=bcps[:, 128:256], func=AF.Silu, bias=cb_p[:, 40 + g:41 + g]),
                                 reads=[b_bcps, b_par], writes=[b_CT[s]])
                            c.op('pe', lambda e: e.transpose(out=btps[:, 0:128], in_=BTt[s][:], identity=identb[:]), reads=[b_BT[s], bconst], writes=[b_btps])
                            c.op('dve', lambda e: e.tensor_copy(out=Btm[s][:], in_=btps[:, 0:128]), reads=[b_btps], writes=[b_Btm[s]])
                            c.op('pe', lambda e: e.matmul(bcps[:, 256:384], lhsT=BTt[s][:], rhs=CTt[s][:], start=True, stop=True),
                                 reads=[b_BT[s], b_CT[s]], writes=[b_bcps])
                            c.op('dve', lambda e: e.tensor_tensor(out=cbm[s][:], in0=bcps[:, 256:384], in1=trile[:], op=ALU.mult),
                                 reads=[b_bcps, bconst], writes=[b_cbm[s]])
                            if cfg.get('lvlB', 99) < 3:
                                continue
                            c.op('dve', lambda e: e.tensor_tensor(out=rhsD[s][:], in0=trile[:].unsqueeze(1).to_broadcast([128, 8, 128]),
                                                                  in1=adt[sd][:, hs].unsqueeze(2).to_broadcast([128, 8, 128]), op=ALU.mult),
                                 reads=[b_adt[sd], bconst], writes=[b_rhsD[s]])
                            for hh in range(2):
                                c.op('pe', lambda e: e.matmul(Dps[hh][:], lhsT=strict[:], rhs=rhsD[s][:, hh * 4:(hh + 1) * 4, :], start=True, stop=True),
                                     reads=[b_rhsD[s], bconst], writes=[b_Dps[hh]])
                                c.op('act', lambda e: e.activation(out=Eb[s][:, hh * 4:(hh + 1) * 4, :], in_=Dps[hh][:], func=AF.Exp),
                                     reads=[b_Dps[hh]], writes=[b_E[s]])
                            c.op('dve', lambda e: e.tensor_tensor(out=MTb[s][:], in0=Eb[s][:], in1=cbm[s][:].unsqueeze(1).to_broadcast([128, 8, 128]), op=ALU.mult),
                                 reads=[b_E[s], b_cbm[s]], writes=[b_MT[s]])
                            if cfg.get('lvlB', 99) < 4:
                                continue
                            x3 = xtm[s][:].rearrange("p (h q) -> p h q", h=8)
                            c.op('dve', lambda e: e.tensor_tensor(out=xdt[s][:].rearrange("p (h q) -> p h q", h=8), in0=x3,
                                                                  in1=dtt[sd][:, hs].unsqueeze(2).to_broadcast([128, 8, 64]), op=ALU.mult),
                                 reads=[b_xtm[s], b_dt[sd]], writes=[b_xdt[s]])
                            c.op('pool', lambda e: e.tensor_tensor(out=xD[s][:].rearrange("p (h q) -> p h q", h=8), in0=x3,
                                                                   in1=dsk_bc[:, hs].unsqueeze(2).to_broadcast([128, 8, 64]), op=ALU.mult),
                                 reads=[b_xtm[s], b_par], writes=[b_xD[s]])
                            c.op('pool', lambda e: e.tensor_tensor(out=xw[s][:].rearrange("p (h q) -> p h q", h=8), in0=xdt[s][:].rearrange("p (h q) -> p h q", h=8),
                                                                   in1=dte[sd][:, hs].unsqueeze(2).to_broadcast([128, 8, 64]), op=ALU.mult),
                                 reads=[b_xdt[s], b_dte[sd]], writes=[b_xw[s]])
                            c.op('pe', lambda e: e.matmul(yps[:], lhsT=identb[:], rhs=xD[s][:], start=True, stop=False, skip_group_check=True),
                                 reads=[b_xD[s], bconst], writes=[b_yps])
                            for r in range(8):
                                c.op('pe', lambda e: e.matmul(yps[:, r * 64:(r + 1) * 64], lhsT=MTb[s][:, r, :], rhs=xdt[s][:, r * 64:(r + 1) * 64], start=False, stop=(r == 7),
                                                              skip_group_check=True),
                                     reads=[b_MT[s], b_xdt[s]], writes=[b_yps])
                            if cfg.get('lvlB', 99) < 5:
                                continue
                            c.op('pe', lambda e: e.matmul(ups[:], lhsT=CTt[s][:], rhs=Hinb[:, g, :], start=True, stop=True),
                                 reads=[b_CT[s], b_Hinb[g]], writes=[b_ups])
                            c.op('dve', lambda e: e.tensor_tensor(out=tmp[0][:].rearrange("p (h q) -> p h q", h=8), in0=ups[:].rearrange("p (h q) -> p h q", h=8),
                                                                  in1=eacs[sd][:, hs].unsqueeze(2).to_broadcast([128, 8, 64]), op=ALU.mult),
                                 reads=[b_ups, b_eacs[sd]], writes=[b_tmp[0]])
                            c.op('dve', lambda e: e.tensor_tensor(out=ysb[0][:], in0=yps[:], in1=tmp[0][:], op=ALU.add),
                                 reads=[b_yps, b_tmp[0]], writes=[b_ysb[0]])
                            c.op('pe', lambda e: e.matmul(sps[:], lhsT=Btm[s][:], rhs=xw[s][:], start=True, stop=True),
                                 reads=[b_Btm[s], b_xw[s]], writes=[b_sps])
                            c.op('dve', lambda e: e.tensor_tensor(out=hup[0][:].rearrange("p (h q) -> p h q", h=8), in0=Hin[:, g, :].rearrange("p (h q) -> p h q", h=8),
                                                                  in1=dec[sd][:, hs].unsqueeze(2).to_broadcast([128, 8, 64]), op=ALU.mult),
                                 reads=[b_Hin[g], b_dec[sd]], writes=[b_hup[0]])
                            c.op('dve', lambda e: e.tensor_tensor(out=Hin[:, g, :], in0=sps[:], in1=hup[0][:], op=ALU.add),
                                 reads=[b_sps, b_hup[0]], writes=[b_Hin[g]])
                            c.op('act', lambda e: e.copy(out=Hinb[:, g, :], in_=Hin[:, g, :]), reads=[b_Hin[g]], writes=[b_Hinb[g]])
                            if cfg.get('lvlB', 99) < 6:
                                continue
                            zs = (i * 8 + g) % 2
                            c.dma('sp', zt_[zs][:], Z[i * 128:(i + 1) * 128, 512 * g:512 * (g + 1)], reads=[dbuf['Z']], writes=[b_zt[zs]])
                            c.op('act', lambda e: e.activation(out=sz[0][:], in_=zt_[zs][:], func=AF.Silu), reads=[b_zt[zs]], writes=[b_sz[0]])
                            c.op('pool', lambda e: e.tensor_tensor(out=gy[0][:], in0=ysb[0][:], in1=sz[0][:], op=ALU.mult),
                                 reads=[b_ysb[0], b_sz[0]], writes=[b_gy[0]])
                            c.op('act', lambda e: e.activation(out=junkB[0][:], in_=gy[0][:], func=AF.Square, accum_out=gss[0][:]),
                                 reads=[b_gy[0]], writes=[b_junkB[0], b_gss[0]])
                            c.op('act', lambda e: e.activation(out=grs[0][:], in_=gss[0][:], func=AF.Ln, scale=1.0 / 512, bias=EPS),
                                 reads=[b_gss[0]], writes=[b_grs[0]])
                            c.op('act', lambda e: e.activation(out=grs[0][:], in_=grs[0][:], func=AF.Exp, scale=-0.5),
                                 reads=[b_grs[0]], writes=[b_grs[0]])
                            c.op('dve', lambda e: e.scalar_tensor_tensor(out=gn[s][:], in0=gy[0][:], scalar=grs[0][:, 0:1], in1=snw_bc[:, 512 * g:512 * (g + 1)],
                                                                         op0=ALU.mult, op1=ALU.mult),
                                 reads=[b_gy[0], b_grs[0], b_par], writes=[b_gn[s]])
                            for a in range(4):
                                c.op('pe', lambda e: e.transpose(out=trps[:, a, :], in_=gn[s][:, a * 128:(a + 1) * 128], identity=identb[:]),
                                     reads=[b_gn[s], bconst], writes=[b_trps])
                            c.op('act', lambda e: e.copy(out=gst[gs_][:, :, o:o + 128], in_=trps[:]), reads=[b_trps], writes=[b_gst[gs_]])
                        c.dma('sp', GT[4 * g:4 * g + 4, :, ib * 512: ib * 512 + ntl * 128].rearrange("a p c -> p a c"), gst[gs_][:, :, :ntl * 128],
                              reads=[b_gst[gs_]], writes=[dbuf['GT']])
            c.barrier()

        def phase_C0(l):
            with ExitStack() as st:
                cosA = sb(st, "cosA", [128, TP], F32)
                sinA = sb(st, "sinA", [128, TP], F32)
                cosI = sb(st, "cosI", [128, TP], F32)
                sinI = sb(st, "sinI", [128, TP], F32)
                rA = sb(st, "rA", [128, 128], BF16)
                rI = sb(st, "rI", [128, 128], BF16)
                qn_w = sb(st, "qn_w", [128, 1], F32)
                kn_w = sb(st, "kn_w", [128, 1], F32)
                b_t = Buf()
                c.dma('sp', cosA[:], c_cosA, writes=[b_t])
                c.dma('sp', sinA[:], c_sinA, writes=[b_t])
                c.dma('sp', cosI[:], c_cosI, writes=[b_t])
                c.dma('sp', sinI[:], c_sinI, writes=[b_t])
                c.dma('sp', rA[:], c_rA, writes=[b_t])
                c.dma('sp', rI[:], c_rI, writes=[b_t])
                c.dma('sp', qn_w[:], qnw[l].unsqueeze(1), writes=[b_t])
                c.dma('sp', kn_w[:], knw[l].unsqueeze(1), writes=[b_t])
                xin = [sb(st, f"xin{s}", [128, 512], BF16) for s in range(2)]
                b_xin = [Buf(), Buf()]
                sq = sb(st, "sqC", [128, 512], BF16)
                b_sq = Buf()
                rstd = sb(st, "rstdC", [128, 512], F32)
                b_rstd = Buf()
                xn = sb(st, "xnC", [128, 512], BF16)
                b_xn = Buf()
                t1 = sb(st, "t1C", [128, 512], F32)
                b_t1 = Buf()
                t2 = sb(st, "t2C", [128, 512], F32)
                b_t2 = Buf()
                xo = [sb(st, f"xoC{s}", [128, 512], BF16) for s in range(2)]
                b_xo = [Buf(), Buf()]
                ssp = ps(st, "sspC", [128, 512], F32)
                b_ssp = PB()
                rotp = ps(st, "rotpC", [128, 512], F32)
                b_rotp = PB()
                n = 0
                jobs = [('q', h) for h in range(16)] + [('k', h) for h in range(4)] + [('i', h) for h in range(8)]
                for (kind, h) in jobs:
                    for s0 in range(0, TP, 512):
                        sn = min(512, TP - s0)
                        s = n % 2
                        n += 1
                        src, sname = {'q': (QT, 'QT'), 'k': (KT, 'KT'), 'i': (IQT, 'IQT')}[kind]
                        c.dma('sp', xin[s][:, :sn], src[h][:, s0:s0 + sn], reads=[dbuf[sname]], writes=[b_xin[s]])
                        if kind in ('q', 'k'):
                            wv = qn_w if kind == 'q' else kn_w
                            c.op('act', lambda e: e.activation(out=sq[:, :sn], in_=xin[s][:, :sn], func=AF.Square), reads=[b_xin[s]], writes=[b_sq])
                            c.op('pe', lambda e: e.matmul(ssp[:, :sn], lhsT=onesb[:], rhs=sq[:, :sn], start=True, stop=True), reads=[b_sq, bconst], writes=[b_ssp])
                            c.op('act', lambda e: e.activation(out=rstd[:, :sn], in_=ssp[:, :sn], func=AF.Ln, scale=1.0 / 128, bias=EPS), reads=[b_ssp], writes=[b_rstd])
                            c.op('act', lambda e: e.activation(out=rstd[:, :sn], in_=rstd[:, :sn], func=AF.Exp, scale=-0.5), reads=[b_rstd], writes=[b_rstd])
                            c.op('dve', lambda e: e.scalar_tensor_tensor(out=xn[:, :sn], in0=xin[s][:, :sn], scalar=wv[:, 0:1], in1=rstd[:, :sn], op0=ALU.mult, op1=ALU.mult),
                                 reads=[b_xin[s], b_rstd, b_t], writes=[b_xn])
                            xsrc, b_xsrc, rm, cs, sn_ = xn, b_xn, rA, cosA, sinA
                        else:
                            xsrc, b_xsrc, rm, cs, sn_ = xin[s], b_xin[s], rI, cosI, sinI
                        c.op('pe', lambda e: e.matmul(rotp[:, :sn], lhsT=rm[:], rhs=xsrc[:, :sn], start=True, stop=True), reads=[b_xsrc, b_t], writes=[b_rotp])
                        c.op('pool', lambda e: e.tensor_tensor(out=t1[:, :sn], in0=xsrc[:, :sn], in1=cs[:, s0:s0 + sn], op=ALU.mult), reads=[b_xsrc, b_t], writes=[b_t1])
                        c.op('dve', lambda e: e.tensor_tensor(out=t2[:, :sn], in0=rotp[:, :sn], in1=sn_[:, s0:s0 + sn], op=ALU.mult), reads=[b_rotp, b_t], writes=[b_t2])
                        c.op('dve', lambda e: e.tensor_tensor(out=xo[s][:, :sn], in0=t1[:, :sn], in1=t2[:, :sn], op=ALU.add), reads=[b_t1, b_t2], writes=[b_xo[s]])
                        if kind == 'q':
                            c.dma('sp', QR[h][:, s0:s0 + sn], xo[s][:, :sn], reads=[b_xo[s]], writes=[dbuf['QR']])
                        elif kind == 'k':
                            c.dma('sp', KR[h][:, s0:s0 + sn], xo[s][:, :sn], reads=[b_xo[s]], writes=[dbuf['KR']])
                        else:
                            c.dma('sp', IQR[2 * h][:, s0:s0 + sn], xo[s][0:64, :sn], reads=[b_xo[s]], writes=[dbuf['IQR']])
                            c.dma('sp', IQR[2 * h + 1][:, s0:s0 + sn], xo[s][64:128, :sn], reads=[b_xo[s]], writes=[dbuf['IQR']])
            c.barrier()

        def phase_C(l):
            with ExitStack() as st:
                KRs = sb(st, "KRs", [128, 4, TP], BF16)
                Vs = sb(st, "Vs", [128, NT, 512], BF16)
                IKT = sb(st, "IKT", [64, TP], BF16)
                b_KR, b_V, b_IKT = Buf(), Buf(), Buf()
                c.dma('sp', KRs[:], KR.rearrange("a p c -> p a c"), reads=[dbuf['KR']], writes=[b_KR])
                for i in range(NT):
                    c.dma('sp', Vs[:, i, :], V[i * 128:(i + 1) * 128, :], reads=[dbuf['V']], writes=[b_V])
                cosK = sb(st, "cosK", [128, NT, 8], F32)
                sinK = sb(st, "sinK", [128, NT, 8], F32)
                iknw_bc = sb(st, "iknw_bc", [128, 64], F32)
                triq = sb(st, "triq", [128, 2, 128], F32)
                negq = sb(st, "negq", [128, 2, 128], F32)
                pow2 = sb(st, "pow2", [128, NIT + 2], F32)
                b_t = Buf()
                c.dma('sp', cosK[:], c_cosK, writes=[b_t])
                c.dma('sp', sinK[:], c_sinK, writes=[b_t])
                c.dma('sp', iknw_bc[:], iknw[l].partition_broadcast(128), writes=[b_t])
                c.dma('sp', triq[:], c_triq.rearrange("a p c -> p a c"), writes=[b_t])
                c.dma('sp', negq[:], c_negq.rearrange("a p c -> p a c"), writes=[b_t])
                c.dma('sp', pow2[:], c_pow2, writes=[b_t])
                ikw = [sb(st, f"ikw{s}", [128, 80], F32) for s in range(2)]
                b_ikw = [Buf(), Buf()]
                kj = sb(st, "kj", [128, 64], F32)
                kss = sb(st, "kss", [128, 1], F32)
                krs = sb(st, "krs", [128, 1], F32)
                kn = sb(st, "kn", [128, 64], F32)
                kr = sb(st, "kr", [128, 64], BF16)
                ka = sb(st, "ka", [128, 8], F32)
                kb = sb(st, "kb", [128, 8], F32)
                b_kj, b_kss, b_krs, b_kn, b_kr, b_ka, b_kb = [Buf() for _ in range(7)]
                tps = ps(st, "tpsC", [128, 8, 128], BF16)
                b_tps = PB()
                for i in range(NT):
                    s = i % 2
                    c.dma('sp', ikw[s][:], IKW[i * 128:(i + 1) * 128, :], reads=[dbuf['IKW']], writes=[b_ikw[s]])
                    c.op('act', lambda e: e.activation(out=kj[:], in_=ikw[s][:, 0:64], func=AF.Square, accum_out=kss[:]), reads=[b_ikw[s]], writes=[b_kj, b_kss])
                    c.op('act', lambda e: e.activation(out=krs[:], in_=kss[:], func=AF.Ln, scale=1.0 / 64, bias=EPS), reads=[b_kss], writes=[b_krs])
                    c.op('act', lambda e: e.activation(out=krs[:], in_=krs[:], func=AF.Exp, scale=-0.5), reads=[b_krs], writes=[b_krs])
                    c.op('dve', lambda e: e.scalar_tensor_tensor(out=kn[:], in0=ikw[s][:, 0:64], scalar=krs[:, 0:1], in1=iknw_bc[:], op0=ALU.mult, op1=ALU.mult),
                         reads=[b_ikw[s], b_krs, b_t], writes=[b_kn])
                    c.op('dve', lambda e: e.tensor_copy(out=kr[:, 16:64], in_=kn[:, 16:64]), reads=[b_kn], writes=[b_kr])
                    c.op('dve', lambda e: e.tensor_tensor(out=ka[:], in0=kn[:, 0:8], in1=cosK[:, i, :], op=ALU.mult), reads=[b_kn, b_t], writes=[b_ka])
                    c.op('dve', lambda e: e.tensor_tensor(out=kb[:], in0=kn[:, 8:16], in1=sinK[:, i, :], op=ALU.mult), reads=[b_kn, b_t], writes=[b_kb])
                    c.op('dve', lambda e: e.tensor_tensor(out=kr[:, 0:8], in0=ka[:], in1=kb[:], op=ALU.subtract), reads=[b_ka, b_kb], writes=[b_kr])
                    c.op('dve', lambda e: e.tensor_tensor(out=ka[:], in0=kn[:, 8:16], in1=cosK[:, i, :], op=ALU.mult), reads=[b_kn, b_t], writes=[b_ka])
                    c.op('dve', lambda e: e.tensor_tensor(out=kb[:], in0=kn[:, 0:8], in1=sinK[:, i, :], op=ALU.mult), reads=[b_kn, b_t], writes=[b_kb])
                    c.op('dve', lambda e: e.tensor_tensor(out=kr[:, 8:16], in0=ka[:], in1=kb[:], op=ALU.add), reads=[b_ka, b_kb], writes=[b_kr])
                    c.op('pe', lambda e: e.transpose(out=tps[0:64, 0, :], in_=kr[:], identity=identb[:]), reads=[b_kr, bconst], writes=[b_tps])
                    c.op('act', lambda e: e.copy(out=IKT[:, i * 128:(i + 1) * 128], in_=tps[0:64, 0, :]), reads=[b_tps], writes=[b_IKT])

                score = sb(st, "score", [128, TP], F32)
                mask = sb(st, "maskC", [128, TP], BF16)
                junk = mask
                maskT = sb(st, "maskT", [128, NT, 128], BF16)
                b_score, b_mask, b_maskT = Buf(), Buf(), Buf()
                b_junk = b_mask
                iqr = [sb(st, f"iqr{s}", [64, 16, 128], BF16) for s in range(2)]
                b_iqr = [Buf(), Buf()]
                qg = [sb(st, f"qg{s}", [128, 16, 128], BF16) for s in range(2)]
                b_qg = [Buf(), Buf()]
                azt = [sb(st, f"azt{s}", [128, 16, 128], BF16) for s in range(2)]
                b_azt = [Buf(), Buf()]
                Rr = [sb(st, f"Rr{s}", [128, 512], F32) for s in range(2)]
                b_Rr = [Buf(), Buf()]
                absw = sb(st, "absw", [128, 16], F32)
                sgnw = sb(st, "sgnw", [128, 16], F32)
                b_w = Buf()
                amax = sb(st, "amax", [128, 1], F32)
                lo = sb(st, "lo", [128, 1], F32)
                mid = sb(st, "mid", [128, 1], F32)
                wk = sb(st, "wk", [128, NIT + 2], F32)
                cntt = sb(st, "cntt", [128, 1], F32)
                gw = sb(st, "gw", [128, 1], F32)
                b_amax, b_lo, b_mid, b_wk, b_cnt, b_gw = [Buf() for _ in range(6)]
                pe_ = [sb(st, f"pe{s}", [128, 512], BF16) for s in range(2)]
                b_pe = [Buf(), Buf()]
                pmk = [sb(st, f"pmk{s}", [128, 512], BF16) for s in range(2)]
                b_pmk = [Buf(), Buf()]
                rden = sb(st, "rden", [128, 512], F32)
                b_rden = Buf()
                osb = sb(st, "osb", [128, 512], F32)
                b_osb = Buf()
                sgz = sb(st, "sgz", [128, 512], F32)
                b_sgz = Buf()
                og = [sb(st, f"og{s}", [128, 16, 128], BF16) for s in range(2)]
                b_og = [Buf(), Buf()]
                lps = [ps(st, f"lps{s}", [128, 512], F32) for s in range(2)]
                b_lps = [PB(), PB()]
                spsC = [ps(st, f"spsC{s}", [128, 512], F32) for s in range(2)]
                b_sps = [PB(), PB()]
                ops = ps(st, "opsC", [128, 512], F32)
                dps = ps(st, "dpsC", [128, 512], F32)
                b_ops, b_dps = PB(), PB()
                SC = 128.0 ** -0.5
                nl = 0
                nsp = 0
                for i in range(NT):
                    s = i % 2
                    nk = (i + 1) * 128
                    c.dma('sp', iqr[s][:], IQR[:, :, i * 128:(i + 1) * 128].rearrange("h d t -> d h t"), reads=[dbuf['IQR']], writes=[b_iqr[s]])
                    c.dma('sp', qg[s][:], QR[:, :, i * 128:(i + 1) * 128].rearrange("h d t -> d h t"), reads=[dbuf['QR']], writes=[b_qg[s]])
                    c.dma('sp', azt[s][:], AZT[:, :, i * 128:(i + 1) * 128].rearrange("h d t -> d h t"), reads=[dbuf['AZT']], writes=[b_azt[s]])
                    c.dma('sp', ikw[s][:], IKW[i * 128:(i + 1) * 128, :], reads=[dbuf['IKW']], writes=[b_ikw[s]])
                    c.op('act', lambda e: e.activation(out=absw[:], in_=ikw[s][:, 64:80], func=AF.Abs, scale=1.0 / 32),
                         reads=[b_ikw[s]], writes=[b_w])
                    c.op('act', lambda e: e.activation(out=sgnw[:], in_=ikw[s][:, 64:80], func=AF.Sign), reads=[b_ikw[s]], writes=[b_w])
                    for k0 in range(0, nk, 512):
                        kn_ = min(512, nk - k0)
                        for h in range(16):
                            li = nl % 2
                            nl += 1
                            c.op('pe', lambda e: e.matmul(lps[li][:, :kn_], lhsT=iqr[s][:, h, :], rhs=IKT[:, k0:k0 + kn_], start=True, stop=True),
                                 reads=[b_iqr[s], b_IKT], writes=[b_lps[li]])
                            c.op('act', lambda e: e.activation(out=Rr[li][:, :kn_], in_=lps[li][:, :kn_], func=AF.Relu, scale=absw[:, h:h + 1]),
                                 reads=[b_lps[li], b_w], writes=[b_Rr[li]])
                            if h == 0:
                                c.op('dve', lambda e: e.tensor_scalar(out=score[:, k0:k0 + kn_], in0=Rr[li][:, :kn_], scalar1=sgnw[:, 0:1], scalar2=None, op0=ALU.mult),
                                     reads=[b_Rr[li], b_w], writes=[b_score])
                            else:
                                c.op('dve', lambda e: e.scalar_tensor_tensor(out=score[:, k0:k0 + kn_], in0=Rr[li][:, :kn_], scalar=sgnw[:, h:h + 1],
                                                                             in1=score[:, k0:k0 + kn_], op0=ALU.mult, op1=ALU.add),
                                     reads=[b_Rr[li], b_w, b_score], writes=[b_score])
                    mi = 1 if i == 0 else 0
                    dsl = slice(i * 128, (i + 1) * 128)
                    c.op('dve', lambda e: e.tensor_tensor(out=score[:, dsl], in0=score[:, dsl], in1=triq[:, mi, :], op=ALU.mult), reads=[b_score, b_t], writes=[b_score])
                    if i > 0:
                        c.op('dve', lambda e: e.memset(score[:, 0:112], 0.0), writes=[b_score])
                    c.op('dve', lambda e: e.tensor_reduce(out=amax[:], in_=score[:, :nk], axis=AX.X, op=ALU.max, apply_absolute_value=True),
                         reads=[b_score], writes=[b_amax])
                    c.op('dve', lambda e: e.tensor_tensor(out=score[:, dsl], in0=score[:, dsl], in1=negq[:, mi, :], op=ALU.add), reads=[b_score, b_t], writes=[b_score])
                    if i > 0:
                        c.op('dve', lambda e: e.memset(score[:, 0:112], NEG), writes=[b_score])
                    c.op('dve', lambda e: e.tensor_scalar(out=lo[:], in0=amax[:], scalar1=-1.0, scalar2=-1.0, op0=ALU.mult, op1=ALU.add), reads=[b_amax], writes=[b_lo])
                    c.op('dve', lambda e: e.tensor_scalar(out=gw[:], in0=amax[:], scalar1=2.0, scalar2=2.0, op0=ALU.mult, op1=ALU.add), reads=[b_amax], writes=[b_gw])
                    c.op('dve', lambda e: e.tensor_scalar(out=wk[:], in0=pow2[:], scalar1=gw[:, 0:1], scalar2=None, op0=ALU.mult), reads=[b_gw, b_t], writes=[b_wk])
                    c.op('dve', lambda e: e.tensor_tensor(out=mid[:], in0=lo[:], in1=wk[:, 1:2], op=ALU.add), reads=[b_lo, b_wk], writes=[b_mid])
                    for it in range(1, NIT + 1):
                        c.op('dve', lambda e: e.tensor_scalar(out=junk[:, :nk], in0=score[:, :nk], scalar1=mid[:, 0:1], scalar2=None, op0=ALU.is_ge, op1=ALU.add,
                                                              accum_out=cntt[:]),
                             reads=[b_score, b_mid], writes=[b_junk, b_cnt])
                        c.op('dve', lambda e: e.tensor_scalar(out=gw[:], in0=cntt[:], scalar1=float(KSEL) - 0.5, scalar2=wk[:, it:it + 1], op0=ALU.is_ge, op1=ALU.mult),
                             reads=[b_cnt, b_wk], writes=[b_gw])
                        c.op('dve', lambda e: e.tensor_tensor(out=lo[:], in0=lo[:], in1=gw[:], op=ALU.add), reads=[b_lo, b_gw], writes=[b_lo])
                        c.op('dve', lambda e: e.tensor_tensor(out=mid[:], in0=lo[:], in1=wk[:, it + 1:it + 2], op=ALU.add), reads=[b_lo, b_wk], writes=[b_mid])
                    c.op('dve', lambda e: e.tensor_scalar(out=mask[:, :nk], in0=score[:, :nk], scalar1=lo[:, 0:1], scalar2=None, op0=ALU.is_ge),
                         reads=[b_score, b_lo], writes=[b_mask])
                    for j0 in range(0, i + 1, 8):
                        jn = min(8, i + 1 - j0)
                        for jj in range(jn):
                            c.op('pe', lambda e: e.transpose(out=tps[:, jj, :], in_=mask[:, (j0 + jj) * 128:(j0 + jj + 1) * 128], identity=identb[:]),
                                 reads=[b_mask, bconst], writes=[b_tps])
                        c.op('act', lambda e: e.copy(out=maskT[:, j0:j0 + jn, :], in_=tps[:, 0:jn, :]), reads=[b_tps], writes=[b_maskT])
                    for kg in range(4):
                        for j in range(i + 1):
                            si = nsp % 2
                            nsp += 1
                            c.op('pe', lambda e: e.matmul(spsC[si][:], lhsT=KRs[:, kg, j * 128:(j + 1) * 128], rhs=qg[s][:, 4 * kg:4 * kg + 4, :], start=True, stop=True),
                                 reads=[b_KR, b_qg[s]], writes=[b_sps[si]])
                            c.op('act', lambda e: e.activation(out=pe_[si][:], in_=spsC[si][:], func=AF.Exp, scale=SC), reads=[b_sps[si]], writes=[b_pe[si]])
                            c.op('pool', lambda e: e.tensor_tensor(out=pmk[si][:].rearrange("p (h t) -> p h t", h=4), in0=pe_[si][:].rearrange("p (h t) -> p h t", h=4),
                                                                   in1=maskT[:, j, :].unsqueeze(1).to_broadcast([128, 4, 128]), op=ALU.mult),
                                 reads=[b_pe[si], b_maskT], writes=[b_pmk[si]])
                            c.op('pe', lambda e: e.matmul(ops[:], lhsT=Vs[:, j, kg * 128:(kg + 1) * 128], rhs=pmk[si][:], start=(j == 0), stop=(j == i)),
                                 reads=[b_V, b_pmk[si]], writes=[b_ops])
                            c.op('pe', lambda e: e.matmul(dps[:], lhsT=onesb[:], rhs=pmk[si][:], start=(j == 0), stop=(j == i)),
                                 reads=[b_pmk[si], bconst], writes=[b_dps])
                        c.op('dve', lambda e: e.tensor_scalar(out=rden[:], in0=dps[:], scalar1=1e-30, scalar2=None, op0=ALU.max), reads=[b_dps], writes=[b_rden])
                        c.op('dve', lambda e: e.reciprocal(out=rden[:], in_=rden[:]), reads=[b_rden], writes=[b_rden])
                        c.op('dve', lambda e: e.tensor_tensor(out=osb[:], in0=ops[:], in1=rden[:], op=ALU.mult), reads=[b_ops, b_rden], writes=[b_osb])
                        c.op('act', lambda e: e.activation(out=sgz[:].rearrange("p (h t) -> p h t", h=4), in_=azt[s][:, 4 * kg:4 * kg + 4, :], func=AF.Silu),
                             reads=[b_azt[s]], writes=[b_sgz])
                        c.op('pool', lambda e: e.tensor_tensor(out=og[s][:, 4 * kg:4 * kg + 4, :], in0=osb[:].rearrange("p (h t) -> p h t", h=4),
                                                               in1=sgz[:].rearrange("p (h t) -> p h t", h=4), op=ALU.mult),
                             reads=[b_osb, b_sgz], writes=[b_og[s]])
                    c.dma('sp', OT[:, :, i * 128:(i + 1) * 128].rearrange("h d t -> d h t"), og[s][:], reads=[b_og[s]], writes=[dbuf['OT']])
            c.barrier()

        def phase_D(l, hin, hin_n, hout, hout_n):
            last = (l == L - 1)
            with ExitStack() as st:
                wso_l = w_so[l].rearrange("(k p) n -> p k n", p=128)
                wao_l = w_ao[l].rearrange("(k p) n -> p k n", p=128)
                wo_l = w_o[l].rearrange("(k p) n -> p k n", p=128)
                Wo = sb(st, "Wo", [128, 16, D], BF16)
                b_Wo = Buf()
                for q4 in range(4):
                    c.dma('pool', Wo[:, :, q4 * 512:(q4 + 1) * 512], wo_l[:, :, q4 * 512:(q4 + 1) * 512], writes=[b_Wo])
                Wso = [sb(st, f"Wso{s}", [128, 32, 256], BF16) for s in range(2)]
                Wao = [sb(st, f"Wao{s}", [128, 16, 256], BF16) for s in range(2)]
                b_Wso = [Buf(), Buf()]
                b_Wao = [Buf(), Buf()]
                gtb = [sb(st, f"gtb{s}", [128, 32, 256], BF16) for s in range(2)]
                otb_ = [sb(st, f"otbD{s}", [128, 16, 256], BF16) for s in range(2)]
                b_gtb = [Buf(), Buf()]
                b_otb = [Buf(), Buf()]
                gsb = [sb(st, f"gsb{s}", [128, 2, 256], BF16) for s in range(2)]
                gab = [sb(st, f"gab{s}", [128, 2, 256], BF16) for s in range(2)]
                b_gsb = [Buf(), Buf()]
                b_gab = [Buf(), Buf()]
                sgs = sb(st, "sgs", [128, 256], F32)
                sga = sb(st, "sga", [128, 256], F32)
                m1 = sb(st, "m1", [128, 256], F32)
                m2 = sb(st, "m2", [128, 256], F32)
                b_sgs, b_sga, b_m1, b_m2 = Buf(), Buf(), Buf(), Buf()
                mg = [sb(st, f"mg{s}", [128, 2, 256], BF16) for s in range(2)]
                b_mg = [Buf(), Buf()]
                ysp = [ps(st, f"yspD{s}", [128, 512], F32) for s in range(2)]
                yap = [ps(st, f"yapD{s}", [128, 512], F32) for s in range(2)]
                b_ysp = [PB(), PB()]
                b_yap = [PB(), PB()]
                nblk = 0
                npp = 0
                for dc in range(8):
                    ws = dc % 2
                    c.dma('pool', Wso[ws][:], wso_l[:, :, dc * 256:(dc + 1) * 256], writes=[b_Wso[ws]])
                    c.dma('pool', Wao[ws][:], wao_l[:, :, dc * 256:(dc + 1) * 256], writes=[b_Wao[ws]])
                    for t0 in range(0, TP, 256):
                        s = nblk % 2
                        tn = min(256, TP - t0)
                        nblk += 1
                        c.dma('sp', gtb[s][:, :, :tn], GT[:, :, t0:t0 + tn].rearrange("a p c -> p a c"), reads=[dbuf['GT']], writes=[b_gtb[s]])
                        c.dma('sp', otb_[s][:, :, :tn], OT[:, :, t0:t0 + tn].rearrange("a p c -> p a c"), reads=[dbuf['OT']], writes=[b_otb[s]])
                        c.dma('sp', gsb[s][:, :, :tn], GST[2 * dc:2 * dc + 2, :, t0:t0 + tn].rearrange("a p c -> p a c"), reads=[dbuf['GST']], writes=[b_gsb[s]])
                        c.dma('sp', gab[s][:, :, :tn], GAT[2 * dc:2 * dc + 2, :, t0:t0 + tn].rearrange("a p c -> p a c"), reads=[dbuf['GAT']], writes=[b_gab[s]])
                        for dd in range(2):
                            pi = npp % 2
                            npp += 1
                            for kc in range(32):
                                c.op('pe', lambda e: e.matmul(ysp[pi][:, :tn], lhsT=Wso[ws][:, kc, dd * 128:(dd + 1) * 128], rhs=gtb[s][:, kc, :tn], start=(kc == 0), stop=(kc == 31)),
                                     reads=[b_Wso[ws], b_gtb[s]], writes=[b_ysp[pi]])
                            for kc in range(16):
                                c.op('pe', lambda e: e.matmul(yap[pi][:, :tn], lhsT=Wao[ws][:, kc, dd * 128:(dd + 1) * 128], rhs=otb_[s][:, kc, :tn], start=(kc == 0), stop=(kc == 15)),
                                     reads=[b_Wao[ws], b_otb[s]], writes=[b_yap[pi]])
                            c.op('act', lambda e: e.activation(out=sgs[:, :tn], in_=gsb[s][:, dd, :tn], func=AF.Sigmoid), reads=[b_gsb[s]], writes=[b_sgs])
                            c.op('act', lambda e: e.activation(out=sga[:, :tn], in_=gab[s][:, dd, :tn], func=AF.Sigmoid), reads=[b_gab[s]], writes=[b_sga])
                            c.op('dve', lambda e: e.tensor_tensor(out=m1[:, :tn], in0=ysp[pi][:, :tn], in1=sgs[:, :tn], op=ALU.mult), reads=[b_ysp[pi], b_sgs], writes=[b_m1])
                            c.op('dve', lambda e: e.tensor_tensor(out=m2[:, :tn], in0=yap[pi][:, :tn], in1=sga[:, :tn], op=ALU.mult), reads=[b_yap[pi], b_sga], writes=[b_m2])
                            c.op('pool', lambda e: e.tensor_tensor(out=mg[s][:, dd, :tn], in0=m1[:, :tn], in1=m2[:, :tn], op=ALU.add), reads=[b_m1, b_m2], writes=[b_mg[s]])
                        c.dma('sp', MT[2 * dc:2 * dc + 2, :, t0:t0 + tn].rearrange("a p c -> p a c"), mg[s][:, :, :tn], reads=[b_mg[s]], writes=[dbuf['MT']])
                mt = [sb(st, f"mtD{s}", [128, 16, 128], BF16) for s in range(2)]
                b_mt = [Buf(), Buf()]
                hold = [sb(st, f"hold{s}", [128, D], F32) for s in range(2)]
                b_hold = [Buf(), Buf()]
                hnew = hold
                b_hnew = b_hold
                for i in range(NT):
                    if last and i == 0:
                        continue
                    s = i % 2
                    c.dma('sp', mt[s][:], MT[:, :, i * 128:(i + 1) * 128].rearrange("a p c -> p a c"), reads=[dbuf['MT']], writes=[b_mt[s]])
                    c.dma('sp', hold[s][:], hin[i * 128:(i + 1) * 128, :], reads=[dbuf[hin_n]], writes=[b_hold[s]])
                    for q4 in range(4):
                        pi = npp % 2
                        npp += 1
                        for kc in range(16):
                            c.op('pe', lambda e: e.matmul(ysp[pi][:], lhsT=mt[s][:, kc, :], rhs=Wo[:, kc, q4 * 512:(q4 + 1) * 512], start=(kc == 0), stop=(kc == 15)),
                                 reads=[b_mt[s], b_Wo], writes=[b_ysp[pi]])
                        c.op('dve', lambda e: e.tensor_tensor(out=hnew[s][:, q4 * 512:(q4 + 1) * 512], in0=ysp[pi][:], in1=hold[s][:, q4 * 512:(q4 + 1) * 512], op=ALU.add),
                             reads=[b_ysp[pi], b_hold[s]], writes=[b_hnew[s]])
                    if last:
                        c.dma('sp', hout[(i - 1) * 128:i * 128, :], hnew[s][:], reads=[b_hnew[s]], writes=[dbuf[hout_n]])
                    elif i == 0:
                        c.dma('sp', hout[112:128, :], hnew[s][112:128, :], reads=[b_hnew[s]], writes=[dbuf[hout_n]])
                    else:
                        c.dma('sp', hout[i * 128:(i + 1) * 128, :], hnew[s][:], reads=[b_hnew[s]], writes=[dbuf[hout_n]])
            c.barrier()

        if L > 1:
            zrow = sb(es, "zrow", [112, D], F32)
            bzr = Buf()
            c.op('pool', lambda e: e.memset(zrow[:], 0.0), writes=[bzr])
            c.dma('sp', HA[0:112, :], zrow[:], reads=[bzr], writes=[dbuf['HA']])
            if L > 2:
                c.dma('sp', HB[0:112, :], zrow[:], reads=[bzr], writes=[dbuf['HB']])

        for l in range(L):
            hin, hin_n = hseq[l]
            hout, hout_n = hseq[l + 1]
            if 'A' in PH:
                phase_A(l, hin, hin_n)
            if 'B' in PH:
                phase_B(l)
            if 'C' in PH:
                phase_C0(l)
                phase_C(l)
            if 'D' in PH:
                phase_D(l, hin, hin_n, hout, hout_n)
        c.barrier()
        print("ops", c.nops, "waits", c.nwaits, "sems", c.nsem)
    return nc


def rope_np(pos, rot):
    inv = (500000.0 ** (-np.arange(0, rot, 2, dtype=np.float32) / rot)).astype(np.float32)
    ang = pos.astype(np.float32)[:, None] * inv[None, :]
    return np.cos(ang).astype(np.float32), np.sin(ang).astype(np.float32)


def make_consts(NT):
    TP = NT * 128
    bf = ml_dtypes.bfloat16
    idx = np.arange(128)
    cst = {}
    cst['c_identb'] = np.eye(128, dtype=np.float32).astype(bf)
    cst['c_identf'] = np.eye(128, dtype=np.float32)
    cst['c_trile'] = (idx[:, None] <= idx[None, :]).astype(np.float32)
    cst['c_strict'] = (idx[:, None] > idx[None, :]).astype(np.float32)
    triq = (idx[None, :] <= idx[:, None]).astype(np.float32)
    triq0 = triq * (idx[None, :] >= 112).astype(np.float32)
    cst['c_triq'] = np.stack([triq, triq0]).astype(np.float32)
    cst['c_negq'] = (np.float32(NEG) * (1.0 - cst['c_triq'])).astype(np.float32)
    cst['c_pow2'] = np.tile((2.0 ** -np.arange(NIT + 2, dtype=np.float64)).astype(np.float32)[None, :], (128, 1))
    pos = np.maximum(np.arange(TP) - 112, 0)
    ca, sa = rope_np(pos, 32)
    ci, si = rope_np(pos, 16)
    cosA = np.ones((128, TP), np.float32)
    sinA = np.zeros((128, TP), np.float32)
    cosA[0:16] = ca.T
    cosA[16:32] = ca.T
    sinA[0:16] = sa.T
    sinA[16:32] = sa.T
    cosI = np.ones((128, TP), np.float32)
    sinI = np.zeros((128, TP), np.float32)
    for hh in range(2):
        cosI[hh * 64:hh * 64 + 8] = ci.T
        cosI[hh * 64 + 8:hh * 64 + 16] = ci.T
        sinI[hh * 64:hh * 64 + 8] = si.T
        sinI[hh * 64 + 8:hh * 64 + 16] = si.T
    cst['c_cosA'], cst['c_sinA'], cst['c_cosI'], cst['c_sinI'] = cosA, sinA, cosI, sinI
    rA = np.zeros((128, 128), np.float32)
    for i in range(16):
        rA[i + 16, i] = -1.0
        rA[i, i + 16] = 1.0
    rI = np.zeros((128, 128), np.float32)
    for hh in range(2):
        for i in range(8):
            rI[hh * 64 + i + 8, hh * 64 + i] = -1.0
            rI[hh * 64 + i, hh * 64 + i + 8] = 1.0
    cst['c_rA'] = rA.astype(bf)
    cst['c_rI'] = rI.astype(bf)
    cst['c_cosK'] = np.ascontiguousarray(ci.reshape(NT, 128, 8).transpose(1, 0, 2))
    cst['c_sinK'] = np.ascontiguousarray(si.reshape(NT, 128, 8).transpose(1, 0, 2))
    return cst


def make_inputs(NT, L, x, meta_tokens, norm_w, w_in, conv_w, conv_b, dt_bias, a_log, d_skip, ssm_norm_w,
                w_ssm_out, q_norm_w, k_norm_w, idx_k_norm_w, w_attn_out, w_out):
    f = lambda a: np.ascontiguousarray(np.asarray(a, dtype=np.float32))
    B = x.shape[0]
    TP = NT * 128
    cst = make_consts(NT)
    shared = dict(cst)
    shared.update(w_in=f(w_in[:L]), w_ssm_out=f(w_ssm_out[:L]), w_attn_out=f(w_attn_out[:L]), w_out=f(w_out[:L]),
                  norm_w=f(norm_w[:L]), conv_b=f(conv_b[:L]), dt_bias=f(dt_bias[:L]), a_log=f(a_log[:L]), d_skip=f(d_skip[:L]),
                  ssm_norm_w=f(ssm_norm_w[:L]), q_norm_w=f(q_norm_w[:L]), k_norm_w=f(k_norm_w[:L]), idx_k_norm_w=f(idx_k_norm_w[:L]))
    cw = f(conv_w[:L])
    shared['convw_p'] = np.ascontiguousarray(cw.reshape(L, 4, 48, 128).transpose(0, 3, 2, 1).reshape(L, 128, 192))
    shared['convb_p'] = np.ascontiguousarray(f(conv_b[:L]).reshape(L, 48, 128).transpose(0, 2, 1))
    maps = []
    for b in range(B):
        h0 = np.zeros((TP, D), np.float32)
        h0[112:128] = f(meta_tokens)
        h0[128:] = f(x[b])
        m = dict(shared)
        m['h0'] = h0
        maps.append(m)
    return maps


_CACHE = {}


def kernel(x, meta_tokens, norm_w, w_in, conv_w, conv_b, dt_bias, a_log, d_skip, ssm_norm_w,
           w_ssm_out, q_norm_w, k_norm_w, idx_k_norm_w, w_attn_out, w_out):
    x = np.asarray(x)
    B, S, _ = x.shape
    NT = S // 128 + 1
    L = np.asarray(w_in).shape[0]
    cfg = dict(NT=NT, DEPTH=L, KSEL=min(256, S // 4))
    nc = build(cfg)
    maps = make_inputs(NT, L, x, meta_tokens, norm_w, w_in, conv_w, conv_b, dt_bias, a_log, d_skip, ssm_norm_w,
                       w_ssm_out, q_norm_w, k_norm_w, idx_k_norm_w, w_attn_out, w_out)
    res = run_bass_kernel_spmd(nc, maps, core_ids=list(range(B)))
    return np.stack([np.asarray(r["out"], dtype=np.float32) for r in res.results], axis=0)
```

```python
import numpy as np
import ml_dtypes
from contextlib import ExitStack
import concourse.bass as bass
import concourse.mybir as mybir
from concourse.bass_utils import run_bass_kernel_spmd

F32 = mybir.dt.float32
BF16 = mybir.dt.bfloat16
AF = mybir.ActivationFunctionType
ALU = mybir.AluOpType
AX = mybir.AxisListType

D = 2048
NIN = 20624
DI = 4096
NH = 64
EPS = 1e-6
NIT = 26
NEG = -1.0e30
NDS = 12
SEM_EPOCH = 20000


class Buf:
    __slots__ = ('w', 'r', 'excl')

    def __init__(self, excl=False):
        self.w = None
        self.r = {}
        self.excl = excl


def PB():
    return Buf(True)


class Ctx:
    def __init__(self, nc, es):
        self.nc = nc
        self.es = es
        self.eng = {'pe': nc.tensor, 'act': nc.scalar, 'dve': nc.vector, 'pool': nc.gpsimd, 'sp': nc.sync}
        self.sem = {}
        self.cnt = {}
        self.nsem = 0
        self.waited = {e: {} for e in self.eng}
        for e in self.eng:
            self._newsem(e)
        self.dsem = {}
        self.dnext = {}
        for q in ('sp', 'act', 'pool'):
            self.dsem[q] = [[es.enter_context(nc.semaphore(f'dq_{q}_{i}')), 0] for i in range(NDS)]
            self.dnext[q] = 0
        self.nops = 0
        self.nwaits = 0

    def _newsem(self, e):
        self.nsem += 1
        self.sem[e] = self.es.enter_context(self.nc.semaphore(f's_{e}_{self.nsem}'))
        self.cnt[e] = 0

    def _wait(self, e, tok):
        sem, val, key, src = tok
        if self.waited[e].get(key, 0) >= val:
            return
        self.eng[e].wait_ge(sem, val)
        self.nwaits += 1
        self.waited[e][key] = val

    def _dep1(self, e, tok):
        if tok[3] == e and e == 'pe':
            return
        self._wait(e, tok)

    def _deps(self, e, reads, writes):
        for b in reads:
            if b.w is not None:
                self._dep1(e, b.w)
        for b in writes:
            if b.w is not None:
                self._dep1(e, b.w)
            for t in b.r.values():
                self._dep1(e, t)

    def _mark(self, tok, reads, writes):
        for b in writes:
            b.w = tok
            b.r = {}
        for b in reads:
            if tok[3] == 'dma':
                b.r[tok[2]] = tok
            else:
                b.r[tok[3]] = tok

    def op(self, e, fn, reads=(), writes=()):
        if any(b.excl for b in reads):
            writes = list(writes) + [b for b in reads if b.excl]
            reads = [b for b in reads if not b.excl]
        self._deps(e, reads, writes)
        inst = fn(self.eng[e])
        if self.cnt[e] >= SEM_EPOCH:
            self._newsem(e)
        self.cnt[e] += 1
        inst.then_inc(self.sem[e], 1)
        tok = (self.sem[e], self.cnt[e], id(self.sem[e]), e)
        self._mark(tok, reads, writes)
        self.nops += 1
        return tok

    def dma(self, q, out, in_, reads=(), writes=(), **kw):
        slot = self.dsem[q][self.dnext[q]]
        self.dnext[q] = (self.dnext[q] + 1) % NDS
        sem, cnt = slot
        if cnt > 0:
            self._wait(q, (sem, cnt, id(sem), 'dma'))
        self._deps(q, reads, writes)
        inst = self.eng[q].dma_start(out=out, in_=in_, **kw)
        inst.then_inc(sem, 16)
        slot[1] = cnt + 16
        tok = (sem, cnt + 16, id(sem), 'dma')
        self._mark(tok, reads, writes)
        self.nops += 1
        return tok

    def barrier(self):
        toks = [(self.sem[e], self.cnt[e], id(self.sem[e]), e) for e in self.eng if self.cnt[e] > 0]
        for q in self.dsem:
            for (s, cn) in self.dsem[q]:
                if cn > 0:
                    toks.append((s, cn, id(s), 'dma'))
        for e in self.eng:
            for t in toks:
                if t[3] != e:
                    self._wait(e, t)


def build(cfg):
    NT = cfg['NT']
    L = cfg['DEPTH']
    KSEL = cfg['KSEL']
    DBG = cfg.get('debug', False)
    PH = cfg.get('phases', 'ABCD')
    TP = NT * 128
    nc = bass.Bass("TRN2", target_bir_lowering=False)

    def din(name, shape, dt=F32):
        return nc.dram_tensor(name, list(shape), dt, kind="ExternalInput").ap()

    def dscr(name, shape, dt):
        return nc.dram_tensor(name, list(shape), dt, kind=("ExternalOutput" if DBG else "Internal")).ap()

    h0 = din("h0", [TP, D])
    w_in = din("w_in", [L, D, NIN])
    w_so = din("w_ssm_out", [L, DI, D])
    w_ao = din("w_attn_out", [L, D, D])
    w_o = din("w_out", [L, D, D])
    norm_w = din("norm_w", [L, D])
    convw_p = din("convw_p", [L, 128, 192])
    convb_p = din("convb_p", [L, 128, 48])
    conv_b = din("conv_b", [L, 6144])
    dt_bias = din("dt_bias", [L, NH])
    a_log = din("a_log", [L, NH])
    d_skip = din("d_skip", [L, NH])
    ssm_nw = din("ssm_norm_w", [L, DI])
    qnw = din("q_norm_w", [L, 128])
    knw = din("k_norm_w", [L, 128])
    iknw = din("idx_k_norm_w", [L, 64])
    c_identb = din("c_identb", [128, 128], BF16)
    c_identf = din("c_identf", [128, 128])
    c_trile = din("c_trile", [128, 128])
    c_strict = din("c_strict", [128, 128])
    c_triq = din("c_triq", [2, 128, 128])
    c_negq = din("c_negq", [2, 128, 128])
    c_pow2 = din("c_pow2", [128, NIT + 2])
    c_cosA = din("c_cosA", [128, TP])
    c_sinA = din("c_sinA", [128, TP])
    c_cosI = din("c_cosI", [128, TP])
    c_sinI = din("c_sinI", [128, TP])
    c_rA = din("c_rA", [128, 128], BF16)
    c_rI = din("c_rI", [128, 128], BF16)
    c_cosK = din("c_cosK", [128, NT, 8])
    c_sinK = din("c_sinK", [128, NT, 8])
    out = nc.dram_tensor("out", [(NT - 1) * 128, D], F32, kind="ExternalOutput").ap()

    HA = dscr("HA", [TP, D], F32)
    HB = dscr("HB", [TP, D], F32)
    Z = dscr("Z", [TP, DI], BF16)
    XBCT = dscr("XBCT", [48, 128, TP + 3], BF16)
    DT = dscr("DT", [TP, 64], F32)
    QT = dscr("QT", [16, 128, TP], BF16)
    KT = dscr("KT", [4, 128, TP], BF16)
    V = dscr("V", [TP, 512], BF16)
    AZT = dscr("AZT", [16, 128, TP], BF16)
    IQT = dscr("IQT", [8, 128, TP], BF16)
    IKW = dscr("IKW", [TP, 80], F32)
    GST = dscr("GST", [16, 128, TP], BF16)
    GAT = dscr("GAT", [16, 128, TP], BF16)
    GT = dscr("GT", [32, 128, TP], BF16)
    QR = dscr("QR", [16, 128, TP], BF16)
    KR = dscr("KR", [4, 128, TP], BF16)
    IQR = dscr("IQR", [16, 64, TP], BF16)
    OT = dscr("OT", [16, 128, TP], BF16)
    MT = dscr("MT", [16, 128, TP], BF16)

    dbuf = {n: Buf() for n in ['HA', 'HB', 'Z', 'XBCT', 'DT', 'QT', 'KT', 'V', 'AZT', 'IQT', 'IKW', 'GST', 'GAT',
                               'GT', 'QR', 'KR', 'IQR', 'OT', 'MT', 'out', 'h0']}

    es = ExitStack()
    with es:
        c = Ctx(nc, es)

        uid = [0]

        def sb(st, name, shape, dt):
            uid[0] += 1
            return st.enter_context(nc.sbuf_tensor(f"{name}_{uid[0]}", list(shape), dt))

        def ps(st, name, shape, dt):
            uid[0] += 1
            return st.enter_context(nc.psum_tensor(f"{name}_{uid[0]}", list(shape), dt))

        identb = sb(es, "identb", [128, 128], BF16)
        identf = sb(es, "identf", [128, 128], F32)
        onesb = sb(es, "onesb", [128, 128], BF16)
        onesf = sb(es, "onesf", [128, 128], F32)
        trile = sb(es, "trile", [128, 128], F32)
        strict = sb(es, "strict", [128, 128], F32)
        bconst = Buf()
        c.dma('sp', identb[:], c_identb, writes=[bconst])
        c.dma('sp', identf[:], c_identf, writes=[bconst])
        c.dma('sp', trile[:], c_trile, writes=[bconst])
        c.dma('sp', strict[:], c_strict, writes=[bconst])
        c.op('dve', lambda e: e.memset(onesb[:], 1.0), writes=[bconst])
        c.op('dve', lambda e: e.memset(onesf[:], 1.0), writes=[bconst])

        hseq = [(h0, 'h0')]
        for l in range(L):
            if l == L - 1:
                hseq.append((out, 'out'))
            else:
                hseq.append((HA, 'HA') if l % 2 == 0 else (HB, 'HB'))

        def phase_A(l, hin, hin_n):
            NBT = min(11, NT)
            with ExitStack() as st:
                hnT = sb(st, "hnT", [128, 16, NBT * 128], BF16)
                b_hnT = Buf()
                normw = sb(st, "normw", [128, D], F32)
                b_nw = Buf()
                c.dma('sp', normw[:], norm_w[l].partition_broadcast(128), writes=[b_nw])
                Ht = [sb(st, f"Ht{s}", [128, D], F32) for s in range(2)]
                b_Ht = [Buf(), Buf()]
                hnb = [sb(st, f"hnb{s}", [128, D], BF16) for s in range(2)]
                b_hnb = [Buf(), Buf()]
                junk = sb(st, "junkA", [128, D], BF16)
                b_junk = Buf()
                ssq = [sb(st, f"ssq{s}", [128, 1], F32) for s in range(2)]
                b_ss = [Buf(), Buf()]
                rs = [sb(st, f"rs{s}", [128, 1], F32) for s in range(2)]
                b_rs = [Buf(), Buf()]
                Wb = [sb(st, f"Wb{s}", [128, 16, 512], BF16) for s in range(3)]
                b_Wb = [Buf() for _ in range(3)]
                otb = [sb(st, f"otb{s}", [128, 512], BF16) for s in range(3)]
                b_otb = [Buf() for _ in range(3)]
                otf = [sb(st, f"otf{s}", [128, 128], F32) for s in range(2)]
                b_otf = [Buf() for _ in range(2)]
                ofm = [sb(st, f"ofm{s}", [128, NBT * 128], BF16) for s in range(3)]
                b_ofm = [Buf() for _ in range(3)]
                pt = [ps(st, f"ptA{s}", [128, 8, 128], BF16) for s in range(2)]
                b_pt = [PB(), PB()]
                pm = [ps(st, f"pmA{s}", [128, 512], F32) for s in range(4)]
                b_pm = [PB() for _ in range(4)]
                cnt = {'w': 0, 'pm': 0, 'otb': 0, 'otf': 0, 'ofm': 0, 'ev': 0}
                w_l = w_in[l].rearrange("(k p) n -> p k n", p=128)

                chunks = []
                for c0 in range(0, 4096, 512):
                    chunks.append(('tok', c0, 512, 'Z', c0))
                for c0 in range(4096, 10240, 512):
                    chunks.append(('fm', c0, 512, 'XBCT', (c0 - 4096) // 128))
                chunks.append(('tok', 10240, 64, 'DT', 0))
                for c0 in range(10304, 12352, 512):
                    chunks.append(('fm', c0, 512, 'QT', (c0 - 10304) // 128))
                chunks.append(('fm', 12352, 512, 'KT', 0))
                chunks.append(('tok', 12864, 512, 'V', 0))
                for c0 in range(13376, 15424, 512):
                    chunks.append(('fm', c0, 512, 'AZT', (c0 - 13376) // 128))
                for c0 in range(15424, 16448, 512):
                    chunks.append(('fm', c0, 512, 'IQT', (c0 - 15424) // 128))
                chunks.append(('tok', 16448, 80, 'IKW', 0))
                for c0 in range(16528, 18576, 512):
                    chunks.append(('fm', c0, 512, 'GST', (c0 - 16528) // 128))
                for c0 in range(18576, 20624, 512):
                    chunks.append(('fm', c0, 512, 'GAT', (c0 - 18576) // 128))
                dmap = {'Z': Z, 'XBCT': XBCT, 'DT': DT, 'QT': QT, 'KT': KT, 'V': V, 'AZT': AZT, 'IQT': IQT,
                        'IKW': IKW, 'GST': GST, 'GAT': GAT}

                def evac(dst_ap, src_ap, reads, writes):
                    cnt['ev'] += 1
                    if cnt['ev'] % 2 == 0:
                        c.op('act', lambda e: e.copy(out=dst_ap, in_=src_ap), reads=reads, writes=writes)
                    else:
                        c.op('dve', lambda e: e.tensor_copy(out=dst_ap, in_=src_ap), reads=reads, writes=writes)

                for t0 in range(0, NT, NBT):
                    nb = min(NBT, NT - t0)
                    for ti in range(nb):
                        i = t0 + ti
                        s = ti % 2
                        c.dma('sp', Ht[s][:], hin[i * 128:(i + 1) * 128, :], reads=[dbuf[hin_n]], writes=[b_Ht[s]])
                        c.op('act', lambda e: e.activation(out=junk[:], in_=Ht[s][:], func=AF.Square, accum_out=ssq[s][:]),
                             reads=[b_Ht[s]], writes=[b_junk, b_ss[s]])
                        c.op('act', lambda e: e.activation(out=rs[s][:], in_=ssq[s][:], func=AF.Ln, scale=1.0 / D, bias=EPS),
                             reads=[b_ss[s]], writes=[b_rs[s]])
                        c.op('act', lambda e: e.activation(out=rs[s][:], in_=rs[s][:], func=AF.Exp, scale=-0.5),
                             reads=[b_rs[s]], writes=[b_rs[s]])
                        c.op('dve', lambda e: e.scalar_tensor_tensor(out=hnb[s][:], in0=Ht[s][:], scalar=rs[s][:, 0:1], in1=normw[:],
                                                                     op0=ALU.mult, op1=ALU.mult),
                             reads=[b_Ht[s], b_rs[s], b_nw], writes=[b_hnb[s]])
                        for hh in range(2):
                            for k in range(8):
                                kk = hh * 8 + k
                                c.op('pe', lambda e: e.transpose(out=pt[hh][:, k, :], in_=hnb[s][:, kk * 128:(kk + 1) * 128], identity=identb[:]),
                                     reads=[b_hnb[s], bconst], writes=[b_pt[hh]])
                            evac(hnT[:, hh * 8:(hh + 1) * 8, ti * 128:(ti + 1) * 128], pt[hh][:], [b_pt[hh]], [b_hnT])
                    for (kind, c0, cw, dn, dof) in chunks[:cfg.get('nchunks', 1000)]:
                        wi = cnt['w'] % 3
                        cnt['w'] += 1
                        wb = Wb[wi]
                        c.dma('pool', wb[:, :, :cw], w_l[:, :, c0:c0 + cw], writes=[b_Wb[wi]])
                        dst = dmap[dn]
                        if kind == 'tok':
                            for ti in range(nb):
                                i = t0 + ti
                                pi = cnt['pm'] % 4
                                cnt['pm'] += 1
                                for k in range(16):
                                    c.op('pe', lambda e: e.matmul(pm[pi][:, :cw], lhsT=hnT[:, k, ti * 128:(ti + 1) * 128], rhs=wb[:, k, :cw],
                                                                  start=(k == 0), stop=(k == 15)),
                                         reads=[b_hnT, b_Wb[wi]], writes=[b_pm[pi]])
                                if dn in ('DT', 'IKW'):
                                    oi = cnt['otf'] % 2
                                    cnt['otf'] += 1
                                    evac(otf[oi][:, :cw], pm[pi][:, :cw], [b_pm[pi]], [b_otf[oi]])
                                    c.dma('sp', dst[i * 128:(i + 1) * 128, :], otf[oi][:, :cw], reads=[b_otf[oi]], writes=[dbuf[dn]])
                                else:
                                    oi = cnt['otb'] % 3
                                    cnt['otb'] += 1
                                    evac(otb[oi][:, :cw], pm[pi][:, :cw], [b_pm[pi]], [b_otb[oi]])
                                    c.dma('sp', dst[i * 128:(i + 1) * 128, dof:dof + cw], otb[oi][:, :cw], reads=[b_otb[oi]], writes=[dbuf[dn]])
                        else:
                            for j in range(cw // 128):
                                fi = cnt['ofm'] % 3
                                cnt['ofm'] += 1
                                for s0 in range(0, nb * 128, 512):
                                    sn = min(512, nb * 128 - s0)
                                    pi = cnt['pm'] % 4
                                    cnt['pm'] += 1
                                    for k in range(16):
                                        c.op('pe', lambda e: e.matmul(pm[pi][:, :sn], lhsT=wb[:, k, j * 128:(j + 1) * 128], rhs=hnT[:, k, s0:s0 + sn],
                                                                      start=(k == 0), stop=(k == 15)),
                                             reads=[b_hnT, b_Wb[wi]], writes=[b_pm[pi]])
                                    evac(ofm[fi][:, s0:s0 + sn], pm[pi][:, :sn], [b_pm[pi]], [b_ofm[fi]])
                                co = 3 if dn == 'XBCT' else 0
                                c.dma('sp', dst[dof + j][:, co + t0 * 128: co + (t0 + nb) * 128], ofm[fi][:, :nb * 128],
                                      reads=[b_ofm[fi]], writes=[dbuf[dn]])
            c.barrier()

        def phase_B(l):
            with ExitStack() as st:
                diagF = sb(st, "diagF", [128, 192, 128], BF16)
                b_diagF = Buf()
                cw_p = sb(st, "cw_p", [128, 192], F32)
                cb_p = sb(st, "cb_p", [128, 48], F32)
                cb_row = sb(st, "cb_row", [1, 6144], BF16)
                ones_row = sb(st, "ones_row", [1, 128], BF16)
                dtb_bc = sb(st, "dtb_bc", [128, NH], F32)
                a_bc = sb(st, "a_bc", [128, NH], F32)
                dsk_bc = sb(st, "dsk_bc", [128, NH], F32)
                snw_bc = sb(st, "snw_bc", [128, DI], F32)
                b_par = Buf()
                c.dma('sp', cw_p[:], convw_p[l], writes=[b_par])
                c.dma('sp', cb_p[:], convb_p[l], writes=[b_par])
                c.dma('pool', cb_row[:], conv_b[l].unsqueeze(0), writes=[b_par])
                c.dma('sp', dtb_bc[:], dt_bias[l].partition_broadcast(128), writes=[b_par])
                c.dma('sp', a_bc[:], a_log[l].partition_broadcast(128), writes=[b_par])
                c.dma('sp', dsk_bc[:], d_skip[l].partition_broadcast(128), writes=[b_par])
                c.dma('sp', snw_bc[:], ssm_nw[l].partition_broadcast(128), writes=[b_par])
                c.op('dve', lambda e: e.memset(ones_row[:], 1.0), writes=[b_par])
                c.op('act', lambda e: e.activation(out=a_bc[:], in_=a_bc[:], func=AF.Exp), reads=[b_par], writes=[b_par])
                c.op('dve', lambda e: e.tensor_scalar(out=a_bc[:], in0=a_bc[:], scalar1=-1.0, scalar2=None, op0=ALU.mult),
                     reads=[b_par], writes=[b_par])

                for a in range(192):
                    c.op('dve', lambda e: e.tensor_scalar(out=diagF[:, a, :], in0=identb[:], scalar1=cw_p[:, a:a + 1], scalar2=None, op0=ALU.mult),
                         reads=[b_par, bconst], writes=[b_diagF])
                Hin = sb(st, "Hin", [128, 8, 512], F32)
                Hinb = sb(st, "Hinb", [128, 8, 512], BF16)
                b_Hin = [Buf() for _ in range(8)]
                b_Hinb = [Buf() for _ in range(8)]
                c.op('dve', lambda e: e.memset(Hin[:], 0.0), writes=b_Hin)
                c.op('pool', lambda e: e.memset(Hinb[:], 0.0), writes=b_Hinb)

                def T2(name, shape, dt, n=2):
                    return [sb(st, f"{name}{s}", shape, dt) for s in range(n)], [Buf() for _ in range(n)]

                u, b_u = T2("u", [128, 6, 515], BF16, 2)
                zt_, b_zt = T2("zt", [128, 512], BF16)
                dtr, b_dtr = T2("dtr", [128, NH], F32)
                dtt, b_dt = T2("dtt", [128, NH], F32, 8)
                tA, b_tA = T2("tA", [128, NH], F32)
                tB, b_tB = T2("tB", [128, NH], F32)
                adt, b_adt = T2("adt", [128, NH], F32, 8)
                acs_sb, b_acs = T2("acs_sb", [128, NH], F32)
                eacs, b_eacs = T2("eacs", [128, NH], F32, 8)
                dte, b_dte = T2("dte", [128, NH], F32, 8)
                dec, b_dec = T2("dec", [128, NH], F32, 8)
                rhsD, b_rhsD = T2("rhsD", [128, 8, 128], F32)
                Eb, b_E = T2("Eb", [128, 8, 128], BF16)
                MTb, b_MT = T2("MTb", [128, 8, 128], BF16)
                xtm, b_xtm = T2("xtm", [128, 512], BF16)
                xdt, b_xdt = T2("xdt", [128, 512], BF16)
                xw, b_xw = T2("xw", [128, 512], BF16)
                xD, b_xD = T2("xD", [128, 512], BF16)
                BTt, b_BT = T2("BTt", [128, 128], BF16)
                CTt, b_CT = T2("CTt", [128, 128], BF16)
                Btm, b_Btm = T2("Btm", [128, 128], BF16)
                cbm, b_cbm = T2("cbm", [128, 128], BF16)
                tmp, b_tmp = T2("tmpB", [128, 512], F32, 2)
                ysb, b_ysb = T2("ysb", [128, 512], F32, 2)
                sz, b_sz = T2("sz", [128, 512], F32, 2)
                gy, b_gy = T2("gy", [128, 512], F32, 2)
                junkB, b_junkB = T2("junkB", [128, 512], BF16, 1)
                gss, b_gss = T2("gss", [128, 1], F32, 1)
                grs, b_grs = T2("grs", [128, 1], F32, 1)
                gn, b_gn = T2("gn", [128, 512], BF16)
                gst, b_gst = T2("gst", [128, 4, 512], BF16, 2)
                hup, b_hup = T2("hup", [128, 512], F32, 2)

                xps = ps(st, "xps", [128, 512], F32)
                bank1 = ps(st, "bank1", [128, 512], F32)
                bcps = bank1[:, 0:384]
                btps = bank1[:, 384:512].bitcast(BF16)
                Dps = [ps(st, f"Dps{s}", [128, 512], F32) for s in range(2)]
                yps = ps(st, "yps", [128, 512], F32)
                ups = ps(st, "ups", [128, 512], F32)
                sps = ps(st, "sps", [128, 512], F32)
                bank7 = ps(st, "bank7", [128, 512], F32)
                acsps = bank7[:, 0:256]
                trps = bank7[:, 256:512].bitcast(BF16).rearrange("p (a t) -> p a t", a=4)
                b_xps, b_bcps, b_yps, b_ups, b_sps, b_acsps = [PB() for _ in range(6)]
                b_btps = b_bcps
                b_trps = b_acsps
                b_Dps = [PB(), PB()]

                n4 = (NT + 3) // 4
                for ib in range(n4 if cfg.get('lvlB', 99) > -3 else 0):
                    tiles = list(range(ib * 4, min(NT, ib * 4 + 4)))
                    ntl = len(tiles)
                    ncol = ntl * 128 + 3
                    for ti, i in enumerate(tiles):
                        sd = (ib % 2) * 4 + ti
                        s = i % 2
                        c.dma('sp', dtr[s][:], DT[i * 128:(i + 1) * 128, :], reads=[dbuf['DT']], writes=[b_dtr[s]])
                        c.op('dve', lambda e: e.tensor_tensor(out=tA[s][:], in0=dtr[s][:], in1=dtb_bc[:], op=ALU.add),
                             reads=[b_dtr[s], b_par], writes=[b_tA[s]])
                        c.op('act', lambda e: e.activation(out=tB[s][:], in_=tA[s][:], func=AF.Abs),
                             reads=[b_tA[s]], writes=[b_tB[s]])
                        c.op('act', lambda e: e.activation(out=tB[s][:], in_=tB[s][:], func=AF.Exp, scale=-1.0),
                             reads=[b_tB[s]], writes=[b_tB[s]])
                        c.op('act', lambda e: e.activation(out=tB[s][:], in_=tB[s][:], func=AF.Ln, bias=1.0),
                             reads=[b_tB[s]], writes=[b_tB[s]])
                        c.op('dve', lambda e: e.scalar_tensor_tensor(out=dtt[sd][:], in0=tA[s][:], scalar=0.0, in1=tB[s][:], op0=ALU.max, op1=ALU.add),
                             reads=[b_tA[s], b_tB[s]], writes=[b_dt[sd]])
                        if i == 0:
                            c.op('dve', lambda e: e.memset(dtt[sd][0:112, :], 0.0), writes=[b_dt[sd]])
                        c.op('dve', lambda e: e.tensor_tensor(out=adt[sd][:], in0=dtt[sd][:], in1=a_bc[:], op=ALU.mult),
                             reads=[b_dt[sd], b_par], writes=[b_adt[sd]])
                        c.op('pe', lambda e: e.matmul(acsps[:, 0:64], lhsT=trile[:], rhs=adt[sd][:], start=True, stop=True),
                             reads=[b_adt[sd], bconst], writes=[b_acsps])
                        c.op('pe', lambda e: e.matmul(acsps[:, 64:128], lhsT=onesf[:], rhs=adt[sd][:], start=True, stop=True),
                             reads=[b_adt[sd], bconst], writes=[b_acsps])
                        c.op('act', lambda e: e.copy(out=acs_sb[s][:], in_=acsps[:, 0:64]), reads=[b_acsps], writes=[b_acs[s]])
                        c.op('act', lambda e: e.activation(out=eacs[sd][:], in_=acsps[:, 0:64], func=AF.Exp), reads=[b_acsps], writes=[b_eacs[sd]])
                        c.op('act', lambda e: e.activation(out=dec[sd][:], in_=acsps[:, 64:128], func=AF.Exp), reads=[b_acsps], writes=[b_dec[sd]])
                        c.op('dve', lambda e: e.tensor_tensor(out=dte[sd][:], in0=acsps[:, 64:128], in1=acs_sb[s][:], op=ALU.subtract),
                             reads=[b_acsps, b_acs[s]], writes=[b_dte[sd]])
                        c.op('act', lambda e: e.activation(out=dte[sd][:], in_=dte[sd][:], func=AF.Exp), reads=[b_dte[sd]], writes=[b_dte[sd]])

                    def body(g, ti, i):
                        us = g % 2
                        gs_ = g % 2
                        diag = diagF
                        b_diag = b_diagF
                        s = (i * 8 + g) % 2
                        o = ti * 128
                        hs = slice(8 * g, 8 * g + 8)
                        sd = (ib % 2) * 4 + ti
                        for a in range(4):
                            ct = 4 * g + a
                            for k in range(4):
                                c.op('pe', lambda e: e.matmul(xps[:, a * 128:(a + 1) * 128], lhsT=u[us][:, a, o + k:o + k + 128], rhs=diag[:, ct * 4 + k, :],
                                                              start=(k == 0), stop=False),
                                     reads=[b_u[us], b_diag], writes=[b_xps])
                            c.op('pe', lambda e: e.matmul(xps[:, a * 128:(a + 1) * 128], lhsT=ones_row[0:1, :], rhs=cb_row[0:1, ct * 128:(ct + 1) * 128],
                                                          start=False, stop=True),
                                 reads=[b_par], writes=[b_xps])
                        c.op('act', lambda e: e.activation(out=xtm[s][:], in_=xps[:], func=AF.Silu), reads=[b_xps], writes=[b_xtm[s]])
                        for (which, a, ct) in ((0, 4, 32 + g), (1, 5, 40 + g)):
                            for k in range(4):
                                c.op('pe', lambda e: e.matmul(bcps[:, which * 128:(which + 1) * 128], lhsT=diag[:, ct * 4 + k, :], rhs=u[us][:, a, o + k:o + k + 128],
                                                              start=(k == 0), stop=(k == 3)),
                                     reads=[b_u[us], b_diag], writes=[b_bcps])
                        c.op('act', lambda e: e.activation(out=BTt[s][:], in_=bcps[:, 0:128], func=AF.Silu, bias=cb_p[:, 32 + g:33 + g]),
                             reads=[b_bcps, b_par], writes=[b_BT[s]])
                        c.op('act', lambda e: e.activation(out=CTt[s][:], in_=bcps[:, 128:256], func=AF.Silu, bias=cb_p[:, 40 + g:41 + g]),
                             reads=[b_bcps, b_par], writes=[b_CT[s]])
                        c.op('pe', lambda e: e.transpose(out=btps[:, 0:128], in_=BTt[s][:], identity=identb[:]), reads=[b_BT[s], bconst], writes=[b_btps])
                        c.op('dve', lambda e: e.tensor_copy(out=Btm[s][:], in_=btps[:, 0:128]), reads=[b_btps], writes=[b_Btm[s]])
                        c.op('pe', lambda e: e.matmul(bcps[:, 256:384], lhsT=BTt[s][:], rhs=CTt[s][:], start=True, stop=True),
                             reads=[b_BT[s], b_CT[s]], writes=[b_bcps])
                        c.op('dve', lambda e: e.tensor_tensor(out=cbm[s][:], in0=bcps[:, 256:384], in1=trile[:], op=ALU.mult),
                             reads=[b_bcps, bconst], writes=[b_cbm[s]])
                        yield
                        c.op('dve', lambda e: e.tensor_tensor(out=rhsD[s][:], in0=trile[:].unsqueeze(1).to_broadcast([128, 8, 128]),
                                                              in1=adt[sd][:, hs].unsqueeze(2).to_broadcast([128, 8, 128]), op=ALU.mult),
                             reads=[b_adt[sd], bconst], writes=[b_rhsD[s]])
                        for hh in range(2):
                            c.op('pe', lambda e: e.matmul(Dps[hh][:], lhsT=strict[:], rhs=rhsD[s][:, hh * 4:(hh + 1) * 4, :], start=True, stop=True),
                                 reads=[b_rhsD[s], bconst], writes=[b_Dps[hh]])
                            c.op('act', lambda e: e.activation(out=Eb[s][:, hh * 4:(hh + 1) * 4, :], in_=Dps[hh][:], func=AF.Exp),
                                 reads=[b_Dps[hh]], writes=[b_E[s]])
                        c.op('dve', lambda e: e.tensor_tensor(out=MTb[s][:], in0=Eb[s][:], in1=cbm[s][:].unsqueeze(1).to_broadcast([128, 8, 128]), op=ALU.mult),
                             reads=[b_E[s], b_cbm[s]], writes=[b_MT[s]])
                        x3 = xtm[s][:].rearrange("p (h q) -> p h q", h=8)
                        c.op('dve', lambda e: e.tensor_tensor(out=xdt[s][:].rearrange("p (h q) -> p h q", h=8), in0=x3,
                                                              in1=dtt[sd][:, hs].unsqueeze(2).to_broadcast([128, 8, 64]), op=ALU.mult),
                             reads=[b_xtm[s], b_dt[sd]], writes=[b_xdt[s]])
                        c.op('pool', lambda e: e.tensor_tensor(out=xD[s][:].rearrange("p (h q) -> p h q", h=8), in0=x3,
                                                               in1=dsk_bc[:, hs].unsqueeze(2).to_broadcast([128, 8, 64]), op=ALU.mult),
                             reads=[b_xtm[s], b_par], writes=[b_xD[s]])
                        c.op('pool', lambda e: e.tensor_tensor(out=xw[s][:].rearrange("p (h q) -> p h q", h=8), in0=xdt[s][:].rearrange("p (h q) -> p h q", h=8),
                                                               in1=dte[sd][:, hs].unsqueeze(2).to_broadcast([128, 8, 64]), op=ALU.mult),
                             reads=[b_xdt[s], b_dte[sd]], writes=[b_xw[s]])
                        yield
                        c.op('pe', lambda e: e.matmul(yps[:], lhsT=identb[:], rhs=xD[s][:], start=True, stop=False, skip_group_check=True),
                             reads=[b_xD[s], bconst], writes=[b_yps])
                        for r in range(8):
                            c.op('pe', lambda e: e.matmul(yps[:, r * 64:(r + 1) * 64], lhsT=MTb[s][:, r, :], rhs=xdt[s][:, r * 64:(r + 1) * 64], start=False, stop=(r == 7),
                                                          skip_group_check=True),
                                 reads=[b_MT[s], b_xdt[s]], writes=[b_yps])
                        c.op('pe', lambda e: e.matmul(ups[:], lhsT=CTt[s][:], rhs=Hinb[:, g, :], start=True, stop=True),
                             reads=[b_CT[s], b_Hinb[g]], writes=[b_ups])
                        c.op('dve', lambda e: e.tensor_tensor(out=tmp[s][:].rearrange("p (h q) -> p h q", h=8), in0=ups[:].rearrange("p (h q) -> p h q", h=8),
                                                              in1=eacs[sd][:, hs].unsqueeze(2).to_broadcast([128, 8, 64]), op=ALU.mult),
                             reads=[b_ups, b_eacs[sd]], writes=[b_tmp[s]])
                        c.op('dve', lambda e: e.tensor_tensor(out=ysb[s][:], in0=yps[:], in1=tmp[s][:], op=ALU.add),
                             reads=[b_yps, b_tmp[s]], writes=[b_ysb[s]])
                        c.op('pe', lambda e: e.matmul(sps[:], lhsT=Btm[s][:], rhs=xw[s][:], start=True, stop=True),
                             reads=[b_Btm[s], b_xw[s]], writes=[b_sps])
                        c.op('dve', lambda e: e.tensor_tensor(out=hup[s][:].rearrange("p (h q) -> p h q", h=8), in0=Hin[:, g, :].rearrange("p (h q) -> p h q", h=8),
                                                              in1=dec[sd][:, hs].unsqueeze(2).to_broadcast([128, 8, 64]), op=ALU.mult),
                             reads=[b_Hin[g], b_dec[sd]], writes=[b_hup[s]])
                        c.op('dve', lambda e: e.tensor_tensor(out=Hin[:, g, :], in0=sps[:], in1=hup[s][:], op=ALU.add),
                             reads=[b_sps, b_hup[s]], writes=[b_Hin[g]])
                        c.op('act', lambda e: e.copy(out=Hinb[:, g, :], in_=Hin[:, g, :]), reads=[b_Hin[g]], writes=[b_Hinb[g]])
                        yield
                        zs = (i * 8 + g) % 2
                        c.dma('sp', zt_[zs][:], Z[i * 128:(i + 1) * 128, 512 * g:512 * (g + 1)], reads=[dbuf['Z']], writes=[b_zt[zs]])
                        c.op('act', lambda e: e.activation(out=sz[s][:], in_=zt_[zs][:], func=AF.Silu), reads=[b_zt[zs]], writes=[b_sz[s]])
                        c.op('pool', lambda e: e.tensor_tensor(out=gy[s][:], in0=ysb[s][:], in1=sz[s][:], op=ALU.mult),
                             reads=[b_ysb[s], b_sz[s]], writes=[b_gy[s]])
                        c.op('act', lambda e: e.activation(out=junkB[0][:], in_=gy[s][:], func=AF.Square, accum_out=gss[0][:]),
                             reads=[b_gy[s]], writes=[b_junkB[0], b_gss[0]])
                        c.op('act', lambda e: e.activation(out=grs[0][:], in_=gss[0][:], func=AF.Ln, scale=1.0 / 512, bias=EPS),
                             reads=[b_gss[0]], writes=[b_grs[0]])
                        c.op('act', lambda e: e.activation(out=grs[0][:], in_=grs[0][:], func=AF.Exp, scale=-0.5),
                             reads=[b_grs[0]], writes=[b_grs[0]])
                        c.op('dve', lambda e: e.scalar_tensor_tensor(out=gn[s][:], in0=gy[s][:], scalar=grs[0][:, 0:1], in1=snw_bc[:, 512 * g:512 * (g + 1)],
                                                                     op0=ALU.mult, op1=ALU.mult),
                             reads=[b_gy[s], b_grs[0], b_par], writes=[b_gn[s]])
                        for a in range(4):
                            c.op('pe', lambda e: e.transpose(out=trps[:, a, :], in_=gn[s][:, a * 128:(a + 1) * 128], identity=identb[:]),
                                 reads=[b_gn[s], bconst], writes=[b_trps])
                        c.op('act', lambda e: e.copy(out=gst[gs_][:, :, o:o + 128], in_=trps[:]), reads=[b_trps], writes=[b_gst[gs_]])
                        yield
                    for gp in range(4):
                        pair = (2 * gp, 2 * gp + 1)
                        for g in pair:
                            us = g % 2
                            c0_ = 3 if ib == 0 else 0
                            if ib == 0:
                                c.op('pool', lambda e: e.memset(u[us][:, :, 0:3], 0.0), writes=[b_u[us]])
                            c.dma('sp', u[us][:, 0:4, c0_:ncol], XBCT[4 * g:4 * g + 4, :, ib * 512 + c0_: ib * 512 + ncol].rearrange("a p c -> p a c"),
                                  reads=[dbuf['XBCT']], writes=[b_u[us]])
                            c.dma('sp', u[us][:, 4, c0_:ncol], XBCT[32 + g, :, ib * 512 + c0_: ib * 512 + ncol], reads=[dbuf['XBCT']], writes=[b_u[us]])
                            c.dma('sp', u[us][:, 5, c0_:ncol], XBCT[40 + g, :, ib * 512 + c0_: ib * 512 + ncol], reads=[dbuf['XBCT']], writes=[b_u[us]])
                        pend = [[body(g, ti, i) for g in pair] for ti, i in enumerate(tiles)]
                        active = []
                        while pend or active:
                            if pend and (not active or all(a_[1] >= 3 for a_ in active)):
                                for gobj in pend.pop(0):
                                    active.append([gobj, 0])
                            for a_ in list(active):
                                try:
                                    next(a_[0])
                                    a_[1] += 1
                                except StopIteration:
                                    active.remove(a_)
                        for g in pair:
                            gs_ = g % 2
                            c.dma('sp', GT[4 * g:4 * g + 4, :, ib * 512: ib * 512 + ntl * 128].rearrange("a p c -> p a c"), gst[gs_][:, :, :ntl * 128],
                                  reads=[b_gst[gs_]], writes=[dbuf['GT']])
            c.barrier()

        def phase_C0(l):
            with ExitStack() as st:
                cosA = sb(st, "cosA", [128, TP], F32)
                sinA = sb(st, "sinA", [128, TP], F32)
                cosI = sb(st, "cosI", [128, TP], F32)
                sinI = sb(st, "sinI", [128, TP], F32)
                rA = sb(st, "rA", [128, 128], BF16)
                rI = sb(st, "rI", [128, 128], BF16)
                qn_w = sb(st, "qn_w", [128, 1], F32)
                kn_w = sb(st, "kn_w", [128, 1], F32)
                b_t = Buf()
                c.dma('sp', cosA[:], c_cosA, writes=[b_t])
                c.dma('sp', sinA[:], c_sinA, writes=[b_t])
                c.dma('sp', cosI[:], c_cosI, writes=[b_t])
                c.dma('sp', sinI[:], c_sinI, writes=[b_t])
                c.dma('sp', rA[:], c_rA, writes=[b_t])
                c.dma('sp', rI[:], c_rI, writes=[b_t])
                c.dma('sp', qn_w[:], qnw[l].unsqueeze(1), writes=[b_t])
                c.dma('sp', kn_w[:], knw[l].unsqueeze(1), writes=[b_t])
                xin = [sb(st, f"xin{s}", [128, 512], BF16) for s in range(2)]
                b_xin = [Buf(), Buf()]
                sqL = [sb(st, f"sqC{q}", [128, 512], BF16) for q in range(2)]
                b_sqL = [Buf(), Buf()]
                rstdL = [sb(st, f"rstdC{q}", [128, 512], F32) for q in range(2)]
                b_rstdL = [Buf(), Buf()]
                xnL = [sb(st, f"xnC{q}", [128, 512], BF16) for q in range(2)]
                b_xnL = [Buf(), Buf()]
                t1L = [sb(st, f"t1C{q}", [128, 512], F32) for q in range(2)]
                b_t1L = [Buf(), Buf()]
                t2L = [sb(st, f"t2C{q}", [128, 512], F32) for q in range(2)]
                b_t2L = [Buf(), Buf()]
                xo = [sb(st, f"xoC{s}", [128, 512], BF16) for s in range(2)]
                b_xo = [Buf(), Buf()]
                sspL = [ps(st, f"sspC{q}", [128, 512], F32) for q in range(2)]
                b_sspL = [PB(), PB()]
                rotpL = [ps(st, f"rotpC{q}", [128, 512], F32) for q in range(2)]
                b_rotpL = [PB(), PB()]
                n = 0
                jobs = [('q', h) for h in range(16)] + [('k', h) for h in range(4)] + [('i', h) for h in range(8)]
                for (kind, h) in jobs:
                    for s0 in range(0, TP, 512):
                        sn = min(512, TP - s0)
                        s = n % 2
                        n += 1
                        sq, b_sq, rstd, b_rstd, xn, b_xn = sqL[s], b_sqL[s], rstdL[s], b_rstdL[s], xnL[s], b_xnL[s]
                        t1, b_t1, t2, b_t2 = t1L[s], b_t1L[s], t2L[s], b_t2L[s]
                        ssp, b_ssp, rotp, b_rotp = sspL[s], b_sspL[s], rotpL[s], b_rotpL[s]
                        src, sname = {'q': (QT, 'QT'), 'k': (KT, 'KT'), 'i': (IQT, 'IQT')}[kind]
                        c.dma('sp', xin[s][:, :sn], src[h][:, s0:s0 + sn], reads=[dbuf[sname]], writes=[b_xin[s]])
                        if kind in ('q', 'k'):
                            wv = qn_w if kind == 'q' else kn_w
                            c.op('act', lambda e: e.activation(out=sq[:, :sn], in_=xin[s][:, :sn], func=AF.Square), reads=[b_xin[s]], writes=[b_sq])
                            c.op('pe', lambda e: e.matmul(ssp[:, :sn], lhsT=onesb[:], rhs=sq[:, :sn], start=True, stop=True), reads=[b_sq, bconst], writes=[b_ssp])
                            c.op('act', lambda e: e.activation(out=rstd[:, :sn], in_=ssp[:, :sn], func=AF.Ln, scale=1.0 / 128, bias=EPS), reads=[b_ssp], writes=[b_rstd])
                            c.op('act', lambda e: e.activation(out=rstd[:, :sn], in_=rstd[:, :sn], func=AF.Exp, scale=-0.5), reads=[b_rstd], writes=[b_rstd])
                            c.op('dve', lambda e: e.scalar_tensor_tensor(out=xn[:, :sn], in0=xin[s][:, :sn], scalar=wv[:, 0:1], in1=rstd[:, :sn], op0=ALU.mult, op1=ALU.mult),
                                 reads=[b_xin[s], b_rstd, b_t], writes=[b_xn])
                            xsrc, b_xsrc, rm, cs, sn_ = xn, b_xn, rA, cosA, sinA
                        else:
                            xsrc, b_xsrc, rm, cs, sn_ = xin[s], b_xin[s], rI, cosI, sinI
                        c.op('pe', lambda e: e.matmul(rotp[:, :sn], lhsT=rm[:], rhs=xsrc[:, :sn], start=True, stop=True), reads=[b_xsrc, b_t], writes=[b_rotp])
                        c.op('pool', lambda e: e.tensor_tensor(out=t1[:, :sn], in0=xsrc[:, :sn], in1=cs[:, s0:s0 + sn], op=ALU.mult), reads=[b_xsrc, b_t], writes=[b_t1])
                        c.op('dve', lambda e: e.tensor_tensor(out=t2[:, :sn], in0=rotp[:, :sn], in1=sn_[:, s0:s0 + sn], op=ALU.mult), reads=[b_rotp, b_t], writes=[b_t2])
                        c.op('dve', lambda e: e.tensor_tensor(out=xo[s][:, :sn], in0=t1[:, :sn], in1=t2[:, :sn], op=ALU.add), reads=[b_t1, b_t2], writes=[b_xo[s]])
                        if kind == 'q':
                            c.dma('sp', QR[h][:, s0:s0 + sn], xo[s][:, :sn], reads=[b_xo[s]], writes=[dbuf['QR']])
                        elif kind == 'k':
                            c.dma('sp', KR[h][:, s0:s0 + sn], xo[s][:, :sn], reads=[b_xo[s]], writes=[dbuf['KR']])
                        else:
                            c.dma('sp', IQR[2 * h][:, s0:s0 + sn], xo[s][0:64, :sn], reads=[b_xo[s]], writes=[dbuf['IQR']])
                            c.dma('sp', IQR[2 * h + 1][:, s0:s0 + sn], xo[s][64:128, :sn], reads=[b_xo[s]], writes=[dbuf['IQR']])
            c.barrier()

        def phase_C(l):
            with ExitStack() as st:
                KRs = sb(st, "KRs", [128, 4, TP], BF16)
                Vs = sb(st, "Vs", [128, NT, 512], BF16)
                IKT = sb(st, "IKT", [64, TP], BF16)
                b_KR, b_V, b_IKT = Buf(), Buf(), Buf()
                c.dma('sp', KRs[:], KR.rearrange("a p c -> p a c"), reads=[dbuf['KR']], writes=[b_KR])
                for i in range(NT):
                    c.dma('sp', Vs[:, i, :], V[i * 128:(i + 1) * 128, :], reads=[dbuf['V']], writes=[b_V])
                cosK = sb(st, "cosK", [128, NT, 8], F32)
                sinK = sb(st, "sinK", [128, NT, 8], F32)
                iknw_bc = sb(st, "iknw_bc", [128, 64], F32)
                triq = sb(st, "triq", [128, 2, 128], F32)
                negq = sb(st, "negq", [128, 2, 128], F32)
                pow2 = sb(st, "pow2", [128, NIT + 2], F32)
                b_t = Buf()
                c.dma('sp', cosK[:], c_cosK, writes=[b_t])
                c.dma('sp', sinK[:], c_sinK, writes=[b_t])
                c.dma('sp', iknw_bc[:], iknw[l].partition_broadcast(128), writes=[b_t])
                c.dma('sp', triq[:], c_triq.rearrange("a p c -> p a c"), writes=[b_t])
                c.dma('sp', negq[:], c_negq.rearrange("a p c -> p a c"), writes=[b_t])
                c.dma('sp', pow2[:], c_pow2, writes=[b_t])
                ikw = [sb(st, f"ikw{s}", [128, 80], F32) for s in range(2)]
                b_ikw = [Buf(), Buf()]
                kj = sb(st, "kj", [128, 64], F32)
                kss = sb(st, "kss", [128, 1], F32)
                krs = sb(st, "krs", [128, 1], F32)
                kn = sb(st, "kn", [128, 64], F32)
                kr = sb(st, "kr", [128, 64], BF16)
                ka = sb(st, "ka", [128, 8], F32)
                kb = sb(st, "kb", [128, 8], F32)
                b_kj, b_kss, b_krs, b_kn, b_kr, b_ka, b_kb = [Buf() for _ in range(7)]
                tps = ps(st, "tpsC", [128, 8, 128], BF16)
                b_tps = PB()
                for i in range(NT):
                    s = i % 2
                    c.dma('sp', ikw[s][:], IKW[i * 128:(i + 1) * 128, :], reads=[dbuf['IKW']], writes=[b_ikw[s]])
                    c.op('act', lambda e: e.activation(out=kj[:], in_=ikw[s][:, 0:64], func=AF.Square, accum_out=kss[:]), reads=[b_ikw[s]], writes=[b_kj, b_kss])
                    c.op('act', lambda e: e.activation(out=krs[:], in_=kss[:], func=AF.Ln, scale=1.0 / 64, bias=EPS), reads=[b_kss], writes=[b_krs])
                    c.op('act', lambda e: e.activation(out=krs[:], in_=krs[:], func=AF.Exp, scale=-0.5), reads=[b_krs], writes=[b_krs])
                    c.op('dve', lambda e: e.scalar_tensor_tensor(out=kn[:], in0=ikw[s][:, 0:64], scalar=krs[:, 0:1], in1=iknw_bc[:], op0=ALU.mult, op1=ALU.mult),
                         reads=[b_ikw[s], b_krs, b_t], writes=[b_kn])
                    c.op('dve', lambda e: e.tensor_copy(out=kr[:, 16:64], in_=kn[:, 16:64]), reads=[b_kn], writes=[b_kr])
                    c.op('dve', lambda e: e.tensor_tensor(out=ka[:], in0=kn[:, 0:8], in1=cosK[:, i, :], op=ALU.mult), reads=[b_kn, b_t], writes=[b_ka])
                    c.op('dve', lambda e: e.tensor_tensor(out=kb[:], in0=kn[:, 8:16], in1=sinK[:, i, :], op=ALU.mult), reads=[b_kn, b_t], writes=[b_kb])
                    c.op('dve', lambda e: e.tensor_tensor(out=kr[:, 0:8], in0=ka[:], in1=kb[:], op=ALU.subtract), reads=[b_ka, b_kb], writes=[b_kr])
                    c.op('dve', lambda e: e.tensor_tensor(out=ka[:], in0=kn[:, 8:16], in1=cosK[:, i, :], op=ALU.mult), reads=[b_kn, b_t], writes=[b_ka])
                    c.op('dve', lambda e: e.tensor_tensor(out=kb[:], in0=kn[:, 0:8], in1=sinK[:, i, :], op=ALU.mult), reads=[b_kn, b_t], writes=[b_kb])
                    c.op('dve', lambda e: e.tensor_tensor(out=kr[:, 8:16], in0=ka[:], in1=kb[:], op=ALU.add), reads=[b_ka, b_kb], writes=[b_kr])
                    c.op('pe', lambda e: e.transpose(out=tps[0:64, 0, :], in_=kr[:], identity=identb[:]), reads=[b_kr, bconst], writes=[b_tps])
                    c.op('act', lambda e: e.copy(out=IKT[:, i * 128:(i + 1) * 128], in_=tps[0:64, 0, :]), reads=[b_tps], writes=[b_IKT])

                scoreL = [sb(st, f"score{q}", [128, TP], F32) for q in range(2)]
                maskL = [sb(st, f"maskC{q}", [128, TP], BF16) for q in range(2)]
                maskTL = [sb(st, f"maskT{q}", [128, NT, 128], BF16) for q in range(2)]
                b_scoreL, b_maskL, b_maskTL = [Buf(), Buf()], [Buf(), Buf()], [Buf(), Buf()]
                iqr = [sb(st, f"iqr{s}", [64, 16, 128], BF16) for s in range(2)]
                b_iqr = [Buf(), Buf()]
                qg = [sb(st, f"qg{s}", [128, 16, 128], BF16) for s in range(2)]
                b_qg = [Buf(), Buf()]
                azt = [sb(st, f"azt{s}", [128, 16, 128], BF16) for s in range(2)]
                b_azt = [Buf(), Buf()]
                Rr = [sb(st, f"Rr{s}", [128, 512], F32) for s in range(2)]
                b_Rr = [Buf(), Buf()]
                abswL = [sb(st, f"absw{q}", [128, 16], F32) for q in range(2)]
                sgnwL = [sb(st, f"sgnw{q}", [128, 16], F32) for q in range(2)]
                amaxL = [sb(st, f"amax{q}", [128, 1], F32) for q in range(2)]
                loL = [sb(st, f"lo{q}", [128, 1], F32) for q in range(2)]
                midL = [sb(st, f"mid{q}", [128, 1], F32) for q in range(2)]
                wkL = [sb(st, f"wk{q}", [128, NIT + 2], F32) for q in range(2)]
                cnttL = [sb(st, f"cntt{q}", [128, 1], F32) for q in range(2)]
                gwL = [sb(st, f"gw{q}", [128, 1], F32) for q in range(2)]
                smallB = [[Buf() for _ in range(7)] for q in range(2)]
                pe_ = [sb(st, f"pe{s}", [128, 512], BF16) for s in range(2)]
                b_pe = [Buf(), Buf()]
                pmk = [sb(st, f"pmk{s}", [128, 512], BF16) for s in range(2)]
                b_pmk = [Buf(), Buf()]
                rden = sb(st, "rden", [128, 512], F32)
                b_rden = Buf()
                osb = sb(st, "osb", [128, 512], F32)
                b_osb = Buf()
                sgz = sb(st, "sgz", [128, 512], F32)
                b_sgz = Buf()
                og = [sb(st, f"og{s}", [128, 16, 128], BF16) for s in range(2)]
                b_og = [Buf(), Buf()]
                lps = [ps(st, f"lps{s}", [128, 512], F32) for s in range(2)]
                b_lps = [PB(), PB()]
                spsC = [ps(st, f"spsC{s}", [128, 512], F32) for s in range(2)]
                b_sps = [PB(), PB()]
                ops = ps(st, "opsC", [128, 512], F32)
                dps = ps(st, "dpsC", [128, 512], F32)
                b_ops, b_dps = PB(), PB()
                SC = 128.0 ** -0.5
                cntC = {'nl': 0, 'nsp': 0}

                def bodyC(i, z):
                    score, mask, maskT = scoreL[z], maskL[z], maskTL[z]
                    junk = mask
                    b_score, b_mask, b_maskT = b_scoreL[z], b_maskL[z], b_maskTL[z]
                    b_junk = b_mask
                    absw, sgnw, amax, lo, mid, wk, cntt, gw = abswL[z], sgnwL[z], amaxL[z], loL[z], midL[z], wkL[z], cnttL[z], gwL[z]
                    b_w, b_amax, b_lo, b_mid, b_wk, b_cnt, b_gw = smallB[z]
                    s = i % 2
                    nk = (i + 1) * 128
                    c.dma('sp', iqr[s][:], IQR[:, :, i * 128:(i + 1) * 128].rearrange("h d t -> d h t"), reads=[dbuf['IQR']], writes=[b_iqr[s]])
                    c.dma('sp', qg[s][:], QR[:, :, i * 128:(i + 1) * 128].rearrange("h d t -> d h t"), reads=[dbuf['QR']], writes=[b_qg[s]])
                    c.dma('sp', azt[s][:], AZT[:, :, i * 128:(i + 1) * 128].rearrange("h d t -> d h t"), reads=[dbuf['AZT']], writes=[b_azt[s]])
                    c.dma('sp', ikw[s][:], IKW[i * 128:(i + 1) * 128, :], reads=[dbuf['IKW']], writes=[b_ikw[s]])
                    c.op('act', lambda e: e.activation(out=absw[:], in_=ikw[s][:, 64:80], func=AF.Abs, scale=1.0 / 32),
                         reads=[b_ikw[s]], writes=[b_w])
                    c.op('act', lambda e: e.activation(out=sgnw[:], in_=ikw[s][:, 64:80], func=AF.Sign), reads=[b_ikw[s]], writes=[b_w])
                    for k0 in range(0, nk, 512):
                        kn_ = min(512, nk - k0)
                        for h in range(16):
                            li = cntC['nl'] % 2
                            cntC['nl'] += 1
                            c.op('pe', lambda e: e.matmul(lps[li][:, :kn_], lhsT=iqr[s][:, h, :], rhs=IKT[:, k0:k0 + kn_], start=True, stop=True),
                                 reads=[b_iqr[s], b_IKT], writes=[b_lps[li]])
                            c.op('act', lambda e: e.activation(out=Rr[li][:, :kn_], in_=lps[li][:, :kn_], func=AF.Relu, scale=absw[:, h:h + 1]),
                                 reads=[b_lps[li], b_w], writes=[b_Rr[li]])
                            if h == 0:
                                c.op('dve', lambda e: e.tensor_scalar(out=score[:, k0:k0 + kn_], in0=Rr[li][:, :kn_], scalar1=sgnw[:, 0:1], scalar2=None, op0=ALU.mult),
                                     reads=[b_Rr[li], b_w], writes=[b_score])
                            else:
                                c.op('dve', lambda e: e.scalar_tensor_tensor(out=score[:, k0:k0 + kn_], in0=Rr[li][:, :kn_], scalar=sgnw[:, h:h + 1],
                                                                             in1=score[:, k0:k0 + kn_], op0=ALU.mult, op1=ALU.add),
                                     reads=[b_Rr[li], b_w, b_score], writes=[b_score])
                    mi = 1 if i == 0 else 0
                    dsl = slice(i * 128, (i + 1) * 128)
                    c.op('dve', lambda e: e.tensor_tensor(out=score[:, dsl], in0=score[:, dsl], in1=triq[:, mi, :], op=ALU.mult), reads=[b_score, b_t], writes=[b_score])
                    if i > 0:
                        c.op('dve', lambda e: e.memset(score[:, 0:112], 0.0), writes=[b_score])
                    c.op('dve', lambda e: e.tensor_reduce(out=amax[:], in_=score[:, :nk], axis=AX.X, op=ALU.max, apply_absolute_value=True),
                         reads=[b_score], writes=[b_amax])
                    c.op('dve', lambda e: e.tensor_tensor(out=score[:, dsl], in0=score[:, dsl], in1=negq[:, mi, :], op=ALU.add), reads=[b_score, b_t], writes=[b_score])
                    if i > 0:
                        c.op('dve', lambda e: e.memset(score[:, 0:112], NEG), writes=[b_score])
                    c.op('dve', lambda e: e.tensor_scalar(out=lo[:], in0=amax[:], scalar1=-1.0, scalar2=-1.0, op0=ALU.mult, op1=ALU.add), reads=[b_amax], writes=[b_lo])
                    c.op('dve', lambda e: e.tensor_scalar(out=gw[:], in0=amax[:], scalar1=2.0, scalar2=2.0, op0=ALU.mult, op1=ALU.add), reads=[b_amax], writes=[b_gw])
                    c.op('dve', lambda e: e.tensor_scalar(out=wk[:], in0=pow2[:], scalar1=gw[:, 0:1], scalar2=None, op0=ALU.mult), reads=[b_gw, b_t], writes=[b_wk])
                    c.op('dve', lambda e: e.tensor_tensor(out=mid[:], in0=lo[:], in1=wk[:, 1:2], op=ALU.add), reads=[b_lo, b_wk], writes=[b_mid])
                    yield
                    for it in range(1, NIT + 1):
                        c.op('dve', lambda e: e.tensor_scalar(out=junk[:, :nk], in0=score[:, :nk], scalar1=mid[:, 0:1], scalar2=None, op0=ALU.is_ge, op1=ALU.add,
                                                              accum_out=cntt[:]),
                             reads=[b_score, b_mid], writes=[b_junk, b_cnt])
                        yield
                        c.op('dve', lambda e: e.tensor_scalar(out=gw[:], in0=cntt[:], scalar1=float(KSEL) - 0.5, scalar2=wk[:, it:it + 1], op0=ALU.is_ge, op1=ALU.mult),
                             reads=[b_cnt, b_wk], writes=[b_gw])
                        c.op('dve', lambda e: e.tensor_tensor(out=lo[:], in0=lo[:], in1=gw[:], op=ALU.add), reads=[b_lo, b_gw], writes=[b_lo])
                        c.op('dve', lambda e: e.tensor_tensor(out=mid[:], in0=lo[:], in1=wk[:, it + 1:it + 2], op=ALU.add), reads=[b_lo, b_wk], writes=[b_mid])
                        yield
                    c.op('dve', lambda e: e.tensor_scalar(out=mask[:, :nk], in0=score[:, :nk], scalar1=lo[:, 0:1], scalar2=None, op0=ALU.is_ge),
                         reads=[b_score, b_lo], writes=[b_mask])
                    for j0 in range(0, i + 1, 8):
                        jn = min(8, i + 1 - j0)
                        for jj in range(jn):
                            c.op('pe', lambda e: e.transpose(out=tps[:, jj, :], in_=mask[:, (j0 + jj) * 128:(j0 + jj + 1) * 128], identity=identb[:]),
                                 reads=[b_mask, bconst], writes=[b_tps])
                        c.op('act', lambda e: e.copy(out=maskT[:, j0:j0 + jn, :], in_=tps[:, 0:jn, :]), reads=[b_tps], writes=[b_maskT])
                    for kg in range(4):
                        for j in range(i + 1):
                            si = cntC['nsp'] % 2
                            cntC['nsp'] += 1
                            c.op('pe', lambda e: e.matmul(spsC[si][:], lhsT=KRs[:, kg, j * 128:(j + 1) * 128], rhs=qg[s][:, 4 * kg:4 * kg + 4, :], start=True, stop=True),
                                 reads=[b_KR, b_qg[s]], writes=[b_sps[si]])
                            c.op('act', lambda e: e.activation(out=pe_[si][:], in_=spsC[si][:], func=AF.Exp, scale=SC), reads=[b_sps[si]], writes=[b_pe[si]])
                            c.op('pool', lambda e: e.tensor_tensor(out=pmk[si][:].rearrange("p (h t) -> p h t", h=4), in0=pe_[si][:].rearrange("p (h t) -> p h t", h=4),
                                                                   in1=maskT[:, j, :].unsqueeze(1).to_broadcast([128, 4, 128]), op=ALU.mult),
                                 reads=[b_pe[si], b_maskT], writes=[b_pmk[si]])
                            c.op('pe', lambda e: e.matmul(ops[:], lhsT=Vs[:, j, kg * 128:(kg + 1) * 128], rhs=pmk[si][:], start=(j == 0), stop=(j == i)),
                                 reads=[b_V, b_pmk[si]], writes=[b_ops])
                            c.op('pe', lambda e: e.matmul(dps[:], lhsT=onesb[:], rhs=pmk[si][:], start=(j == 0), stop=(j == i)),
                                 reads=[b_pmk[si], bconst], writes=[b_dps])
                        c.op('dve', lambda e: e.tensor_scalar(out=rden[:], in0=dps[:], scalar1=1e-30, scalar2=None, op0=ALU.max), reads=[b_dps], writes=[b_rden])
                        c.op('dve', lambda e: e.reciprocal(out=rden[:], in_=rden[:]), reads=[b_rden], writes=[b_rden])
                        c.op('dve', lambda e: e.tensor_tensor(out=osb[:], in0=ops[:], in1=rden[:], op=ALU.mult), reads=[b_ops, b_rden], writes=[b_osb])
                        c.op('act', lambda e: e.activation(out=sgz[:].rearrange("p (h t) -> p h t", h=4), in_=azt[s][:, 4 * kg:4 * kg + 4, :], func=AF.Silu),
                             reads=[b_azt[s]], writes=[b_sgz])
                        c.op('pool', lambda e: e.tensor_tensor(out=og[s][:, 4 * kg:4 * kg + 4, :], in0=osb[:].rearrange("p (h t) -> p h t", h=4),
                                                               in1=sgz[:].rearrange("p (h t) -> p h t", h=4), op=ALU.mult),
                             reads=[b_osb, b_sgz], writes=[b_og[s]])
                    c.dma('sp', OT[:, :, i * 128:(i + 1) * 128].rearrange("h d t -> d h t"), og[s][:], reads=[b_og[s]], writes=[dbuf['OT']])
                    yield

                for i0 in range(0, NT, 2):
                    gens = [bodyC(i0, 0)]
                    if i0 + 1 < NT:
                        gens.append(bodyC(i0 + 1, 1))
                    while gens:
                        for g_ in list(gens):
                            try:
                                next(g_)
                            except StopIteration:
                                gens.remove(g_)
            c.barrier()

        def phase_D(l, hin, hin_n, hout, hout_n):
            last = (l == L - 1)
            with ExitStack() as st:
                wso_l = w_so[l].rearrange("(k p) n -> p k n", p=128)
                wao_l = w_ao[l].rearrange("(k p) n -> p k n", p=128)
                wo_l = w_o[l].rearrange("(k p) n -> p k n", p=128)
                Wo = sb(st, "Wo", [128, 16, D], BF16)
                b_Wo = Buf()
                for q4 in range(4):
                    c.dma('pool', Wo[:, :, q4 * 512:(q4 + 1) * 512], wo_l[:, :, q4 * 512:(q4 + 1) * 512], writes=[b_Wo])
                Wso = [sb(st, f"Wso{s}", [128, 32, 256], BF16) for s in range(2)]
                Wao = [sb(st, f"Wao{s}", [128, 16, 256], BF16) for s in range(2)]
                b_Wso = [Buf(), Buf()]
                b_Wao = [Buf(), Buf()]
                gtb = [sb(st, f"gtb{s}", [128, 32, 256], BF16) for s in range(2)]
                otb_ = [sb(st, f"otbD{s}", [128, 16, 256], BF16) for s in range(2)]
                b_gtb = [Buf(), Buf()]
                b_otb = [Buf(), Buf()]
                gsb = [sb(st, f"gsb{s}", [128, 2, 256], BF16) for s in range(2)]
                gab = [sb(st, f"gab{s}", [128, 2, 256], BF16) for s in range(2)]
                b_gsb = [Buf(), Buf()]
                b_gab = [Buf(), Buf()]
                sgs = sb(st, "sgs", [128, 256], F32)
                sga = sb(st, "sga", [128, 256], F32)
                m1 = sb(st, "m1", [128, 256], F32)
                m2 = sb(st, "m2", [128, 256], F32)
                b_sgs, b_sga, b_m1, b_m2 = Buf(), Buf(), Buf(), Buf()
                mg = [sb(st, f"mg{s}", [128, 2, 256], BF16) for s in range(2)]
                b_mg = [Buf(), Buf()]
                ysp = [ps(st, f"yspD{s}", [128, 512], F32) for s in range(2)]
                yap = [ps(st, f"yapD{s}", [128, 512], F32) for s in range(2)]
                b_ysp = [PB(), PB()]
                b_yap = [PB(), PB()]
                nblk = 0
                npp = 0
                for dc in range(8):
                    ws = dc % 2
                    c.dma('pool', Wso[ws][:], wso_l[:, :, dc * 256:(dc + 1) * 256], writes=[b_Wso[ws]])
                    c.dma('pool', Wao[ws][:], wao_l[:, :, dc * 256:(dc + 1) * 256], writes=[b_Wao[ws]])
                    for t0 in range(0, TP, 256):
                        s = nblk % 2
                        tn = min(256, TP - t0)
                        nblk += 1
                        c.dma('sp', gtb[s][:, :, :tn], GT[:, :, t0:t0 + tn].rearrange("a p c -> p a c"), reads=[dbuf['GT']], writes=[b_gtb[s]])
                        c.dma('sp', otb_[s][:, :, :tn], OT[:, :, t0:t0 + tn].rearrange("a p c -> p a c"), reads=[dbuf['OT']], writes=[b_otb[s]])
                        c.dma('sp', gsb[s][:, :, :tn], GST[2 * dc:2 * dc + 2, :, t0:t0 + tn].rearrange("a p c -> p a c"), reads=[dbuf['GST']], writes=[b_gsb[s]])
                        c.dma('sp', gab[s][:, :, :tn], GAT[2 * dc:2 * dc + 2, :, t0:t0 + tn].rearrange("a p c -> p a c"), reads=[dbuf['GAT']], writes=[b_gab[s]])
                        for dd in range(2):
                            pi = npp % 2
                            npp += 1
                            for kc in range(32):
                                c.op('pe', lambda e: e.matmul(ysp[pi][:, :tn], lhsT=Wso[ws][:, kc, dd * 128:(dd + 1) * 128], rhs=gtb[s][:, kc, :tn], start=(kc == 0), stop=(kc == 31)),
                                     reads=[b_Wso[ws], b_gtb[s]], writes=[b_ysp[pi]])
                            for kc in range(16):
                                c.op('pe', lambda e: e.matmul(yap[pi][:, :tn], lhsT=Wao[ws][:, kc, dd * 128:(dd + 1) * 128], rhs=otb_[s][:, kc, :tn], start=(kc == 0), stop=(kc == 15)),
                                     reads=[b_Wao[ws], b_otb[s]], writes=[b_yap[pi]])
                            c.op('act', lambda e: e.activation(out=sgs[:, :tn], in_=gsb[s][:, dd, :tn], func=AF.Sigmoid), reads=[b_gsb[s]], writes=[b_sgs])
                            c.op('act', lambda e: e.activation(out=sga[:, :tn], in_=gab[s][:, dd, :tn], func=AF.Sigmoid), reads=[b_gab[s]], writes=[b_sga])
                            c.op('dve', lambda e: e.tensor_tensor(out=m1[:, :tn], in0=ysp[pi][:, :tn], in1=sgs[:, :tn], op=ALU.mult), reads=[b_ysp[pi], b_sgs], writes=[b_m1])
                            c.op('dve', lambda e: e.tensor_tensor(out=m2[:, :tn], in0=yap[pi][:, :tn], in1=sga[:, :tn], op=ALU.mult), reads=[b_yap[pi], b_sga], writes=[b_m2])
                            c.op('pool', lambda e: e.tensor_tensor(out=mg[s][:, dd, :tn], in0=m1[:, :tn], in1=m2[:, :tn], op=ALU.add), reads=[b_m1, b_m2], writes=[b_mg[s]])
                        c.dma('sp', MT[2 * dc:2 * dc + 2, :, t0:t0 + tn].rearrange("a p c -> p a c"), mg[s][:, :, :tn], reads=[b_mg[s]], writes=[dbuf['MT']])
                mt = [sb(st, f"mtD{s}", [128, 16, 128], BF16) for s in range(2)]
                b_mt = [Buf(), Buf()]
                hold = [sb(st, f"hold{s}", [128, D], F32) for s in range(2)]
                b_hold = [Buf(), Buf()]
                hnew = hold
                b_hnew = b_hold
                for i in range(NT):
                    if last and i == 0:
                        continue
                    s = i % 2
                    c.dma('sp', mt[s][:], MT[:, :, i * 128:(i + 1) * 128].rearrange("a p c -> p a c"), reads=[dbuf['MT']], writes=[b_mt[s]])
                    c.dma('sp', hold[s][:], hin[i * 128:(i + 1) * 128, :], reads=[dbuf[hin_n]], writes=[b_hold[s]])
                    for q4 in range(4):
                        pi = npp % 2
                        npp += 1
                        for kc in range(16):
                            c.op('pe', lambda e: e.matmul(ysp[pi][:], lhsT=mt[s][:, kc, :], rhs=Wo[:, kc, q4 * 512:(q4 + 1) * 512], start=(kc == 0), stop=(kc == 15)),
                                 reads=[b_mt[s], b_Wo], writes=[b_ysp[pi]])
                        c.op('dve', lambda e: e.tensor_tensor(out=hnew[s][:, q4 * 512:(q4 + 1) * 512], in0=ysp[pi][:], in1=hold[s][:, q4 * 512:(q4 + 1) * 512], op=ALU.add),
                             reads=[b_ysp[pi], b_hold[s]], writes=[b_hnew[s]])
                    if last:
                        c.dma('sp', hout[(i - 1) * 128:i * 128, :], hnew[s][:], reads=[b_hnew[s]], writes=[dbuf[hout_n]])
                    elif i == 0:
                        c.dma('sp', hout[112:128, :], hnew[s][112:128, :], reads=[b_hnew[s]], writes=[dbuf[hout_n]])
                    else:
                        c.dma('sp', hout[i * 128:(i + 1) * 128, :], hnew[s][:], reads=[b_hnew[s]], writes=[dbuf[hout_n]])
            c.barrier()

        if L > 1:
            zrow = sb(es, "zrow", [112, D], F32)
            bzr = Buf()
            c.op('pool', lambda e: e.memset(zrow[:], 0.0), writes=[bzr])
            c.dma('sp', HA[0:112, :], zrow[:], reads=[bzr], writes=[dbuf['HA']])
            if L > 2:
                c.dma('sp', HB[0:112, :], zrow[:], reads=[bzr], writes=[dbuf['HB']])

        for l in range(L):
            hin, hin_n = hseq[l]
            hout, hout_n = hseq[l + 1]
            if 'A' in PH:
                phase_A(l, hin, hin_n)
            if 'B' in PH:
                phase_B(l)
            if 'C' in PH:
                phase_C0(l)
                phase_C(l)
            if 'D' in PH:
                phase_D(l, hin, hin_n, hout, hout_n)
        c.barrier()
        print("ops", c.nops, "waits", c.nwaits, "sems", c.nsem)
    return nc


def rope_np(pos, rot):
    inv = (500000.0 ** (-np.arange(0, rot, 2, dtype=np.float32) / rot)).astype(np.float32)
    ang = pos.astype(np.float32)[:, None] * inv[None, :]
    return np.cos(ang).astype(np.float32), np.sin(ang).astype(np.float32)


def make_consts(NT):
    TP = NT * 128
    bf = ml_dtypes.bfloat16
    idx = np.arange(128)
    cst = {}
    cst['c_identb'] = np.eye(128, dtype=np.float32).astype(bf)
    cst['c_identf'] = np.eye(128, dtype=np.float32)
    cst['c_trile'] = (idx[:, None] <= idx[None, :]).astype(np.float32)
    cst['c_strict'] = (idx[:, None] > idx[None, :]).astype(np.float32)
    triq = (idx[None, :] <= idx[:, None]).astype(np.float32)
    triq0 = triq * (idx[None, :] >= 112).astype(np.float32)
    cst['c_triq'] = np.stack([triq, triq0]).astype(np.float32)
    cst['c_negq'] = (np.float32(NEG) * (1.0 - cst['c_triq'])).astype(np.float32)
    cst['c_pow2'] = np.tile((2.0 ** -np.arange(NIT + 2, dtype=np.float64)).astype(np.float32)[None, :], (128, 1))
    pos = np.maximum(np.arange(TP) - 112, 0)
    ca, sa = rope_np(pos, 32)
    ci, si = rope_np(pos, 16)
    cosA = np.ones((128, TP), np.float32)
    sinA = np.zeros((128, TP), np.float32)
    cosA[0:16] = ca.T
    cosA[16:32] = ca.T
    sinA[0:16] = sa.T
    sinA[16:32] = sa.T
    cosI = np.ones((128, TP), np.float32)
    sinI = np.zeros((128, TP), np.float32)
    for hh in range(2):
        cosI[hh * 64:hh * 64 + 8] = ci.T
        cosI[hh * 64 + 8:hh * 64 + 16] = ci.T
        sinI[hh * 64:hh * 64 + 8] = si.T
        sinI[hh * 64 + 8:hh * 64 + 16] = si.T
    cst['c_cosA'], cst['c_sinA'], cst['c_cosI'], cst['c_sinI'] = cosA, sinA, cosI, sinI
    rA = np.zeros((128, 128), np.float32)
    for i in range(16):
        rA[i + 16, i] = -1.0
        rA[i, i + 16] = 1.0
    rI = np.zeros((128, 128), np.float32)
    for hh in range(2):
        for i in range(8):
            rI[hh * 64 + i + 8, hh * 64 + i] = -1.0
            rI[hh * 64 + i, hh * 64 + i + 8] = 1.0
    cst['c_rA'] = rA.astype(bf)
    cst['c_rI'] = rI.astype(bf)
    cst['c_cosK'] = np.ascontiguousarray(ci.reshape(NT, 128, 8).transpose(1, 0, 2))
    cst['c_sinK'] = np.ascontiguousarray(si.reshape(NT, 128, 8).transpose(1, 0, 2))
    return cst


def make_inputs(NT, L, x, meta_tokens, norm_w, w_in, conv_w, conv_b, dt_bias, a_log, d_skip, ssm_norm_w,
                w_ssm_out, q_norm_w, k_norm_w, idx_k_norm_w, w_attn_out, w_out):
    f = lambda a: np.ascontiguousarray(np.asarray(a, dtype=np.float32))
    B = x.shape[0]
    TP = NT * 128
    cst = make_consts(NT)
    shared = dict(cst)
    shared.update(w_in=f(w_in[:L]), w_ssm_out=f(w_ssm_out[:L]), w_attn_out=f(w_attn_out[:L]), w_out=f(w_out[:L]),
                  norm_w=f(norm_w[:L]), conv_b=f(conv_b[:L]), dt_bias=f(dt_bias[:L]), a_log=f(a_log[:L]), d_skip=f(d_skip[:L]),
                  ssm_norm_w=f(ssm_norm_w[:L]), q_norm_w=f(q_norm_w[:L]), k_norm_w=f(k_norm_w[:L]), idx_k_norm_w=f(idx_k_norm_w[:L]))
    cw = f(conv_w[:L])
    shared['convw_p'] = np.ascontiguousarray(cw.reshape(L, 4, 48, 128).transpose(0, 3, 2, 1).reshape(L, 128, 192))
    shared['convb_p'] = np.ascontiguousarray(f(conv_b[:L]).reshape(L, 48, 128).transpose(0, 2, 1))
    maps = []
    for b in range(B):
        h0 = np.zeros((TP, D), np.float32)
        h0[112:128] = f(meta_tokens)
        h0[128:] = f(x[b])
        m = dict(shared)
        m['h0'] = h0
        maps.append(m)
    return maps


_CACHE = {}


def kernel(x, meta_tokens, norm_w, w_in, conv_w, conv_b, dt_bias, a_log, d_skip, ssm_norm_w,
           w_ssm_out, q_norm_w, k_norm_w, idx_k_norm_w, w_attn_out, w_out):
    x = np.asarray(x)
    B, S, _ = x.shape
    NT = S // 128 + 1
    L = np.asarray(w_in).shape[0]
    cfg = dict(NT=NT, DEPTH=L, KSEL=min(256, S // 4))
    nc = build(cfg)
    maps = make_inputs(NT, L, x, meta_tokens, norm_w, w_in, conv_w, conv_b, dt_bias, a_log, d_skip, ssm_norm_w,
                       w_ssm_out, q_norm_w, k_norm_w, idx_k_norm_w, w_attn_out, w_out)
    res = run_bass_kernel_spmd(nc, maps, core_ids=list(range(B)))
    return np.stack([np.asarray(r["out"], dtype=np.float32) for r in res.results], axis=0)
```

```python
import numpy as np
import ml_dtypes
from contextlib import ExitStack
import concourse.bass as bass
import concourse.mybir as mybir
from concourse.bass_utils import run_bass_kernel_spmd

F32 = mybir.dt.float32
BF16 = mybir.dt.bfloat16
AF = mybir.ActivationFunctionType
ALU = mybir.AluOpType
AX = mybir.AxisListType

D = 2048
NIN = 20624
DI = 4096
NH = 64
EPS = 1e-6
NIT = 26
NEG = -1.0e30
NDS = 12
SEM_EPOCH = 20000


class Buf:
    __slots__ = ('w', 'r', 'excl')

    def __init__(self, excl=False):
        self.w = None
        self.r = {}
        self.excl = excl


def PB():
    return Buf(True)


class Ctx:
    def __init__(self, nc, es):
        self.nc = nc
        self.es = es
        self.eng = {'pe': nc.tensor, 'act': nc.scalar, 'dve': nc.vector, 'pool': nc.gpsimd, 'sp': nc.sync}
        self.sem = {}
        self.cnt = {}
        self.nsem = 0
        self.waited = {e: {} for e in self.eng}
        for e in self.eng:
            self._newsem(e)
        self.dsem = {}
        self.dnext = {}
        for q in ('sp', 'act', 'pool'):
            self.dsem[q] = [[es.enter_context(nc.semaphore(f'dq_{q}_{i}')), 0] for i in range(NDS)]
            self.dnext[q] = 0
        self.nops = 0
        self.nwaits = 0

    def _newsem(self, e):
        self.nsem += 1
        self.sem[e] = self.es.enter_context(self.nc.semaphore(f's_{e}_{self.nsem}'))
        self.cnt[e] = 0

    def _wait(self, e, tok):
        sem, val, key, src = tok
        if self.waited[e].get(key, 0) >= val:
            return
        self.eng[e].wait_ge(sem, val)
        self.nwaits += 1
        self.waited[e][key] = val

    def _dep1(self, e, tok):
        if tok[3] == e and e == 'pe':
            return
        self._wait(e, tok)

    def _deps(self, e, reads, writes):
        for b in reads:
            if b.w is not None:
                self._dep1(e, b.w)
        for b in writes:
            if b.w is not None:
                self._dep1(e, b.w)
            for t in b.r.values():
                self._dep1(e, t)

    def _mark(self, tok, reads, writes):
        for b in writes:
            b.w = tok
            b.r = {}
        for b in reads:
            if tok[3] == 'dma':
                b.r[tok[2]] = tok
            else:
                b.r[tok[3]] = tok

    def op(self, e, fn, reads=(), writes=()):
        if any(b.excl for b in reads):
            writes = list(writes) + [b for b in reads if b.excl]
            reads = [b for b in reads if not b.excl]
        self._deps(e, reads, writes)
        inst = fn(self.eng[e])
        if self.cnt[e] >= SEM_EPOCH:
            self._newsem(e)
        self.cnt[e] += 1
        inst.then_inc(self.sem[e], 1)
        tok = (self.sem[e], self.cnt[e], id(self.sem[e]), e)
        self._mark(tok, reads, writes)
        self.nops += 1
        return tok

    def dma(self, q, out, in_, reads=(), writes=(), **kw):
        slot = self.dsem[q][self.dnext[q]]
        self.dnext[q] = (self.dnext[q] + 1) % NDS
        sem, cnt = slot
        if cnt > 0:
            self._wait(q, (sem, cnt, id(sem), 'dma'))
        self._deps(q, reads, writes)
        inst = self.eng[q].dma_start(out=out, in_=in_, **kw)
        inst.then_inc(sem, 16)
        slot[1] = cnt + 16
        tok = (sem, cnt + 16, id(sem), 'dma')
        self._mark(tok, reads, writes)
        self.nops += 1
        return tok

    def barrier(self):
        toks = [(self.sem[e], self.cnt[e], id(self.sem[e]), e) for e in self.eng if self.cnt[e] > 0]
        for q in self.dsem:
            for (s, cn) in self.dsem[q]:
                if cn > 0:
                    toks.append((s, cn, id(s), 'dma'))
        for e in self.eng:
            for t in toks:
                if t[3] != e:
                    self._wait(e, t)


def build(cfg):
    NT = cfg['NT']
    L = cfg['DEPTH']
    KSEL = cfg['KSEL']
    DBG = cfg.get('debug', False)
    PH = cfg.get('phases', 'ABCD')
    TP = NT * 128
    nc = bass.Bass("TRN2", target_bir_lowering=False)

    def din(name, shape, dt=F32):
        return nc.dram_tensor(name, list(shape), dt, kind="ExternalInput").ap()

    def dscr(name, shape, dt):
        return nc.dram_tensor(name, list(shape), dt, kind=("ExternalOutput" if DBG else "Internal")).ap()

    h0 = din("h0", [TP, D])
    w_in = din("w_in", [L, D, NIN])
    w_so = din("w_ssm_out", [L, DI, D])
    w_ao = din("w_attn_out", [L, D, D])
    w_o = din("w_out", [L, D, D])
    norm_w = din("norm_w", [L, D])
    convw_p = din("convw_p", [L, 128, 192])
    convb_p = din("convb_p", [L, 128, 48])
    conv_b = din("conv_b", [L, 6144])
    dt_bias = din("dt_bias", [L, NH])
    a_log = din("a_log", [L, NH])
    d_skip = din("d_skip", [L, NH])
    ssm_nw = din("ssm_norm_w", [L, DI])
    qnw = din("q_norm_w", [L, 128])
    knw = din("k_norm_w", [L, 128])
    iknw = din("idx_k_norm_w", [L, 64])
    c_identb = din("c_identb", [128, 128], BF16)
    c_identf = din("c_identf", [128, 128])
    c_trile = din("c_trile", [128, 128])
    c_strict = din("c_strict", [128, 128])
    c_triq = din("c_triq", [2, 128, 128])
    c_negq = din("c_negq", [2, 128, 128])
    c_pow2 = din("c_pow2", [128, NIT + 2])
    c_cosA = din("c_cosA", [128, TP])
    c_sinA = din("c_sinA", [128, TP])
    c_cosI = din("c_cosI", [128, TP])
    c_sinI = din("c_sinI", [128, TP])
    c_rA = din("c_rA", [128, 128], BF16)
    c_rI = din("c_rI", [128, 128], BF16)
    c_cosK = din("c_cosK", [128, NT, 8])
    c_sinK = din("c_sinK", [128, NT, 8])
    out = nc.dram_tensor("out", [(NT - 1) * 128, D], F32, kind="ExternalOutput").ap()

    HA = dscr("HA", [TP, D], F32)
    HB = dscr("HB", [TP, D], F32)
    Z = dscr("Z", [TP, DI], BF16)
    XBCT = dscr("XBCT", [48, 128, TP + 3], BF16)
    DT = dscr("DT", [TP, 64], F32)
    QT = dscr("QT", [16, 128, TP], BF16)
    KT = dscr("KT", [4, 128, TP], BF16)
    V = dscr("V", [TP, 512], BF16)
    AZT = dscr("AZT", [16, 128, TP], BF16)
    IQT = dscr("IQT", [8, 128, TP], BF16)
    IKW = dscr("IKW", [TP, 80], F32)
    GST = dscr("GST", [16, 128, TP], BF16)
    GAT = dscr("GAT", [16, 128, TP], BF16)
    GT = dscr("GT", [32, 128, TP], BF16)
    QR = dscr("QR", [16, 128, TP], BF16)
    KR = dscr("KR", [4, 128, TP], BF16)
    IQR = dscr("IQR", [16, 64, TP], BF16)
    OT = dscr("OT", [16, 128, TP], BF16)
    MT = dscr("MT", [16, 128, TP], BF16)

    dbuf = {n: Buf() for n in ['HA', 'HB', 'Z', 'XBCT', 'DT', 'QT', 'KT', 'V', 'AZT', 'IQT', 'IKW', 'GST', 'GAT',
                               'GT', 'QR', 'KR', 'IQR', 'OT', 'MT', 'out', 'h0']}

    es = ExitStack()
    with es:
        c = Ctx(nc, es)

        uid = [0]

        def sb(st, name, shape, dt):
            uid[0] += 1
            return st.enter_context(nc.sbuf_tensor(f"{name}_{uid[0]}", list(shape), dt))

        def ps(st, name, shape, dt):
            uid[0] += 1
            return st.enter_context(nc.psum_tensor(f"{name}_{uid[0]}", list(shape), dt))

        identb = sb(es, "identb", [128, 128], BF16)
        identf = sb(es, "identf", [128, 128], F32)
        onesb = sb(es, "onesb", [128, 128], BF16)
        onesf = sb(es, "onesf", [128, 128], F32)
        trile = sb(es, "trile", [128, 128], F32)
        strict = sb(es, "strict", [128, 128], F32)
        bconst = Buf()
        c.dma('sp', identb[:], c_identb, writes=[bconst])
        c.dma('sp', identf[:], c_identf, writes=[bconst])
        c.dma('sp', trile[:], c_trile, writes=[bconst])
        c.dma('sp', strict[:], c_strict, writes=[bconst])
        c.op('dve', lambda e: e.memset(onesb[:], 1.0), writes=[bconst])
        c.op('dve', lambda e: e.memset(onesf[:], 1.0), writes=[bconst])

        hseq = [(h0, 'h0')]
        for l in range(L):
            if l == L - 1:
                hseq.append((out, 'out'))
            else:
                hseq.append((HA, 'HA') if l % 2 == 0 else (HB, 'HB'))

        def phase_A(l, hin, hin_n):
            NBT = min(11, NT)
            with ExitStack() as st:
                hnT = sb(st, "hnT", [128, 16, NBT * 128], BF16)
                b_hnT = Buf()
                normw = sb(st, "normw", [128, D], F32)
                b_nw = Buf()
                c.dma('sp', normw[:], norm_w[l].partition_broadcast(128), writes=[b_nw])
                Ht = [sb(st, f"Ht{s}", [128, D], F32) for s in range(2)]
                b_Ht = [Buf(), Buf()]
                hnb = [sb(st, f"hnb{s}", [128, D], BF16) for s in range(2)]
                b_hnb = [Buf(), Buf()]
                junk = sb(st, "junkA", [128, D], BF16)
                b_junk = Buf()
                ssq = [sb(st, f"ssq{s}", [128, 1], F32) for s in range(2)]
                b_ss = [Buf(), Buf()]
                rs = [sb(st, f"rs{s}", [128, 1], F32) for s in range(2)]
                b_rs = [Buf(), Buf()]
                Wb = [sb(st, f"Wb{s}", [128, 16, 512], BF16) for s in range(3)]
                b_Wb = [Buf() for _ in range(3)]
                otb = [sb(st, f"otb{s}", [128, 512], BF16) for s in range(3)]
                b_otb = [Buf() for _ in range(3)]
                otf = [sb(st, f"otf{s}", [128, 128], F32) for s in range(2)]
                b_otf = [Buf() for _ in range(2)]
                ofm = [sb(st, f"ofm{s}", [128, NBT * 128], BF16) for s in range(3)]
                b_ofm = [Buf() for _ in range(3)]
                pt = [ps(st, f"ptA{s}", [128, 8, 128], BF16) for s in range(2)]
                b_pt = [PB(), PB()]
                pm = [ps(st, f"pmA{s}", [128, 512], F32) for s in range(4)]
                b_pm = [PB() for _ in range(4)]
                cnt = {'w': 0, 'pm': 0, 'otb': 0, 'otf': 0, 'ofm': 0, 'ev': 0}
                w_l = w_in[l].rearrange("(k p) n -> p k n", p=128)

                chunks = []
                for c0 in range(0, 4096, 512):
                    chunks.append(('tok', c0, 512, 'Z', c0))
                for c0 in range(4096, 10240, 512):
                    chunks.append(('fm', c0, 512, 'XBCT', (c0 - 4096) // 128))
                chunks.append(('tok', 10240, 64, 'DT', 0))
                for c0 in range(10304, 12352, 512):
                    chunks.append(('fm', c0, 512, 'QT', (c0 - 10304) // 128))
                chunks.append(('fm', 12352, 512, 'KT', 0))
                chunks.append(('tok', 12864, 512, 'V', 0))
                for c0 in range(13376, 15424, 512):
                    chunks.append(('fm', c0, 512, 'AZT', (c0 - 13376) // 128))
                for c0 in range(15424, 16448, 512):
                    chunks.append(('fm', c0, 512, 'IQT', (c0 - 15424) // 128))
                chunks.append(('tok', 16448, 80, 'IKW', 0))
                for c0 in range(16528, 18576, 512):
                    chunks.append(('fm', c0, 512, 'GST', (c0 - 16528) // 128))
                for c0 in range(18576, 20624, 512):
                    chunks.append(('fm', c0, 512, 'GAT', (c0 - 18576) // 128))
                dmap = {'Z': Z, 'XBCT': XBCT, 'DT': DT, 'QT': QT, 'KT': KT, 'V': V, 'AZT': AZT, 'IQT': IQT,
                        'IKW': IKW, 'GST': GST, 'GAT': GAT}

                def evac(dst_ap, src_ap, reads, writes):
                    cnt['ev'] += 1
                    if cnt['ev'] % 2 == 0:
                        c.op('act', lambda e: e.copy(out=dst_ap, in_=src_ap), reads=reads, writes=writes)
                    else:
                        c.op('dve', lambda e: e.tensor_copy(out=dst_ap, in_=src_ap), reads=reads, writes=writes)

                for t0 in range(0, NT, NBT):
                    nb = min(NBT, NT - t0)
                    for ti in range(nb):
                        i = t0 + ti
                        s = ti % 2
                        c.dma('sp', Ht[s][:], hin[i * 128:(i + 1) * 128, :], reads=[dbuf[hin_n]], writes=[b_Ht[s]])
                        c.op('act', lambda e: e.activation(out=junk[:], in_=Ht[s][:], func=AF.Square, accum_out=ssq[s][:]),
                             reads=[b_Ht[s]], writes=[b_junk, b_ss[s]])
                        c.op('act', lambda e: e.activation(out=rs[s][:], in_=ssq[s][:], func=AF.Ln, scale=1.0 / D, bias=EPS),
                             reads=[b_ss[s]], writes=[b_rs[s]])
                        c.op('act', lambda e: e.activation(out=rs[s][:], in_=rs[s][:], func=AF.Exp, scale=-0.5),
                             reads=[b_rs[s]], writes=[b_rs[s]])
                        c.op('dve', lambda e: e.scalar_tensor_tensor(out=hnb[s][:], in0=Ht[s][:], scalar=rs[s][:, 0:1], in1=normw[:],
                                                                     op0=ALU.mult, op1=ALU.mult),
                             reads=[b_Ht[s], b_rs[s], b_nw], writes=[b_hnb[s]])
                        for hh in range(2):
                            for k in range(8):
                                kk = hh * 8 + k
                                c.op('pe', lambda e: e.transpose(out=pt[hh][:, k, :], in_=hnb[s][:, kk * 128:(kk + 1) * 128], identity=identb[:]),
                                     reads=[b_hnb[s], bconst], writes=[b_pt[hh]])
                            evac(hnT[:, hh * 8:(hh + 1) * 8, ti * 128:(ti + 1) * 128], pt[hh][:], [b_pt[hh]], [b_hnT])
                    for (kind, c0, cw, dn, dof) in chunks[:cfg.get('nchunks', 1000)]:
                        wi = cnt['w'] % 3
                        cnt['w'] += 1
                        wb = Wb[wi]
                        c.dma('pool', wb[:, :, :cw], w_l[:, :, c0:c0 + cw], writes=[b_Wb[wi]])
                        dst = dmap[dn]
                        if kind == 'tok':
                            for ti in range(nb):
                                i = t0 + ti
                                pi = cnt['pm'] % 4
                                cnt['pm'] += 1
                                for k in range(16):
                                    c.op('pe', lambda e: e.matmul(pm[pi][:, :cw], lhsT=hnT[:, k, ti * 128:(ti + 1) * 128], rhs=wb[:, k, :cw],
                                                                  start=(k == 0), stop=(k == 15)),
                                         reads=[b_hnT, b_Wb[wi]], writes=[b_pm[pi]])
                                if dn in ('DT', 'IKW'):
                                    oi = cnt['otf'] % 2
                                    cnt['otf'] += 1
                                    evac(otf[oi][:, :cw], pm[pi][:, :cw], [b_pm[pi]], [b_otf[oi]])
                                    c.dma('sp', dst[i * 128:(i + 1) * 128, :], otf[oi][:, :cw], reads=[b_otf[oi]], writes=[dbuf[dn]])
                                else:
                                    oi = cnt['otb'] % 3
                                    cnt['otb'] += 1
                                    evac(otb[oi][:, :cw], pm[pi][:, :cw], [b_pm[pi]], [b_otb[oi]])
                                    c.dma('sp', dst[i * 128:(i + 1) * 128, dof:dof + cw], otb[oi][:, :cw], reads=[b_otb[oi]], writes=[dbuf[dn]])
                        else:
                            for j in range(cw // 128):
                                fi = cnt['ofm'] % 3
                                cnt['ofm'] += 1
                                for s0 in range(0, nb * 128, 512):
                                    sn = min(512, nb * 128 - s0)
                                    pi = cnt['pm'] % 4
                                    cnt['pm'] += 1
                                    for k in range(16):
                                        c.op('pe', lambda e: e.matmul(pm[pi][:, :sn], lhsT=wb[:, k, j * 128:(j + 1) * 128], rhs=hnT[:, k, s0:s0 + sn],
                                                                      start=(k == 0), stop=(k == 15)),
                                             reads=[b_hnT, b_Wb[wi]], writes=[b_pm[pi]])
                                    evac(ofm[fi][:, s0:s0 + sn], pm[pi][:, :sn], [b_pm[pi]], [b_ofm[fi]])
                                co = 3 if dn == 'XBCT' else 0
                                c.dma('sp', dst[dof + j][:, co + t0 * 128: co + (t0 + nb) * 128], ofm[fi][:, :nb * 128],
                                      reads=[b_ofm[fi]], writes=[dbuf[dn]])
            c.barrier()

        def phase_B(l):
            with ExitStack() as st:
                diagF = sb(st, "diagF", [128, 192, 128], BF16)
                b_diagF = Buf()
                cw_p = sb(st, "cw_p", [128, 192], F32)
                cb_p = sb(st, "cb_p", [128, 48], F32)
                cb_row = sb(st, "cb_row", [1, 6144], BF16)
                ones_row = sb(st, "ones_row", [1, 128], BF16)
                dtb_bc = sb(st, "dtb_bc", [128, NH], F32)
                a_bc = sb(st, "a_bc", [128, NH], F32)
                dsk_bc = sb(st, "dsk_bc", [128, NH], F32)
                snw_bc = sb(st, "snw_bc", [128, DI], F32)
                b_par = Buf()
                c.dma('sp', cw_p[:], convw_p[l], writes=[b_par])
                c.dma('sp', cb_p[:], convb_p[l], writes=[b_par])
                c.dma('pool', cb_row[:], conv_b[l].unsqueeze(0), writes=[b_par])
                c.dma('sp', dtb_bc[:], dt_bias[l].partition_broadcast(128), writes=[b_par])
                c.dma('sp', a_bc[:], a_log[l].partition_broadcast(128), writes=[b_par])
                c.dma('sp', dsk_bc[:], d_skip[l].partition_broadcast(128), writes=[b_par])
                c.dma('sp', snw_bc[:], ssm_nw[l].partition_broadcast(128), writes=[b_par])
                c.op('dve', lambda e: e.memset(ones_row[:], 1.0), writes=[b_par])
                c.op('act', lambda e: e.activation(out=a_bc[:], in_=a_bc[:], func=AF.Exp), reads=[b_par], writes=[b_par])
                c.op('dve', lambda e: e.tensor_scalar(out=a_bc[:], in0=a_bc[:], scalar1=-1.0, scalar2=None, op0=ALU.mult),
                     reads=[b_par], writes=[b_par])

                for a in range(192):
                    c.op('dve', lambda e: e.tensor_scalar(out=diagF[:, a, :], in0=identb[:], scalar1=cw_p[:, a:a + 1], scalar2=None, op0=ALU.mult),
                         reads=[b_par, bconst], writes=[b_diagF])
                Hin = sb(st, "Hin", [128, 8, 512], F32)
                Hinb = sb(st, "Hinb", [128, 8, 512], BF16)
                b_Hin = [Buf() for _ in range(8)]
                b_Hinb = [Buf() for _ in range(8)]
                c.op('dve', lambda e: e.memset(Hin[:], 0.0), writes=b_Hin)
                c.op('pool', lambda e: e.memset(Hinb[:], 0.0), writes=b_Hinb)

                def T2(name, shape, dt, n=2):
                    return [sb(st, f"{name}{s}", shape, dt) for s in range(n)], [Buf() for _ in range(n)]

                u, b_u = T2("u", [128, 6, 515], BF16, 2)
                zt_, b_zt = T2("zt", [128, 512], BF16)
                dtr, b_dtr = T2("dtr", [128, NH], F32)
                dtt, b_dt = T2("dtt", [128, NH], F32, 8)
                tA, b_tA = T2("tA", [128, NH], F32)
                tB, b_tB = T2("tB", [128, NH], F32)
                adt, b_adt = T2("adt", [128, NH], F32, 8)
                acs_sb, b_acs = T2("acs_sb", [128, NH], F32)
                eacs, b_eacs = T2("eacs", [128, NH], F32, 8)
                dte, b_dte = T2("dte", [128, NH], F32, 8)
                dec, b_dec = T2("dec", [128, NH], F32, 8)
                rhsD, b_rhsD = T2("rhsD", [128, 8, 128], F32)
                Eb, b_E = T2("Eb", [128, 8, 128], BF16)
                MTb, b_MT = T2("MTb", [128, 8, 128], BF16)
                xtm, b_xtm = T2("xtm", [128, 512], BF16)
                xdt, b_xdt = T2("xdt", [128, 512], BF16)
                xw, b_xw = T2("xw", [128, 512], BF16)
                xD, b_xD = T2("xD", [128, 512], BF16)
                BTt, b_BT = T2("BTt", [128, 128], BF16)
                CTt, b_CT = T2("CTt", [128, 128], BF16)
                Btm, b_Btm = T2("Btm", [128, 128], BF16)
                cbm, b_cbm = T2("cbm", [128, 128], BF16)
                tmp, b_tmp = T2("tmpB", [128, 512], F32, 2)
                ysb, b_ysb = T2("ysb", [128, 512], F32, 2)
                sz, b_sz = T2("sz", [128, 512], F32, 2)
                gy, b_gy = T2("gy", [128, 512], F32, 2)
                junkB, b_junkB = T2("junkB", [128, 512], BF16, 1)
                gss, b_gss = T2("gss", [128, 1], F32, 1)
                grs, b_grs = T2("grs", [128, 1], F32, 1)
                gn, b_gn = T2("gn", [128, 512], BF16)
                gst, b_gst = T2("gst", [128, 4, 512], BF16, 2)
                hup, b_hup = T2("hup", [128, 512], F32, 2)

                xps = ps(st, "xps", [128, 512], F32)
                bank1 = ps(st, "bank1", [128, 512], F32)
                bcps = bank1[:, 0:384]
                btps = bank1[:, 384:512].bitcast(BF16)
                Dps = [ps(st, f"Dps{s}", [128, 512], F32) for s in range(2)]
                yps = ps(st, "yps", [128, 512], F32)
                ups = ps(st, "ups", [128, 512], F32)
                sps = ps(st, "sps", [128, 512], F32)
                bank7 = ps(st, "bank7", [128, 512], F32)
                acsps = bank7[:, 0:256]
                trps = bank7[:, 256:512].bitcast(BF16).rearrange("p (a t) -> p a t", a=4)
                b_xps, b_bcps, b_yps, b_ups, b_sps, b_acsps = [PB() for _ in range(6)]
                b_btps = b_bcps
                b_trps = b_acsps
                b_Dps = [PB(), PB()]

                n4 = (NT + 3) // 4
                for ib in range(n4 if cfg.get('lvlB', 99) > -3 else 0):
                    tiles = list(range(ib * 4, min(NT, ib * 4 + 4)))
                    ntl = len(tiles)
                    ncol = ntl * 128 + 3
                    for ti, i in enumerate(tiles):
                        sd = (ib % 2) * 4 + ti
                        s = i % 2
                        c.dma('sp', dtr[s][:], DT[i * 128:(i + 1) * 128, :], reads=[dbuf['DT']], writes=[b_dtr[s]])
                        c.op('dve', lambda e: e.tensor_tensor(out=tA[s][:], in0=dtr[s][:], in1=dtb_bc[:], op=ALU.add),
                             reads=[b_dtr[s], b_par], writes=[b_tA[s]])
                        c.op('act', lambda e: e.activation(out=tB[s][:], in_=tA[s][:], func=AF.Abs),
                             reads=[b_tA[s]], writes=[b_tB[s]])
                        c.op('act', lambda e: e.activation(out=tB[s][:], in_=tB[s][:], func=AF.Exp, scale=-1.0),
                             reads=[b_tB[s]], writes=[b_tB[s]])
                        c.op('act', lambda e: e.activation(out=tB[s][:], in_=tB[s][:], func=AF.Ln, bias=1.0),
                             reads=[b_tB[s]], writes=[b_tB[s]])
                        c.op('dve', lambda e: e.scalar_tensor_tensor(out=dtt[sd][:], in0=tA[s][:], scalar=0.0, in1=tB[s][:], op0=ALU.max, op1=ALU.add),
                             reads=[b_tA[s], b_tB[s]], writes=[b_dt[sd]])
                        if i == 0:
                            c.op('dve', lambda e: e.memset(dtt[sd][0:112, :], 0.0), writes=[b_dt[sd]])
                        c.op('dve', lambda e: e.tensor_tensor(out=adt[sd][:], in0=dtt[sd][:], in1=a_bc[:], op=ALU.mult),
                             reads=[b_dt[sd], b_par], writes=[b_adt[sd]])
                        c.op('pe', lambda e: e.matmul(acsps[:, 0:64], lhsT=trile[:], rhs=adt[sd][:], start=True, stop=True),
                             reads=[b_adt[sd], bconst], writes=[b_acsps])
                        c.op('pe', lambda e: e.matmul(acsps[:, 64:128], lhsT=onesf[:], rhs=adt[sd][:], start=True, stop=True),
                             reads=[b_adt[sd], bconst], writes=[b_acsps])
                        c.op('act', lambda e: e.copy(out=acs_sb[s][:], in_=acsps[:, 0:64]), reads=[b_acsps], writes=[b_acs[s]])
                        c.op('act', lambda e: e.activation(out=eacs[sd][:], in_=acsps[:, 0:64], func=AF.Exp), reads=[b_acsps], writes=[b_eacs[sd]])
                        c.op('act', lambda e: e.activation(out=dec[sd][:], in_=acsps[:, 64:128], func=AF.Exp), reads=[b_acsps], writes=[b_dec[sd]])
                        c.op('dve', lambda e: e.tensor_tensor(out=dte[sd][:], in0=acsps[:, 64:128], in1=acs_sb[s][:], op=ALU.subtract),
                             reads=[b_acsps, b_acs[s]], writes=[b_dte[sd]])
                        c.op('act', lambda e: e.activation(out=dte[sd][:], in_=dte[sd][:], func=AF.Exp), reads=[b_dte[sd]], writes=[b_dte[sd]])

                    def body(g, ti, i):
                        us = g % 2
                        gs_ = g % 2
                        diag = diagF
                        b_diag = b_diagF
                        s = (i * 8 + g) % 2
                        o = ti * 128
                        hs = slice(8 * g, 8 * g + 8)
                        sd = (ib % 2) * 4 + ti
                        for a in range(4):
                            ct = 4 * g + a
                            for k in range(4):
                                c.op('pe', lambda e: e.matmul(xps[:, a * 128:(a + 1) * 128], lhsT=u[us][:, a, o + k:o + k + 128], rhs=diag[:, ct * 4 + k, :],
                                                              start=(k == 0), stop=False),
                                     reads=[b_u[us], b_diag], writes=[b_xps])
                            c.op('pe', lambda e: e.matmul(xps[:, a * 128:(a + 1) * 128], lhsT=ones_row[0:1, :], rhs=cb_row[0:1, ct * 128:(ct + 1) * 128],
                                                          start=False, stop=True),
                                 reads=[b_par], writes=[b_xps])
                        c.op('act', lambda e: e.activation(out=xtm[s][:], in_=xps[:], func=AF.Silu), reads=[b_xps], writes=[b_xtm[s]])
                        for (which, a, ct) in ((0, 4, 32 + g), (1, 5, 40 + g)):
                            for k in range(4):
                                c.op('pe', lambda e: e.matmul(bcps[:, which * 128:(which + 1) * 128], lhsT=diag[:, ct * 4 + k, :], rhs=u[us][:, a, o + k:o + k + 128],
                                                              start=(k == 0), stop=(k == 3)),
                                     reads=[b_u[us], b_diag], writes=[b_bcps])
                        c.op('act', lambda e: e.activation(out=BTt[s][:], in_=bcps[:, 0:128], func=AF.Silu, bias=cb_p[:, 32 + g:33 + g]),
                             reads=[b_bcps, b_par], writes=[b_BT[s]])
                        c.op('act', lambda e: e.activation(out=CTt[s][:], in_=bcps[:, 128:256], func=AF.Silu, bias=cb_p[:, 40 + g:41 + g]),
                             reads=[b_bcps, b_par], writes=[b_CT[s]])
                        c.op('pe', lambda e: e.transpose(out=btps[:, 0:128], in_=BTt[s][:], identity=identb[:]), reads=[b_BT[s], bconst], writes=[b_btps])
                        c.op('dve', lambda e: e.tensor_copy(out=Btm[s][:], in_=btps[:, 0:128]), reads=[b_btps], writes=[b_Btm[s]])
                        c.op('pe', lambda e: e.matmul(bcps[:, 256:384], lhsT=BTt[s][:], rhs=CTt[s][:], start=True, stop=True),
                             reads=[b_BT[s], b_CT[s]], writes=[b_bcps])
                        c.op('dve', lambda e: e.tensor_tensor(out=cbm[s][:], in0=bcps[:, 256:384], in1=trile[:], op=ALU.mult),
                             reads=[b_bcps, bconst], writes=[b_cbm[s]])
                        yield
                        c.op('dve', lambda e: e.tensor_tensor(out=rhsD[s][:], in0=trile[:].unsqueeze(1).to_broadcast([128, 8, 128]),
                                                              in1=adt[sd][:, hs].unsqueeze(2).to_broadcast([128, 8, 128]), op=ALU.mult),
                             reads=[b_adt[sd], bconst], writes=[b_rhsD[s]])
                        for hh in range(2):
                            c.op('pe', lambda e: e.matmul(Dps[hh][:], lhsT=strict[:], rhs=rhsD[s][:, hh * 4:(hh + 1) * 4, :], start=True, stop=True),
                                 reads=[b_rhsD[s], bconst], writes=[b_Dps[hh]])
                            c.op('act', lambda e: e.activation(out=Eb[s][:, hh * 4:(hh + 1) * 4, :], in_=Dps[hh][:], func=AF.Exp),
                                 reads=[b_Dps[hh]], writes=[b_E[s]])
                        c.op('dve', lambda e: e.tensor_tensor(out=MTb[s][:], in0=Eb[s][:], in1=cbm[s][:].unsqueeze(1).to_broadcast([128, 8, 128]), op=ALU.mult),
                             reads=[b_E[s], b_cbm[s]], writes=[b_MT[s]])
                        x3 = xtm[s][:].rearrange("p (h q) -> p h q", h=8)
                        c.op('dve', lambda e: e.tensor_tensor(out=xdt[s][:].rearrange("p (h q) -> p h q", h=8), in0=x3,
                                                              in1=dtt[sd][:, hs].unsqueeze(2).to_broadcast([128, 8, 64]), op=ALU.mult),
                             reads=[b_xtm[s], b_dt[sd]], writes=[b_xdt[s]])
                        c.op('pool', lambda e: e.tensor_tensor(out=xD[s][:].rearrange("p (h q) -> p h q", h=8), in0=x3,
                                                               in1=dsk_bc[:, hs].unsqueeze(2).to_broadcast([128, 8, 64]), op=ALU.mult),
                             reads=[b_xtm[s], b_par], writes=[b_xD[s]])
                        c.op('pool', lambda e: e.tensor_tensor(out=xw[s][:].rearrange("p (h q) -> p h q", h=8), in0=xdt[s][:].rearrange("p (h q) -> p h q", h=8),
                                                               in1=dte[sd][:, hs].unsqueeze(2).to_broadcast([128, 8, 64]), op=ALU.mult),
                             reads=[b_xdt[s], b_dte[sd]], writes=[b_xw[s]])
                        yield
                        c.op('pe', lambda e: e.matmul(yps[:], lhsT=identb[:], rhs=xD[s][:], start=True, stop=False, skip_group_check=True),
                             reads=[b_xD[s], bconst], writes=[b_yps])
                        for r in range(8):
                            c.op('pe', lambda e: e.matmul(yps[:, r * 64:(r + 1) * 64], lhsT=MTb[s][:, r, :], rhs=xdt[s][:, r * 64:(r + 1) * 64], start=False, stop=(r == 7),
                                                          skip_group_check=True),
                                 reads=[b_MT[s], b_xdt[s]], writes=[b_yps])
                        c.op('pe', lambda e: e.matmul(ups[:], lhsT=CTt[s][:], rhs=Hinb[:, g, :], start=True, stop=True),
                             reads=[b_CT[s], b_Hinb[g]], writes=[b_ups])
                        c.op('dve', lambda e: e.tensor_tensor(out=tmp[s][:].rearrange("p (h q) -> p h q", h=8), in0=ups[:].rearrange("p (h q) -> p h q", h=8),
                                                              in1=eacs[sd][:, hs].unsqueeze(2).to_broadcast([128, 8, 64]), op=ALU.mult),
                             reads=[b_ups, b_eacs[sd]], writes=[b_tmp[s]])
                        c.op('dve', lambda e: e.tensor_tensor(out=ysb[s][:], in0=yps[:], in1=tmp[s][:], op=ALU.add),
                             reads=[b_yps, b_tmp[s]], writes=[b_ysb[s]])
                        c.op('pe', lambda e: e.matmul(sps[:], lhsT=Btm[s][:], rhs=xw[s][:], start=True, stop=True),
                             reads=[b_Btm[s], b_xw[s]], writes=[b_sps])
                        c.op('dve', lambda e: e.tensor_tensor(out=hup[s][:].rearrange("p (h q) -> p h q", h=8), in0=Hin[:, g, :].rearrange("p (h q) -> p h q", h=8),
                                                              in1=dec[sd][:, hs].unsqueeze(2).to_broadcast([128, 8, 64]), op=ALU.mult),
                             reads=[b_Hin[g], b_dec[sd]], writes=[b_hup[s]])
                        c.op('dve', lambda e: e.tensor_tensor(out=Hin[:, g, :], in0=sps[:], in1=hup[s][:], op=ALU.add),
                             reads=[b_sps, b_hup[s]], writes=[b_Hin[g]])
                        c.op('act', lambda e: e.copy(out=Hinb[:, g, :], in_=Hin[:, g, :]), reads=[b_Hin[g]], writes=[b_Hinb[g]])
                        yield
                        zs = (i * 8 + g) % 2
                        c.dma('sp', zt_[zs][:], Z[i * 128:(i + 1) * 128, 512 * g:512 * (g + 1)], reads=[dbuf['Z']], writes=[b_zt[zs]])
                        c.op('act', lambda e: e.activation(out=sz[s][:], in_=zt_[zs][:], func=AF.Silu), reads=[b_zt[zs]], writes=[b_sz[s]])
                        c.op('pool', lambda e: e.tensor_tensor(out=gy[s][:], in0=ysb[s][:], in1=sz[s][:], op=ALU.mult),
                             reads=[b_ysb[s], b_sz[s]], writes=[b_gy[s]])
                        c.op('act', lambda e: e.activation(out=junkB[0][:], in_=gy[s][:], func=AF.Square, accum_out=gss[0][:]),
                             reads=[b_gy[s]], writes=[b_junkB[0], b_gss[0]])
                        c.op('act', lambda e: e.activation(out=grs[0][:], in_=gss[0][:], func=AF.Ln, scale=1.0 / 512, bias=EPS),
                             reads=[b_gss[0]], writes=[b_grs[0]])
                        c.op('act', lambda e: e.activation(out=grs[0][:], in_=grs[0][:], func=AF.Exp, scale=-0.5),
                             reads=[b_grs[0]], writes=[b_grs[0]])
                        c.op('dve', lambda e: e.scalar_tensor_tensor(out=gn[s][:], in0=gy[s][:], scalar=grs[0][:, 0:1], in1=snw_bc[:, 512 * g:512 * (g + 1)],
                                                                     op0=ALU.mult, op1=ALU.mult),
                             reads=[b_gy[s], b_grs[0], b_par], writes=[b_gn[s]])
                        for a in range(4):
                            c.op('pe', lambda e: e.transpose(out=trps[:, a, :], in_=gn[s][:, a * 128:(a + 1) * 128], identity=identb[:]),
                                 reads=[b_gn[s], bconst], writes=[b_trps])
                        c.op('act', lambda e: e.copy(out=gst[gs_][:, :, o:o + 128], in_=trps[:]), reads=[b_trps], writes=[b_gst[gs_]])
                        yield
                    for gp in range(4):
                        pair = (2 * gp, 2 * gp + 1)
                        for g in pair:
                            us = g % 2
                            c0_ = 3 if ib == 0 else 0
                            if ib == 0:
                                c.op('pool', lambda e: e.memset(u[us][:, :, 0:3], 0.0), writes=[b_u[us]])
                            c.dma('sp', u[us][:, 0:4, c0_:ncol], XBCT[4 * g:4 * g + 4, :, ib * 512 + c0_: ib * 512 + ncol].rearrange("a p c -> p a c"),
                                  reads=[dbuf['XBCT']], writes=[b_u[us]])
                            c.dma('sp', u[us][:, 4, c0_:ncol], XBCT[32 + g, :, ib * 512 + c0_: ib * 512 + ncol], reads=[dbuf['XBCT']], writes=[b_u[us]])
                            c.dma('sp', u[us][:, 5, c0_:ncol], XBCT[40 + g, :, ib * 512 + c0_: ib * 512 + ncol], reads=[dbuf['XBCT']], writes=[b_u[us]])
                        pend = [[body(g, ti, i) for g in pair] for ti, i in enumerate(tiles)]
                        active = []
                        while pend or active:
                            if pend and (not active or all(a_[1] >= 3 for a_ in active)):
                                for gobj in pend.pop(0):
                                    active.append([gobj, 0])
                            for a_ in list(active):
                                try:
                                    next(a_[0])
                                    a_[1] += 1
                                except StopIteration:
                                    active.remove(a_)
                        for g in pair:
                            gs_ = g % 2
                            c.dma('sp', GT[4 * g:4 * g + 4, :, ib * 512: ib * 512 + ntl * 128].rearrange("a p c -> p a c"), gst[gs_][:, :, :ntl * 128],
                                  reads=[b_gst[gs_]], writes=[dbuf['GT']])
            c.barrier()

        def phase_C0(l):
            with ExitStack() as st:
                cosA = sb(st, "cosA", [128, TP], F32)
                sinA = sb(st, "sinA", [128, TP], F32)
                cosI = sb(st, "cosI", [128, TP], F32)
                sinI = sb(st, "sinI", [128, TP], F32)
                rA = sb(st, "rA", [128, 128], BF16)
                rI = sb(st, "rI", [128, 128], BF16)
                qn_w = sb(st, "qn_w", [128, 1], F32)
                kn_w = sb(st, "kn_w", [128, 1], F32)
                b_t = Buf()
                c.dma('sp', cosA[:], c_cosA, writes=[b_t])
                c.dma('sp', sinA[:], c_sinA, writes=[b_t])
                c.dma('sp', cosI[:], c_cosI, writes=[b_t])
                c.dma('sp', sinI[:], c_sinI, writes=[b_t])
                c.dma('sp', rA[:], c_rA, writes=[b_t])
                c.dma('sp', rI[:], c_rI, writes=[b_t])
                c.dma('sp', qn_w[:], qnw[l].unsqueeze(1), writes=[b_t])
                c.dma('sp', kn_w[:], knw[l].unsqueeze(1), writes=[b_t])
                xin = [sb(st, f"xin{s}", [128, 512], BF16) for s in range(2)]
                b_xin = [Buf(), Buf()]
                sqL = [sb(st, f"sqC{q}", [128, 512], BF16) for q in range(2)]
                b_sqL = [Buf(), Buf()]
                rstdL = [sb(st, f"rstdC{q}", [128, 512], F32) for q in range(2)]
                b_rstdL = [Buf(), Buf()]
                xnL = [sb(st, f"xnC{q}", [128, 512], BF16) for q in range(2)]
                b_xnL = [Buf(), Buf()]
                t1L = [sb(st, f"t1C{q}", [128, 512], F32) for q in range(2)]
                b_t1L = [Buf(), Buf()]
                t2L = [sb(st, f"t2C{q}", [128, 512], F32) for q in range(2)]
                b_t2L = [Buf(), Buf()]
                xo = [sb(st, f"xoC{s}", [128, 512], BF16) for s in range(2)]
                b_xo = [Buf(), Buf()]
                sspL = [ps(st, f"sspC{q}", [128, 512], F32) for q in range(2)]
                b_sspL = [PB(), PB()]
                rotpL = [ps(st, f"rotpC{q}", [128, 512], F32) for q in range(2)]
                b_rotpL = [PB(), PB()]
                n = 0
                jobs = [('q', h) for h in range(16)] + [('k', h) for h in range(4)] + [('i', h) for h in range(8)]
                for (kind, h) in jobs:
                    for s0 in range(0, TP, 512):
                        sn = min(512, TP - s0)
                        s = n % 2
                        n += 1
                        sq, b_sq, rstd, b_rstd, xn, b_xn = sqL[s], b_sqL[s], rstdL[s], b_rstdL[s], xnL[s], b_xnL[s]
                        t1, b_t1, t2, b_t2 = t1L[s], b_t1L[s], t2L[s], b_t2L[s]
                        ssp, b_ssp, rotp, b_rotp = sspL[s], b_sspL[s], rotpL[s], b_rotpL[s]
                        src, sname = {'q': (QT, 'QT'), 'k': (KT, 'KT'), 'i': (IQT, 'IQT')}[kind]
                        c.dma('sp', xin[s][:, :sn], src[h][:, s0:s0 + sn], reads=[dbuf[sname]], writes=[b_xin[s]])
                        if kind in ('q', 'k'):
                            wv = qn_w if kind == 'q' else kn_w
                            c.op('act', lambda e: e.activation(out=sq[:, :sn], in_=xin[s][:, :sn], func=AF.Square), reads=[b_xin[s]], writes=[b_sq])
                            c.op('pe', lambda e: e.matmul(ssp[:, :sn], lhsT=onesb[:], rhs=sq[:, :sn], start=True, stop=True), reads=[b_sq, bconst], writes=[b_ssp])
                            c.op('act', lambda e: e.activation(out=rstd[:, :sn], in_=ssp[:, :sn], func=AF.Ln, scale=1.0 / 128, bias=EPS), reads=[b_ssp], writes=[b_rstd])
                            c.op('act', lambda e: e.activation(out=rstd[:, :sn], in_=rstd[:, :sn], func=AF.Exp, scale=-0.5), reads=[b_rstd], writes=[b_rstd])
                            c.op('dve', lambda e: e.scalar_tensor_tensor(out=xn[:, :sn], in0=xin[s][:, :sn], scalar=wv[:, 0:1], in1=rstd[:, :sn], op0=ALU.mult, op1=ALU.mult),
                                 reads=[b_xin[s], b_rstd, b_t], writes=[b_xn])
                            xsrc, b_xsrc, rm, cs, sn_ = xn, b_xn, rA, cosA, sinA
                        else:
                            xsrc, b_xsrc, rm, cs, sn_ = xin[s], b_xin[s], rI, cosI, sinI
                        c.op('pe', lambda e: e.matmul(rotp[:, :sn], lhsT=rm[:], rhs=xsrc[:, :sn], start=True, stop=True), reads=[b_xsrc, b_t], writes=[b_rotp])
                        c.op('pool', lambda e: e.tensor_tensor(out=t1[:, :sn], in0=xsrc[:, :sn], in1=cs[:, s0:s0 + sn], op=ALU.mult), reads=[b_xsrc, b_t], writes=[b_t1])
                        c.op('dve', lambda e: e.tensor_tensor(out=t2[:, :sn], in0=rotp[:, :sn], in1=sn_[:, s0:s0 + sn], op=ALU.mult), reads=[b_rotp, b_t], writes=[b_t2])
                        c.op('dve', lambda e: e.tensor_tensor(out=xo[s][:, :sn], in0=t1[:, :sn], in1=t2[:, :sn], op=ALU.add), reads=[b_t1, b_t2], writes=[b_xo[s]])
                        if kind == 'q':
                            c.dma('sp', QR[h][:, s0:s0 + sn], xo[s][:, :sn], reads=[b_xo[s]], writes=[dbuf['QR']])
                        elif kind == 'k':
                            c.dma('sp', KR[h][:, s0:s0 + sn], xo[s][:, :sn], reads=[b_xo[s]], writes=[dbuf['KR']])
                        else:
                            c.dma('sp', IQR[2 * h][:, s0:s0 + sn], xo[s][0:64, :sn], reads=[b_xo[s]], writes=[dbuf['IQR']])
                            c.dma('sp', IQR[2 * h + 1][:, s0:s0 + sn], xo[s][64:128, :sn], reads=[b_xo[s]], writes=[dbuf['IQR']])
            c.barrier()

        def phase_C(l):
            with ExitStack() as st:
                KRs = sb(st, "KRs", [128, 4, TP], BF16)
                Vs = sb(st, "Vs", [128, NT, 512], BF16)
                IKT = sb(st, "IKT", [64, TP], BF16)
                b_KR, b_V, b_IKT = Buf(), Buf(), Buf()
                c.dma('sp', KRs[:], KR.rearrange("a p c -> p a c"), reads=[dbuf['KR']], writes=[b_KR])
                for i in range(NT):
                    c.dma('sp', Vs[:, i, :], V[i * 128:(i + 1) * 128, :], reads=[dbuf['V']], writes=[b_V])
                cosK = sb(st, "cosK", [128, NT, 8], F32)
                sinK = sb(st, "sinK", [128, NT, 8], F32)
                iknw_bc = sb(st, "iknw_bc", [128, 64], F32)
                triq = sb(st, "triq", [128, 2, 128], F32)
                negq = sb(st, "negq", [128, 2, 128], F32)
                pow2 = sb(st, "pow2", [128, NIT + 2], F32)
                b_t = Buf()
                c.dma('sp', cosK[:], c_cosK, writes=[b_t])
                c.dma('sp', sinK[:], c_sinK, writes=[b_t])
                c.dma('sp', iknw_bc[:], iknw[l].partition_broadcast(128), writes=[b_t])
                c.dma('sp', triq[:], c_triq.rearrange("a p c -> p a c"), writes=[b_t])
                c.dma('sp', negq[:], c_negq.rearrange("a p c -> p a c"), writes=[b_t])
                c.dma('sp', pow2[:], c_pow2, writes=[b_t])
                ikw = [sb(st, f"ikw{s}", [128, 80], F32) for s in range(2)]
                b_ikw = [Buf(), Buf()]
                kj = sb(st, "kj", [128, 64], F32)
                kss = sb(st, "kss", [128, 1], F32)
                krs = sb(st, "krs", [128, 1], F32)
                kn = sb(st, "kn", [128, 64], F32)
                kr = sb(st, "kr", [128, 64], BF16)
                ka = sb(st, "ka", [128, 8], F32)
                kb = sb(st, "kb", [128, 8], F32)
                b_kj, b_kss, b_krs, b_kn, b_kr, b_ka, b_kb = [Buf() for _ in range(7)]
                tps = ps(st, "tpsC", [128, 8, 128], BF16)
                b_tps = PB()
                for i in range(NT):
                    s = i % 2
                    c.dma('sp', ikw[s][:], IKW[i * 128:(i + 1) * 128, :], reads=[dbuf['IKW']], writes=[b_ikw[s]])
                    c.op('act', lambda e: e.activation(out=kj[:], in_=ikw[s][:, 0:64], func=AF.Square, accum_out=kss[:]), reads=[b_ikw[s]], writes=[b_kj, b_kss])
                    c.op('act', lambda e: e.activation(out=krs[:], in_=kss[:], func=AF.Ln, scale=1.0 / 64, bias=EPS), reads=[b_kss], writes=[b_krs])
                    c.op('act', lambda e: e.activation(out=krs[:], in_=krs[:], func=AF.Exp, scale=-0.5), reads=[b_krs], writes=[b_krs])
                    c.op('dve', lambda e: e.scalar_tensor_tensor(out=kn[:], in0=ikw[s][:, 0:64], scalar=krs[:, 0:1], in1=iknw_bc[:], op0=ALU.mult, op1=ALU.mult),
                         reads=[b_ikw[s], b_krs, b_t], writes=[b_kn])
                    c.op('dve', lambda e: e.tensor_copy(out=kr[:, 16:64], in_=kn[:, 16:64]), reads=[b_kn], writes=[b_kr])
                    c.op('dve', lambda e: e.tensor_tensor(out=ka[:], in0=kn[:, 0:8], in1=cosK[:, i, :], op=ALU.mult), reads=[b_kn, b_t], writes=[b_ka])
                    c.op('dve', lambda e: e.tensor_tensor(out=kb[:], in0=kn[:, 8:16], in1=sinK[:, i, :], op=ALU.mult), reads=[b_kn, b_t], writes=[b_kb])
                    c.op('dve', lambda e: e.tensor_tensor(out=kr[:, 0:8], in0=ka[:], in1=kb[:], op=ALU.subtract), reads=[b_ka, b_kb], writes=[b_kr])
                    c.op('dve', lambda e: e.tensor_tensor(out=ka[:], in0=kn[:, 8:16], in1=cosK[:, i, :], op=ALU.mult), reads=[b_kn, b_t], writes=[b_ka])
                    c.op('dve', lambda e: e.tensor_tensor(out=kb[:], in0=kn[:, 0:8], in1=sinK[:, i, :], op=ALU.mult), reads=[b_kn, b_t], writes=[b_kb])
                    c.op('dve', lambda e: e.tensor_tensor(out=kr[:, 8:16], in0=ka[:], in1=kb[:], op=ALU.add), reads=[b_ka, b_kb], writes=[b_kr])
                    c.op('pe', lambda e: e.transpose(out=tps[0:64, 0, :], in_=kr[:], identity=identb[:]), reads=[b_kr, bconst], writes=[b_tps])
                    c.op('act', lambda e: e.copy(out=IKT[:, i * 128:(i + 1) * 128], in_=tps[0:64, 0, :]), reads=[b_tps], writes=[b_IKT])

                scoreL = [sb(st, f"score{q}", [128, TP], F32) for q in range(2)]
                maskL = [sb(st, f"maskC{q}", [128, TP], BF16) for q in range(2)]
                maskTL = [sb(st, f"maskT{q}", [128, NT, 128], BF16) for q in range(2)]
                b_scoreL, b_maskL, b_maskTL = [Buf(), Buf()], [Buf(), Buf()], [Buf(), Buf()]
                iqr = [sb(st, f"iqr{s}", [64, 16, 128], BF16) for s in range(2)]
                b_iqr = [Buf(), Buf()]
                qg = [sb(st, f"qg{s}", [128, 16, 128], BF16) for s in range(2)]
                b_qg = [Buf(), Buf()]
                azt = [sb(st, f"azt{s}", [128, 16, 128], BF16) for s in range(2)]
                b_azt = [Buf(), Buf()]
                Rr = [sb(st, f"Rr{s}", [128, 512], F32) for s in range(2)]
                b_Rr = [Buf(), Buf()]
                abswL = [sb(st, f"absw{q}", [128, 16], F32) for q in range(2)]
                sgnwL = [sb(st, f"sgnw{q}", [128, 16], F32) for q in range(2)]
                amaxL = [sb(st, f"amax{q}", [128, 1], F32) for q in range(2)]
                loL = [sb(st, f"lo{q}", [128, 1], F32) for q in range(2)]
                midL = [sb(st, f"mid{q}", [128, 1], F32) for q in range(2)]
                wkL = [sb(st, f"wk{q}", [128, NIT + 2], F32) for q in range(2)]
                cnttL = [sb(st, f"cntt{q}", [128, 1], F32) for q in range(2)]
                gwL = [sb(st, f"gw{q}", [128, 1], F32) for q in range(2)]
                smallB = [[Buf() for _ in range(7)] for q in range(2)]
                pe_ = [sb(st, f"pe{s}", [128, 512], BF16) for s in range(2)]
                b_pe = [Buf(), Buf()]
                pmk = [sb(st, f"pmk{s}", [128, 512], BF16) for s in range(2)]
                b_pmk = [Buf(), Buf()]
                rden = sb(st, "rden", [128, 512], F32)
                b_rden = Buf()
                osb = sb(st, "osb", [128, 512], F32)
                b_osb = Buf()
                sgz = sb(st, "sgz", [128, 512], F32)
                b_sgz = Buf()
                og = [sb(st, f"og{s}", [128, 16, 128], BF16) for s in range(2)]
                b_og = [Buf(), Buf()]
                lps = [ps(st, f"lps{s}", [128, 512], F32) for s in range(2)]
                b_lps = [PB(), PB()]
                spsC = [ps(st, f"spsC{s}", [128, 512], F32) for s in range(2)]
                b_sps = [PB(), PB()]
                ops = ps(st, "opsC", [128, 512], F32)
                dps = ps(st, "dpsC", [128, 512], F32)
                b_ops, b_dps = PB(), PB()
                SC = 128.0 ** -0.5
                cntC = {'nl': 0, 'nsp': 0}

                def bodyC(i, z):
                    score, mask, maskT = scoreL[z], maskL[z], maskTL[z]
                    junk = mask
                    b_score, b_mask, b_maskT = b_scoreL[z], b_maskL[z], b_maskTL[z]
                    b_junk = b_mask
                    absw, sgnw, amax, lo, mid, wk, cntt, gw = abswL[z], sgnwL[z], amaxL[z], loL[z], midL[z], wkL[z], cnttL[z], gwL[z]
                    b_w, b_amax, b_lo, b_mid, b_wk, b_cnt, b_gw = smallB[z]
                    s = i % 2
                    nk = (i + 1) * 128
                    c.dma('sp', iqr[s][:], IQR[:, :, i * 128:(i + 1) * 128].rearrange("h d t -> d h t"), reads=[dbuf['IQR']], writes=[b_iqr[s]])
                    c.dma('sp', qg[s][:], QR[:, :, i * 128:(i + 1) * 128].rearrange("h d t -> d h t"), reads=[dbuf['QR']], writes=[b_qg[s]])
                    c.dma('sp', azt[s][:], AZT[:, :, i * 128:(i + 1) * 128].rearrange("h d t -> d h t"), reads=[dbuf['AZT']], writes=[b_azt[s]])
                    c.dma('sp', ikw[s][:], IKW[i * 128:(i + 1) * 128, :], reads=[dbuf['IKW']], writes=[b_ikw[s]])
                    c.op('act', lambda e: e.activation(out=absw[:], in_=ikw[s][:, 64:80], func=AF.Abs, scale=1.0 / 32),
                         reads=[b_ikw[s]], writes=[b_w])
                    c.op('act', lambda e: e.activation(out=sgnw[:], in_=ikw[s][:, 64:80], func=AF.Sign), reads=[b_ikw[s]], writes=[b_w])
                    for k0 in range(0, nk, 512):
                        kn_ = min(512, nk - k0)
                        for h in range(16):
                            li = cntC['nl'] % 2
                            cntC['nl'] += 1
                            c.op('pe', lambda e: e.matmul(lps[li][:, :kn_], lhsT=iqr[s][:, h, :], rhs=IKT[:, k0:k0 + kn_], start=True, stop=True),
                                 reads=[b_iqr[s], b_IKT], writes=[b_lps[li]])
                            c.op('act', lambda e: e.activation(out=Rr[li][:, :kn_], in_=lps[li][:, :kn_], func=AF.Relu, scale=absw[:, h:h + 1]),
                                 reads=[b_lps[li], b_w], writes=[b_Rr[li]])
                            if h == 0:
                                c.op('dve', lambda e: e.tensor_scalar(out=score[:, k0:k0 + kn_], in0=Rr[li][:, :kn_], scalar1=sgnw[:, 0:1], scalar2=None, op0=ALU.mult),
                                     reads=[b_Rr[li], b_w], writes=[b_score])
                            else:
                                c.op('dve', lambda e: e.scalar_tensor_tensor(out=score[:, k0:k0 + kn_], in0=Rr[li][:, :kn_], scalar=sgnw[:, h:h + 1],
                                                                             in1=score[:, k0:k0 + kn_], op0=ALU.mult, op1=ALU.add),
                                     reads=[b_Rr[li], b_w, b_score], writes=[b_score])
                    mi = 1 if i == 0 else 0
                    dsl = slice(i * 128, (i + 1) * 128)
                    c.op('dve', lambda e: e.tensor_tensor(out=score[:, dsl], in0=score[:, dsl], in1=triq[:, mi, :], op=ALU.mult), reads=[b_score, b_t], writes=[b_score])
                    if i > 0:
                        c.op('dve', lambda e: e.memset(score[:, 0:112], 0.0), writes=[b_score])
                    c.op('dve', lambda e: e.tensor_reduce(out=amax[:], in_=score[:, :nk], axis=AX.X, op=ALU.max, apply_absolute_value=True),
                         reads=[b_score], writes=[b_amax])
                    c.op('dve', lambda e: e.tensor_tensor(out=score[:, dsl], in0=score[:, dsl], in1=negq[:, mi, :], op=ALU.add), reads=[b_score, b_t], writes=[b_score])
                    if i > 0:
                        c.op('dve', lambda e: e.memset(score[:, 0:112], NEG), writes=[b_score])
                    c.op('dve', lambda e: e.tensor_scalar(out=lo[:], in0=amax[:], scalar1=-1.0, scalar2=-1.0, op0=ALU.mult, op1=ALU.add), reads=[b_amax], writes=[b_lo])
                    c.op('dve', lambda e: e.tensor_scalar(out=gw[:], in0=amax[:], scalar1=2.0, scalar2=2.0, op0=ALU.mult, op1=ALU.add), reads=[b_amax], writes=[b_gw])
                    c.op('dve', lambda e: e.tensor_scalar(out=wk[:], in0=pow2[:], scalar1=gw[:, 0:1], scalar2=None, op0=ALU.mult), reads=[b_gw, b_t], writes=[b_wk])
                    c.op('dve', lambda e: e.tensor_tensor(out=mid[:], in0=lo[:], in1=wk[:, 1:2], op=ALU.add), reads=[b_lo, b_wk], writes=[b_mid])
                    yield
                    for it in range(1, NIT + 1):
                        c.op('dve', lambda e: e.tensor_scalar(out=junk[:, :nk], in0=score[:, :nk], scalar1=mid[:, 0:1], scalar2=None, op0=ALU.is_ge, op1=ALU.add,
                                                              accum_out=cntt[:]),
                             reads=[b_score, b_mid], writes=[b_junk, b_cnt])
                        yield
                        c.op('dve', lambda e: e.tensor_scalar(out=gw[:], in0=cntt[:], scalar1=float(KSEL) - 0.5, scalar2=wk[:, it:it + 1], op0=ALU.is_ge, op1=ALU.mult),
                             reads=[b_cnt, b_wk], writes=[b_gw])
                        c.op('dve', lambda e: e.tensor_tensor(out=lo[:], in0=lo[:], in1=gw[:], op=ALU.add), reads=[b_lo, b_gw], writes=[b_lo])
                        c.op('dve', lambda e: e.tensor_tensor(out=mid[:], in0=lo[:], in1=wk[:, it + 1:it + 2], op=ALU.add), reads=[b_lo, b_wk], writes=[b_mid])
                        yield
                    c.op('dve', lambda e: e.tensor_scalar(out=mask[:, :nk], in0=score[:, :nk], scalar1=lo[:, 0:1], scalar2=None, op0=ALU.is_ge),
                         reads=[b_score, b_lo], writes=[b_mask])
                    for j0 in range(0, i + 1, 8):
                        jn = min(8, i + 1 - j0)
                        for jj in range(jn):
                            c.op('pe', lambda e: e.transpose(out=tps[:, jj, :], in_=mask[:, (j0 + jj) * 128:(j0 + jj + 1) * 128], identity=identb[:]),
                                 reads=[b_mask, bconst], writes=[b_tps])
                        c.op('act', lambda e: e.copy(out=maskT[:, j0:j0 + jn, :], in_=tps[:, 0:jn, :]), reads=[b_tps], writes=[b_maskT])
                    for kg in range(4):
                        for j in range(i + 1):
                            si = cntC['nsp'] % 2
                            cntC['nsp'] += 1
                            c.op('pe', lambda e: e.matmul(spsC[si][:], lhsT=KRs[:, kg, j * 128:(j + 1) * 128], rhs=qg[s][:, 4 * kg:4 * kg + 4, :], start=True, stop=True),
                                 reads=[b_KR, b_qg[s]], writes=[b_sps[si]])
                            c.op('act', lambda e: e.activation(out=pe_[si][:], in_=spsC[si][:], func=AF.Exp, scale=SC, bias=-20.0), reads=[b_sps[si]], writes=[b_pe[si]])
                            c.op('dve', lambda e: e.tensor_tensor(out=pmk[si][:].rearrange("p (h t) -> p h t", h=4), in0=pe_[si][:].rearrange("p (h t) -> p h t", h=4),
                                                                   in1=maskT[:, j, :].unsqueeze(1).to_broadcast([128, 4, 128]), op=ALU.mult),
                                 reads=[b_pe[si], b_maskT], writes=[b_pmk[si]])
                            c.op('pe', lambda e: e.matmul(ops[:], lhsT=Vs[:, j, kg * 128:(kg + 1) * 128], rhs=pmk[si][:], start=(j == 0), stop=(j == i)),
                                 reads=[b_V, b_pmk[si]], writes=[b_ops])
                            c.op('pe', lambda e: e.matmul(dps[:], lhsT=onesb[:], rhs=pmk[si][:], start=(j == 0), stop=(j == i)),
                                 reads=[b_pmk[si], bconst], writes=[b_dps])
                        c.op('dve', lambda e: e.tensor_scalar(out=rden[:], in0=dps[:], scalar1=1e-30, scalar2=None, op0=ALU.max), reads=[b_dps], writes=[b_rden])
                        c.op('dve', lambda e: e.reciprocal(out=rden[:], in_=rden[:]), reads=[b_rden], writes=[b_rden])
                        c.op('dve', lambda e: e.tensor_tensor(out=osb[:], in0=ops[:], in1=rden[:], op=ALU.mult), reads=[b_ops, b_rden], writes=[b_osb])
                        c.op('act', lambda e: e.activation(out=sgz[:].rearrange("p (h t) -> p h t", h=4), in_=azt[s][:, 4 * kg:4 * kg + 4, :], func=AF.Silu),
                             reads=[b_azt[s]], writes=[b_sgz])
                        c.op('pool', lambda e: e.tensor_tensor(out=og[s][:, 4 * kg:4 * kg + 4, :], in0=osb[:].rearrange("p (h t) -> p h t", h=4),
                                                               in1=sgz[:].rearrange("p (h t) -> p h t", h=4), op=ALU.mult),
                             reads=[b_osb, b_sgz], writes=[b_og[s]])
                    c.dma('sp', OT[:, :, i * 128:(i + 1) * 128].rearrange("h d t -> d h t"), og[s][:], reads=[b_og[s]], writes=[dbuf['OT']])
                    yield

                for i0 in range(0, NT, 2):
                    gens = [bodyC(i0, 0)]
                    if i0 + 1 < NT:
                        gens.append(bodyC(i0 + 1, 1))
                    while gens:
                        for g_ in list(gens):
                            try:
                                next(g_)
                            except StopIteration:
                                gens.remove(g_)
            c.barrier()

        def phase_D(l, hin, hin_n, hout, hout_n):
            last = (l == L - 1)
            with ExitStack() as st:
                wso_l = w_so[l].rearrange("(k p) n -> p k n", p=128)
                wao_l = w_ao[l].rearrange("(k p) n -> p k n", p=128)
                wo_l = w_o[l].rearrange("(k p) n -> p k n", p=128)
                Wo = sb(st, "Wo", [128, 16, D], BF16)
                b_Wo = Buf()
                for q4 in range(4):
                    c.dma('pool', Wo[:, :, q4 * 512:(q4 + 1) * 512], wo_l[:, :, q4 * 512:(q4 + 1) * 512], writes=[b_Wo])
                Wso = [sb(st, f"Wso{s}", [128, 32, 256], BF16) for s in range(2)]
                Wao = [sb(st, f"Wao{s}", [128, 16, 256], BF16) for s in range(2)]
                b_Wso = [Buf(), Buf()]
                b_Wao = [Buf(), Buf()]
                gtb = [sb(st, f"gtb{s}", [128, 32, 256], BF16) for s in range(2)]
                otb_ = [sb(st, f"otbD{s}", [128, 16, 256], BF16) for s in range(2)]
                b_gtb = [Buf(), Buf()]
                b_otb = [Buf(), Buf()]
                gsb = [sb(st, f"gsb{s}", [128, 2, 256], BF16) for s in range(2)]
                gab = [sb(st, f"gab{s}", [128, 2, 256], BF16) for s in range(2)]
                b_gsb = [Buf(), Buf()]
                b_gab = [Buf(), Buf()]
                sgs = sb(st, "sgs", [128, 256], F32)
                sga = sb(st, "sga", [128, 256], F32)
                m1 = sb(st, "m1", [128, 256], F32)
                m2 = sb(st, "m2", [128, 256], F32)
                b_sgs, b_sga, b_m1, b_m2 = Buf(), Buf(), Buf(), Buf()
                mg = [sb(st, f"mg{s}", [128, 2, 256], BF16) for s in range(2)]
                b_mg = [Buf(), Buf()]
                ysp = [ps(st, f"yspD{s}", [128, 512], F32) for s in range(2)]
                yap = [ps(st, f"yapD{s}", [128, 512], F32) for s in range(2)]
                b_ysp = [PB(), PB()]
                b_yap = [PB(), PB()]
                nblk = 0
                npp = 0
                for dc in range(8):
                    ws = dc % 2
                    c.dma('pool', Wso[ws][:], wso_l[:, :, dc * 256:(dc + 1) * 256], writes=[b_Wso[ws]])
                    c.dma('pool', Wao[ws][:], wao_l[:, :, dc * 256:(dc + 1) * 256], writes=[b_Wao[ws]])
                    for t0 in range(0, TP, 256):
                        s = nblk % 2
                        tn = min(256, TP - t0)
                        nblk += 1
                        c.dma('sp', gtb[s][:, :, :tn], GT[:, :, t0:t0 + tn].rearrange("a p c -> p a c"), reads=[dbuf['GT']], writes=[b_gtb[s]])
                        c.dma('sp', otb_[s][:, :, :tn], OT[:, :, t0:t0 + tn].rearrange("a p c -> p a c"), reads=[dbuf['OT']], writes=[b_otb[s]])
                        c.dma('sp', gsb[s][:, :, :tn], GST[2 * dc:2 * dc + 2, :, t0:t0 + tn].rearrange("a p c -> p a c"), reads=[dbuf['GST']], writes=[b_gsb[s]])
                        c.dma('sp', gab[s][:, :, :tn], GAT[2 * dc:2 * dc + 2, :, t0:t0 + tn].rearrange("a p c -> p a c"), reads=[dbuf['GAT']], writes=[b_gab[s]])
                        for dd in range(2):
                            pi = npp % 2
                            npp += 1
                            for kc in range(32):
                                c.op('pe', lambda e: e.matmul(ysp[pi][:, :tn], lhsT=Wso[ws][:, kc, dd * 128:(dd + 1) * 128], rhs=gtb[s][:, kc, :tn], start=(kc == 0), stop=(kc == 31)),
                                     reads=[b_Wso[ws], b_gtb[s]], writes=[b_ysp[pi]])
                            for kc in range(16):
                                c.op('pe', lambda e: e.matmul(yap[pi][:, :tn], lhsT=Wao[ws][:, kc, dd * 128:(dd + 1) * 128], rhs=otb_[s][:, kc, :tn], start=(kc == 0), stop=(kc == 15)),
                                     reads=[b_Wao[ws], b_otb[s]], writes=[b_yap[pi]])
                            c.op('act', lambda e: e.activation(out=sgs[:, :tn], in_=gsb[s][:, dd, :tn], func=AF.Sigmoid), reads=[b_gsb[s]], writes=[b_sgs])
                            c.op('act', lambda e: e.activation(out=sga[:, :tn], in_=gab[s][:, dd, :tn], func=AF.Sigmoid), reads=[b_gab[s]], writes=[b_sga])
                            c.op('dve', lambda e: e.tensor_tensor(out=m1[:, :tn], in0=ysp[pi][:, :tn], in1=sgs[:, :tn], op=ALU.mult), reads=[b_ysp[pi], b_sgs], writes=[b_m1])
                            c.op('dve', lambda e: e.tensor_tensor(out=m2[:, :tn], in0=yap[pi][:, :tn], in1=sga[:, :tn], op=ALU.mult), reads=[b_yap[pi], b_sga], writes=[b_m2])
                            c.op('pool', lambda e: e.tensor_tensor(out=mg[s][:, dd, :tn], in0=m1[:, :tn], in1=m2[:, :tn], op=ALU.add), reads=[b_m1, b_m2], writes=[b_mg[s]])
                        c.dma('sp', MT[2 * dc:2 * dc + 2, :, t0:t0 + tn].rearrange("a p c -> p a c"), mg[s][:, :, :tn], reads=[b_mg[s]], writes=[dbuf['MT']])
                mt = [sb(st, f"mtD{s}", [128, 16, 128], BF16) for s in range(2)]
                b_mt = [Buf(), Buf()]
                hold = [sb(st, f"hold{s}", [128, D], F32) for s in range(2)]
                b_hold = [Buf(), Buf()]
                hnew = hold
                b_hnew = b_hold
                for i in range(NT):
                    if last and i == 0:
                        continue
                    s = i % 2
                    c.dma('sp', mt[s][:], MT[:, :, i * 128:(i + 1) * 128].rearrange("a p c -> p a c"), reads=[dbuf['MT']], writes=[b_mt[s]])
                    c.dma('sp', hold[s][:], hin[i * 128:(i + 1) * 128, :], reads=[dbuf[hin_n]], writes=[b_hold[s]])
                    for q4 in range(4):
                        pi = npp % 2
                        npp += 1
                        for kc in range(16):
                            c.op('pe', lambda e: e.matmul(ysp[pi][:], lhsT=mt[s][:, kc, :], rhs=Wo[:, kc, q4 * 512:(q4 + 1) * 512], start=(kc == 0), stop=(kc == 15)),
                                 reads=[b_mt[s], b_Wo], writes=[b_ysp[pi]])
                        c.op('dve', lambda e: e.tensor_tensor(out=hnew[s][:, q4 * 512:(q4 + 1) * 512], in0=ysp[pi][:], in1=hold[s][:, q4 * 512:(q4 + 1) * 512], op=ALU.add),
                             reads=[b_ysp[pi], b_hold[s]], writes=[b_hnew[s]])
                    if last:
                        c.dma('sp', hout[(i - 1) * 128:i * 128, :], hnew[s][:], reads=[b_hnew[s]], writes=[dbuf[hout_n]])
                    elif i == 0:
                        c.dma('sp', hout[112:128, :], hnew[s][112:128, :], reads=[b_hnew[s]], writes=[dbuf[hout_n]])
                    else:
                        c.dma('sp', hout[i * 128:(i + 1) * 128, :], hnew[s][:], reads=[b_hnew[s]], writes=[dbuf[hout_n]])
            c.barrier()

        if L > 1:
            zrow = sb(es, "zrow", [112, D], F32)
            bzr = Buf()
            c.op('pool', lambda e: e.memset(zrow[:], 0.0), writes=[bzr])
            c.dma('sp', HA[0:112, :], zrow[:], reads=[bzr], writes=[dbuf['HA']])
            if L > 2:
                c.dma('sp', HB[0:112, :], zrow[:], reads=[bzr], writes=[dbuf['HB']])

        for l in range(L):
            hin, hin_n = hseq[l]
            hout, hout_n = hseq[l + 1]
            if 'A' in PH:
                phase_A(l, hin, hin_n)
            if 'B' in PH:
                phase_B(l)
            if 'C' in PH:
                phase_C0(l)
                phase_C(l)
            if 'D' in PH:
                phase_D(l, hin, hin_n, hout, hout_n)
        c.barrier()
        print("ops", c.nops, "waits", c.nwaits, "sems", c.nsem)
    return nc


def rope_np(pos, rot):
    inv = (500000.0 ** (-np.arange(0, rot, 2, dtype=np.float32) / rot)).astype(np.float32)
    ang = pos.astype(np.float32)[:, None] * inv[None, :]
    return np.cos(ang).astype(np.float32), np.sin(ang).astype(np.float32)


def make_consts(NT):
    TP = NT * 128
    bf = ml_dtypes.bfloat16
    idx = np.arange(128)
    cst = {}
    cst['c_identb'] = np.eye(128, dtype=np.float32).astype(bf)
    cst['c_identf'] = np.eye(128, dtype=np.float32)
    cst['c_trile'] = (idx[:, None] <= idx[None, :]).astype(np.float32)
    cst['c_strict'] = (idx[:, None] > idx[None, :]).astype(np.float32)
    triq = (idx[None, :] <= idx[:, None]).astype(np.float32)
    triq0 = triq * (idx[None, :] >= 112).astype(np.float32)
    cst['c_triq'] = np.stack([triq, triq0]).astype(np.float32)
    cst['c_negq'] = (np.float32(NEG) * (1.0 - cst['c_triq'])).astype(np.float32)
    cst['c_pow2'] = np.tile((2.0 ** -np.arange(NIT + 2, dtype=np.float64)).astype(np.float32)[None, :], (128, 1))
    pos = np.maximum(np.arange(TP) - 112, 0)
    ca, sa = rope_np(pos, 32)
    ci, si = rope_np(pos, 16)
    cosA = np.ones((128, TP), np.float32)
    sinA = np.zeros((128, TP), np.float32)
    cosA[0:16] = ca.T
    cosA[16:32] = ca.T
    sinA[0:16] = sa.T
    sinA[16:32] = sa.T
    cosI = np.ones((128, TP), np.float32)
    sinI = np.zeros((128, TP), np.float32)
    for hh in range(2):
        cosI[hh * 64:hh * 64 + 8] = ci.T
        cosI[hh * 64 + 8:hh * 64 + 16] = ci.T
        sinI[hh * 64:hh * 64 + 8] = si.T
        sinI[hh * 64 + 8:hh * 64 + 16] = si.T
    cst['c_cosA'], cst['c_sinA'], cst['c_cosI'], cst['c_sinI'] = cosA, sinA, cosI, sinI
    rA = np.zeros((128, 128), np.float32)
    for i in range(16):
        rA[i + 16, i] = -1.0
        rA[i, i + 16] = 1.0
    rI = np.zeros((128, 128), np.float32)
    for hh in range(2):
        for i in range(8):
            rI[hh * 64 + i + 8, hh * 64 + i] = -1.0
            rI[hh * 64 + i, hh * 64 + i + 8] = 1.0
    cst['c_rA'] = rA.astype(bf)
    cst['c_rI'] = rI.astype(bf)
    cst['c_cosK'] = np.ascontiguousarray(ci.reshape(NT, 128, 8).transpose(1, 0, 2))
    cst['c_sinK'] = np.ascontiguousarray(si.reshape(NT, 128, 8).transpose(1, 0, 2))
    return cst


def make_inputs(NT, L, x, meta_tokens, norm_w, w_in, conv_w, conv_b, dt_bias, a_log, d_skip, ssm_norm_w,
                w_ssm_out, q_norm_w, k_norm_w, idx_k_norm_w, w_attn_out, w_out):
    f = lambda a: np.ascontiguousarray(np.asarray(a, dtype=np.float32))
    B = x.shape[0]
    TP = NT * 128
    cst = make_consts(NT)
    shared = dict(cst)
    shared.update(w_in=f(w_in[:L]), w_ssm_out=f(w_ssm_out[:L]), w_attn_out=f(w_attn_out[:L]), w_out=f(w_out[:L]),
                  norm_w=f(norm_w[:L]), conv_b=f(conv_b[:L]), dt_bias=f(dt_bias[:L]), a_log=f(a_log[:L]), d_skip=f(d_skip[:L]),
                  ssm_norm_w=f(ssm_norm_w[:L]), q_norm_w=f(q_norm_w[:L]), k_norm_w=f(k_norm_w[:L]), idx_k_norm_w=f(idx_k_norm_w[:L]))
    cw = f(conv_w[:L])
    shared['convw_p'] = np.ascontiguousarray(cw.reshape(L, 4, 48, 128).transpose(0, 3, 2, 1).reshape(L, 128, 192))
    shared['convb_p'] = np.ascontiguousarray(f(conv_b[:L]).reshape(L, 48, 128).transpose(0, 2, 1))
    maps = []
    for b in range(B):
        h0 = np.zeros((TP, D), np.float32)
        h0[112:128] = f(meta_tokens)
        h0[128:] = f(x[b])
        m = dict(shared)
        m['h0'] = h0
        maps.append(m)
    return maps


_CACHE = {}


def kernel(x, meta_tokens, norm_w, w_in, conv_w, conv_b, dt_bias, a_log, d_skip, ssm_norm_w,
           w_ssm_out, q_norm_w, k_norm_w, idx_k_norm_w, w_attn_out, w_out):
    x = np.asarray(x)
    B, S, _ = x.shape
    NT = S // 128 + 1
    L = np.asarray(w_in).shape[0]
    cfg = dict(NT=NT, DEPTH=L, KSEL=min(256, S // 4))
    nc = build(cfg)
    maps = make_inputs(NT, L, x, meta_tokens, norm_w, w_in, conv_w, conv_b, dt_bias, a_log, d_skip, ssm_norm_w,
                       w_ssm_out, q_norm_w, k_norm_w, idx_k_norm_w, w_attn_out, w_out)
    res = run_bass_kernel_spmd(nc, maps, core_ids=list(range(B)))
    return np.stack([np.asarray(r["out"], dtype=np.float32) for r in res.results], axis=0)
```
